# Optimizing a Trainium2 kernel written in Bass

```python
import jax, jax.numpy as jnp
from jax import lax
import numpy as np

D_MODEL = 1024
BATCH = 2
SEQ = 8192
DEPTH = 2

N_BRANCH = 4
BRANCH_WIDTH = D_MODEL // 4
EPS = 1e-6
CONV_CH = BRANCH_WIDTH
CONV_K = 31
GLA_HEADS = 4
GLA_DV = BRANCH_WIDTH // GLA_HEADS
GLA_DK = GLA_DV // 2
GLA_RANK = 16
GLA_TAU = 16.0
GLA_CHUNK = 64
ATT_HEADS = 4
ATT_KV_HEADS = 2
ATT_HD = BRANCH_WIDTH // ATT_HEADS
ATT_GROUP = ATT_HEADS // ATT_KV_HEADS
WINDOW = 128
ATT_BLOCK = 128
ROPE_THETA = 500000.0
ROPE_DIM = ATT_HD // 4
POOL_WIDTHS = (2, 4, 8, 16)
POOL_GROUP = BRANCH_WIDTH // len(POOL_WIDTHS)
D_FF = 4 * D_MODEL

IN_SPLITS = (
    2 * CONV_CH,
    GLA_HEADS * GLA_DK,
    GLA_HEADS * GLA_DK,
    GLA_HEADS * GLA_DV,
    GLA_HEADS * GLA_DV,
    2 * GLA_RANK,
    ATT_HEADS * ATT_HD,
    ATT_KV_HEADS * ATT_HD,
    ATT_KV_HEADS * ATT_HD,
    len(POOL_WIDTHS) * POOL_GROUP,
    N_BRANCH * D_MODEL,
)
D_IN = int(sum(IN_SPLITS))
IN_OFFSETS = [int(o) for o in np.cumsum(IN_SPLITS)[:-1]]

kernel_name = "hybrid_gated_parallel_encoder"


def rms_norm(x, g):
    xf = x.astype(jnp.float32)
    y = xf * lax.rsqrt(jnp.mean(xf * xf, axis=-1, keepdims=True) + EPS)
    return (y * g.astype(jnp.float32)).astype(x.dtype)


def conv_module(glu, conv_w, conv_b, ln_g, ln_b):
    a, gate = jnp.split(glu, 2, axis=-1)
    u = a * jax.nn.sigmoid(gate)
    y = lax.conv_general_dilated(
        u, conv_w[:, None, :], window_strides=(1,),
        padding=[(CONV_K // 2, CONV_K // 2)],
        dimension_numbers=("NWC", "WIO", "NWC"),
        feature_group_count=CONV_CH) + conv_b
    yf = y.astype(jnp.float32)
    mu = jnp.mean(yf, axis=-1, keepdims=True)
    var = jnp.mean(jnp.square(yf - mu), axis=-1, keepdims=True)
    yf = (yf - mu) * lax.rsqrt(var + EPS) * ln_g.astype(jnp.float32) + ln_b.astype(jnp.float32)
    return jax.nn.silu(yf).astype(glu.dtype)


def gla_one_direction(q, k, v, logg):
    b_, t, h, dk = q.shape
    dv = v.shape[-1]
    n = t // GLA_CHUNK

    def chunks(a):
        return jnp.swapaxes(a.reshape(b_, n, GLA_CHUNK, h, a.shape[-1]), 0, 1)

    bcum = jnp.cumsum(logg.reshape(b_, n, GLA_CHUNK, h, dk), axis=2)
    bcum = jnp.swapaxes(bcum, 0, 1)
    mask = jnp.tril(jnp.ones((GLA_CHUNK, GLA_CHUNK), dtype=bool))[None, :, :, None, None]

    def step(state, inp):
        qc, kc, vc, bc = inp
        decay = jnp.exp(jnp.where(mask, bc[:, :, None] - bc[:, None, :], -jnp.inf))
        scores = jnp.einsum('bihd,bjhd,bijhd->bhij', qc, kc, decay)
        o = (jnp.einsum('bhij,bjhe->bihe', scores, vc)
             + jnp.einsum('bihd,bhde->bihe', qc * jnp.exp(bc), state))
        blast = bc[:, -1]
        state = (jnp.exp(blast)[..., None] * state
                 + jnp.einsum('bjhd,bjhe->bhde', kc * jnp.exp(blast[:, None] - bc), vc))
        return state, o

    s0 = jnp.zeros((b_, h, dk, dv), jnp.float32)
    _, o = lax.scan(step, s0, (chunks(q), chunks(k), chunks(v), bcum))
    return jnp.swapaxes(o, 0, 1).reshape(b_, t, h, dv)


def gla_branch(gq, gk, gv, gr, glr, w_up, b_up, norm_g):
    b_, t, _ = gq.shape
    f32 = jnp.float32
    q = gq.astype(f32).reshape(b_, t, GLA_HEADS, GLA_DK) * (GLA_DK ** -0.5)
    k = gk.astype(f32).reshape(b_, t, GLA_HEADS, GLA_DK)
    v = gv.astype(f32).reshape(b_, t, GLA_HEADS, GLA_DV)
    z = jnp.einsum('btsr,srk->btsk', glr.reshape(b_, t, 2, GLA_RANK), w_up) + b_up
    logg = (jax.nn.log_sigmoid(z.astype(f32)) / GLA_TAU).reshape(b_, t, 2, GLA_HEADS, GLA_DK)
    o_f = gla_one_direction(q, k, v, logg[:, :, 0])
    o_b = jnp.flip(gla_one_direction(jnp.flip(q, 1), jnp.flip(k, 1), jnp.flip(v, 1),
                                     jnp.flip(logg[:, :, 1], 1)), 1)
    o = o_f + o_b
    o = o * lax.rsqrt(jnp.mean(o * o, axis=-1, keepdims=True) + EPS)
    o = o * norm_g.astype(f32).reshape(GLA_HEADS, GLA_DV)
    o = o * jax.nn.silu(gr.astype(f32).reshape(b_, t, GLA_HEADS, GLA_DV))
    return o.reshape(b_, t, GLA_HEADS * GLA_DV).astype(gq.dtype)


def partial_rope(x, pos):
    half = ROPE_DIM // 2
    inv = 1.0 / (ROPE_THETA ** (jnp.arange(0, ROPE_DIM, 2, dtype=jnp.float32) / ROPE_DIM))
    ang = pos.astype(jnp.float32)[:, None] * inv[None, :]
    cos = jnp.cos(ang)[None, :, None, :]
    sin = jnp.sin(ang)[None, :, None, :]
    xf = x.astype(jnp.float32)
    x1, x2, rest = xf[..., :half], xf[..., half:ROPE_DIM], xf[..., ROPE_DIM:]
    return jnp.concatenate([x1 * cos - x2 * sin, x2 * cos + x1 * sin, rest], axis=-1)


def window_attention(q, k, v, sink):
    b_, t = q.shape[0], q.shape[1]
    n = t // ATT_BLOCK
    qb = q.reshape(b_, n, ATT_BLOCK, ATT_KV_HEADS, ATT_GROUP, ATT_HD)

    def key_windows(a):
        ap = jnp.pad(a, ((0, 0), (ATT_BLOCK, ATT_BLOCK), (0, 0), (0, 0)))
        ap = ap.reshape(b_, n + 2, ATT_BLOCK, ATT_KV_HEADS, ATT_HD)
        return jnp.concatenate([ap[:, :-2], ap[:, 1:-1], ap[:, 2:]], axis=2)

    kw = key_windows(k)
    vw = key_windows(v)
    s = jnp.einsum('bnqkgd,bnskd->bnkgqs', qb, kw) * (ATT_HD ** -0.5)
    qpos = jnp.arange(n)[:, None] * ATT_BLOCK + jnp.arange(ATT_BLOCK)[None, :]
    kpos = jnp.arange(n)[:, None] * ATT_BLOCK - ATT_BLOCK + jnp.arange(3 * ATT_BLOCK)[None, :]
    valid = ((kpos[:, None, :] >= 0) & (kpos[:, None, :] < t)
             & (jnp.abs(qpos[:, :, None] - kpos[:, None, :]) <= WINDOW))
    s = jnp.where(valid[None, :, None, None], s, -jnp.inf)
    sk = sink.astype(jnp.float32).reshape(1, 1, ATT_KV_HEADS, ATT_GROUP, 1, 1)
    m = jnp.maximum(jnp.max(s, axis=-1, keepdims=True), sk)
    p = jnp.exp(s - m)
    denom = jnp.sum(p, axis=-1, keepdims=True) + jnp.exp(sk - m)
    o = jnp.einsum('bnkgqs,bnskd->bnqkgd', p / denom, vw)
    return o.reshape(b_, t, ATT_HEADS * ATT_HD)


def attention_branch(aq, ak, av, sink):
    b_, t, _ = aq.shape
    pos = jnp.arange(t)
    q = partial_rope(aq.reshape(b_, t, ATT_HEADS, ATT_HD), pos)
    k = partial_rope(ak.reshape(b_, t, ATT_KV_HEADS, ATT_HD), pos)
    v = av.astype(jnp.float32).reshape(b_, t, ATT_KV_HEADS, ATT_HD)
    return window_attention(q, k, v, sink).astype(aq.dtype)


def pool_branch(u, pool_w, pool_scale):
    b_, t, c = u.shape
    uf = u.astype(jnp.float32)
    cs = jnp.concatenate([jnp.zeros((b_, 1, c), jnp.float32), jnp.cumsum(uf, axis=1)], axis=1)
    pos = jnp.arange(t)
    outs = []
    for g, w in enumerate(POOL_WIDTHS):
        sl = slice(g * POOL_GROUP, (g + 1) * POOL_GROUP)
        lo = jnp.clip(pos - w // 2, 0, t)
        hi = jnp.clip(pos + w // 2, 0, t)
        csg = cs[:, :, sl]
        mean = (jnp.take(csg, hi, axis=1) - jnp.take(csg, lo, axis=1)) / (hi - lo).astype(jnp.float32)[None, :, None]
        outs.append(mean - uf[:, :, sl])
    d = jnp.stack(outs, axis=2)
    y = jnp.einsum('btgc,gcd->btgd', d, pool_w.astype(jnp.float32)).reshape(b_, t, c)
    return (y * pool_scale.astype(jnp.float32)).astype(u.dtype)


def mixer(xn, w_in, conv_w, conv_b, conv_ln_g, conv_ln_b, gla_w_up, gla_b_up, gla_norm_g,
          attn_sink, pool_w, pool_scale, w_branch, w_out):
    b_, t, _ = xn.shape
    h = xn @ w_in
    glu, gq, gk, gv, gr, glr, aq, ak, av, pin, gate = jnp.split(h, IN_OFFSETS, axis=-1)
    ya = conv_module(glu, conv_w, conv_b, conv_ln_g, conv_ln_b)
    yb = gla_branch(gq, gk, gv, gr, glr, gla_w_up, gla_b_up, gla_norm_g)
    yc = attention_branch(aq, ak, av, attn_sink)
    yd = pool_branch(pin, pool_w, pool_scale)
    ys = jnp.stack([ya, yb, yc, yd], axis=2)
    proj = jnp.einsum('btnc,ncd->btnd', ys, w_branch)
    g = jax.nn.sigmoid(gate.reshape(b_, t, N_BRANCH, D_MODEL))
    merged = jnp.sum(g * proj, axis=2)
    return merged @ w_out


def setup_inputs(seed: int = 0) -> dict:
    key = jax.random.key(seed)
    ks = jax.random.split(key, 19)
    nrm = jax.random.normal
    f32 = jnp.float32
    L = DEPTH
    return {
        "x": nrm(ks[0], (BATCH, SEQ, D_MODEL), f32),
        "norm_mix_g": 1.0 + 0.02 * nrm(ks[1], (L, D_MODEL), f32),
        "w_in": nrm(ks[2], (L, D_MODEL, D_IN), f32) * D_MODEL ** -0.5,
        "conv_w": nrm(ks[3], (L, CONV_K, CONV_CH), f32) * CONV_K ** -0.5,
        "conv_b": 0.02 * nrm(ks[4], (L, CONV_CH), f32),
        "conv_ln_g": 1.0 + 0.02 * nrm(ks[5], (L, CONV_CH), f32),
        "conv_ln_b": 0.02 * nrm(ks[6], (L, CONV_CH), f32),
        "gla_w_up": nrm(ks[7], (L, 2, GLA_RANK, GLA_HEADS * GLA_DK), f32) * GLA_RANK ** -0.5,
        "gla_b_up": 0.01 * nrm(ks[8], (L, 2, GLA_HEADS * GLA_DK), f32),
        "gla_norm_g": 1.0 + 0.02 * nrm(ks[9], (L, GLA_HEADS * GLA_DV), f32),
        "attn_sink": 0.5 * nrm(ks[10], (L, ATT_HEADS), f32),
        "pool_w": nrm(ks[11], (L, len(POOL_WIDTHS), POOL_GROUP, POOL_GROUP), f32) * POOL_GROUP ** -0.5,
        "pool_scale": 1.0 + 0.02 * nrm(ks[12], (L, BRANCH_WIDTH), f32),
        "w_branch": nrm(ks[13], (L, N_BRANCH, BRANCH_WIDTH, D_MODEL), f32) * BRANCH_WIDTH ** -0.5,
        "w_out": nrm(ks[14], (L, D_MODEL, D_MODEL), f32) * D_MODEL ** -0.5,
        "norm_ffn_g": 1.0 + 0.02 * nrm(ks[15], (L, D_MODEL), f32),
        "w_ffn_up": nrm(ks[16], (L, D_MODEL, D_FF), f32) * D_MODEL ** -0.5,
        "w_ffn_down": nrm(ks[17], (L, D_FF, D_MODEL), f32) * D_FF ** -0.5,
        "final_norm_g": 1.0 + 0.02 * nrm(ks[18], (D_MODEL,), f32),
    }


def reference(x, norm_mix_g, w_in, conv_w, conv_b, conv_ln_g, conv_ln_b, gla_w_up, gla_b_up,
              gla_norm_g, attn_sink, pool_w, pool_scale, w_branch, w_out, norm_ffn_g,
              w_ffn_up, w_ffn_down, final_norm_g):
    for l in range(DEPTH):
        xn = rms_norm(x, norm_mix_g[l])
        x = x + mixer(xn, w_in[l], conv_w[l], conv_b[l], conv_ln_g[l], conv_ln_b[l],
                      gla_w_up[l], gla_b_up[l], gla_norm_g[l], attn_sink[l],
                      pool_w[l], pool_scale[l], w_branch[l], w_out[l])
        hn = rms_norm(x, norm_ffn_g[l])
        x = x + jnp.square(jax.nn.relu(hn @ w_ffn_up[l])) @ w_ffn_down[l]
    return rms_norm(x, final_norm_g)
```

```python
import math
import numpy as np
import concourse.bass as bass
import concourse.mybir as mybir
from concourse.bass_utils import run_bass_kernel_spmd
from contextlib import ExitStack

F32 = mybir.dt.float32
BF16 = mybir.dt.bfloat16
ALU = mybir.AluOpType
AF = mybir.ActivationFunctionType
AX = mybir.AxisListType

D = 1024
KC = 8
NT = 2048
TL = 512
NTL = 4
HB = 128
NE = NT + 2 * HB
NB = 16
SEQ = 8192
EPS = 1e-6
LNC = -0.5 * math.log(32.0)

MIXG, FFNG, FING, CONVB, LNG, LNB, GLAG, SINK, PSC, CONVW, WUP, POOLW, NSP = 0, 8, 16, 24, 26, 28, 30, 32, 36, 38, 100, 356, 612
IDENT, ONES, ONESBLK, PMAT, TRIF, TRIB, NCBF = 0, 128, 256, 384, 512, 640, 768
BDM, HEADM, INVW, SCANM, NCST = 768, 1024, 1028, 1032, 1544
AMASK, PCORR, COEFF, COEFB, NPC = 0, 1152, 1184, 1196, 1208

IN_OFF = dict(glu_a=0, glu_g=256, gq=512, gk=640, gv=768, gr=1024, glr=1280, aq=1312, ak=1568, av=1696, pin=1824, gate=2080)


class Prog:
    ENGS = ("pe", "act", "dve", "pool", "sp")

    def __init__(self, nc, stack):
        self.nc = nc
        self.stack = stack
        self.ops = []
        self.last_w = {}
        self.readers = {}
        self.sems = {}
        self.semcnt = {}
        self.nsb = 0

    def sem(self, key):
        if key not in self.sems:
            self.sems[key] = self.stack.enter_context(self.nc.semaphore("s%d" % len(self.sems)))
            self.semcnt[key] = 0
        return self.sems[key]

    def sb(self, shape, dt, name=None):
        self.nsb += 1
        return self.stack.enter_context(self.nc.sbuf_tensor("s_" + (name or ("sb%d" % self.nsb)), list(shape), dt))

    def ps(self, shape, dt=F32, name=None):
        self.nsb += 1
        return self.stack.enter_context(self.nc.psum_tensor(name or ("ps%d" % self.nsb), list(shape), dt))

    def op(self, eng, fn, r=(), w=(), dma=None):
        idx = len(self.ops)
        deps = set()
        for k in r:
            x = self.last_w.get(k)
            if x is not None:
                deps.add(x)
        for k in w:
            x = self.last_w.get(k)
            if x is not None:
                deps.add(x)
            rl = self.readers.get(k)
            if rl:
                deps.update(rl)
        if dma is not None:
            skey = ("dma", dma)
            self.sem(skey)
            self.semcnt[skey] += 16
        else:
            skey = ("eng", eng)
            self.sem(skey)
            self.semcnt[skey] += 1
        done = (skey, self.semcnt[skey])
        self.ops.append((eng, fn, deps, done, dma is not None))
        for k in w:
            self.last_w[k] = idx
            self.readers[k] = []
        for k in r:
            self.readers.setdefault(k, []).append(idx)
        return idx

    def barrier(self, eng, fn, key):
        last = {}
        dmas = []
        start = getattr(self, "_bar_at", 0)
        for i in range(len(self.ops) - 1, -1, -1):
            o = self.ops[i]
            if o[4]:
                if i >= start:
                    dmas.append(i)
            elif o[0] not in last:
                last[o[0]] = i
            if i < start and len(last) >= 4:
                break
        idx = self.op(eng, fn, w=[key])
        e, f, deps, done, isd = self.ops[idx]
        deps.update(last.values())
        deps.update(dmas)
        deps.discard(idx)
        self._bar_at = idx
        return idx

    def emit(self, final_waits=()):
        nc = self.nc
        known = {e: {} for e in self.ENGS}
        per = {e: [] for e in self.ENGS}
        for (e, fn, deps, done, isdma) in self.ops:
            need = {}
            for d in deps:
                (de, _, _, (sk, val), ddma) = self.ops[d]
                if de == "pe" and e == "pe" and not ddma:
                    continue
                if need.get(sk, 0) < val:
                    need[sk] = val
            waits = []
            kn = known[e]
            for sk, val in need.items():
                if kn.get(sk, 0) >= val:
                    continue
                kn[sk] = val
                waits.append((sk, val))
            per[e].append((waits, fn, done, isdma))
        block = self.stack.enter_context(nc.Block())
        sems = self.sems

        def run(engobj, lst, extra=()):
            for waits, fn, (sk, val), isdma in lst:
                for wk, wv in waits:
                    engobj.wait_ge(sems[wk], wv)
                ins = fn(engobj)
                ins.then_inc(sems[sk], 16 if isdma else 1)
            for wk, wv in extra:
                engobj.wait_ge(sems[wk], wv)

        fin = [self.ops[i][3] for i in final_waits]

        @block.tensor
        def _(e):
            run(e, per["pe"])

        @block.scalar
        def _(e):
            run(e, per["act"])

        @block.vector
        def _(e):
            run(e, per["dve"])

        @block.gpsimd
        def _(e):
            run(e, per["pool"])

        @block.sync
        def _(e):
            run(e, per["sp"], fin)


class Arena:
    def __init__(self, t, nwords):
        self.t = t
        self.n = nwords
        self.off = 0

    def reset(self):
        self.off = 0

    def f32(self, n):
        n8 = (n + 7) // 8 * 8
        assert self.off + n8 <= self.n, ("arena overflow", self.off, n8, self.n)
        v = self.t[:, self.off:self.off + n]
        self.off += n8
        return v

    def bf16(self, n):
        w = (n + 1) // 2
        w = (w + 7) // 8 * 8
        assert self.off + w <= self.n, ("arena overflow", self.off, w, self.n)
        v = self.t[:, self.off:self.off + w].bitcast(BF16)[:, 0:n]
        self.off += w
        return v


class WStream:
    def __init__(self, P, dram, nch, wst, wbf, look=1):
        self.P, self.dram, self.nch, self.wst, self.wbf, self.look = P, dram, nch, wst, wbf, look
        self.issued = 0
        self.nxt = 0
        self.live = {}

    def _issue(self, i):
        s = i % len(self.wst)
        b = i % len(self.wbf)
        assert self.live.get(b) is None, ("weight slot still live", i, b, self.live.get(b))
        self.live[b] = i
        wst, wbf, dram = self.wst, self.wbf, self.dram
        self.P.op("sp", lambda e: e.dma_start(out=wst[s][:], in_=dram[i]), w=["wst%d" % s], dma="wst%d" % s)
        self.P.op("pool", lambda e: e.tensor_copy(out=wbf[b][:], in_=wst[s][:]), r=["wst%d" % s], w=["wbf%d" % b])

    def get(self):
        i = self.nxt
        self.nxt += 1
        while self.issued < min(self.nch, i + 1 + self.look):
            self._issue(self.issued)
            self.issued += 1
        b = i % len(self.wbf)
        return self.wbf[b], "wbf%d" % b, b

    def done(self, b):
        self.live[b] = None


class Ctx:
    pass


def setup_common(nc, P, C, nch, look=1, nbf=6):
    C.nc, C.P = nc, P
    C.ws_d = nc.dram_tensor("ws", [nch, 128, 1024], F32, kind="ExternalInput").ap()
    C.sp_d = nc.dram_tensor("sp", [128, NSP], F32, kind="ExternalInput").ap()
    C.cst_d = nc.dram_tensor("cst", [128, NCST], F32, kind="ExternalInput").ap()
    C.banks = [P.ps([128, 512], F32, name="bank%d" % i) for i in range(8)]
    C.wst = [P.sb([128, 1024], F32, name="wst%d" % i) for i in range(2)]
    C.wbf = [P.sb([128, 1024], BF16, name="wbf%d" % i) for i in range(nbf)]
    C.W = WStream(P, C.ws_d, nch, C.wst, C.wbf, look=look)
    C.spt = P.sb([128, NSP], F32, name="spt")
    C.cbf = P.sb([128, NCBF], BF16, name="cbf")
    C.cf = P.sb([128, NCST - NCBF], F32, name="cf")
    C.wupb = P.sb([128, 256], BF16, name="wupb")
    P.op("sp", lambda e: e.dma_start(out=C.spt[:], in_=C.sp_d[:, :]), w=["spt"], dma="spt")
    P.op("sp", lambda e: e.dma_start(out=C.cf[:], in_=C.cst_d[:, NCBF:NCST]), w=["cf"], dma="cf")
    P.op("pool", lambda e: e.tensor_copy(out=C.wupb[:], in_=C.spt[:, WUP:WUP + 256]), r=["spt"], w=["wupb"])


def bfview(bank):
    return bank[:].bitcast(BF16)


def rms_tile(C, xsrc, gcol, dst, ntok, rkeys, wkeys, ar, bank, bkey, tag):
    P = C.P
    sq = ar["sq"][:, 0:KC * ntok].rearrange("p (k t) -> p k t", k=KC)
    rs = ar["rs"][:, 0:ntok]
    for kc in range(KC):
        P.op("act", lambda e, kc=kc: e.activation(out=sq[:, kc, :], in_=xsrc(kc), func=AF.Square),
             r=rkeys + ["AE"], w=["sq%d" % kc])
    for kc in range(KC):
        P.op("pe", lambda e, kc=kc: e.matmul(bank[:, 0:ntok], lhsT=C.cbf[:, ONES:ONES + 128], rhs=sq[:, kc, :],
                                              start=(kc == 0), stop=(kc == KC - 1)),
             r=["sq%d" % kc, "cbf"], w=[bkey])
    P.op("act", lambda e: e.activation(out=rs, in_=bank[:, 0:ntok], func=AF.Sqrt, bias=EPS, scale=1.0 / D),
         r=[bkey, "AE"], w=["rs"])
    P.op("dve", lambda e: e.reciprocal(out=rs, in_=rs), r=["rs"], w=["rs"])
    for kc in range(KC):
        P.op("dve", lambda e, kc=kc: e.scalar_tensor_tensor(out=dst(kc), in0=xsrc(kc), scalar=C.spt[:, gcol + kc:gcol + kc + 1],
                                                             in1=rs, op0=ALU.mult, op1=ALU.mult),
             r=rkeys + ["rs", "spt"], w=[wkeys[kc]])


def barrier(C):
    C.P.barrier("pool", lambda e: e.memset(C.scr[:, 0:8], 0.0), "AE")


def gla_phase1(C, xn_tile, xn_blk, ar, with_q, kt_f, kt_b, qt_f, qt_b, v_sb, U, etf, etb):
    P, W, banks = C.P, C.W, C.banks
    w_lr, k_lr, b_lr = W.get()
    if with_q:
        w_q, k_q, b_q = W.get()
    w_k, k_k, b_k = W.get()
    glra = ar["glra"]
    t1 = [ar["t1f"], ar["t1b"]]
    t2 = [ar["t2f"], ar["t2b"]]
    t3 = [ar["t3f"], ar["t3b"]]
    t4 = [ar["t4f"], ar["t4b"]]
    t5 = ar["t5"]
    P.op("pool", lambda e: e.memset(glra[32:33, :], 1.0), r=["AE"], w=["glra1"])
    for t in range(NTL):
        sl = slice(t * TL, (t + 1) * TL)
        for kc in range(KC):
            P.op("pe", lambda e, kc=kc, t=t: e.matmul(banks[6][0:32, :], lhsT=w_lr[:, kc * 128:kc * 128 + 32], rhs=xn_tile(kc, t),
                                                       start=(kc == 0), stop=(kc == KC - 1)),
                 r=[k_lr, "xn:%d:%d" % (kc, t)], w=["b6"])
        P.op("act", lambda e: e.activation(out=glra[0:32, :], in_=banks[6][0:32, :], func=AF.Copy), r=["b6", "AE"], w=["glra"])
        for d in range(2):
            bk = banks[4 + d]
            P.op("pe", lambda e, d=d, bk=bk: e.matmul(bk[:, :], lhsT=C.wupb[0:33, d * 128:(d + 1) * 128], rhs=glra[0:33, :],
                                                       start=True, stop=True),
                 r=["glra", "glra1", "wupb"], w=["b%d" % (4 + d)])
            P.op("act", lambda e, d=d, bk=bk: e.activation(out=t1[d], in_=bk[:, :], func=AF.Exp, scale=-1.0),
                 r=["b%d" % (4 + d), "AE"], w=["t1%d" % d])
            P.op("act", lambda e, d=d: e.activation(out=t1[d], in_=t1[d], func=AF.Ln, bias=1.0), r=["t1%d" % d], w=["t1%d" % d])
            P.op("dve", lambda e, d=d: e.tensor_tensor_scan(out=t2[d], data0=C.cf[:, SCANM - NCBF:SCANM - NCBF + 512], data1=t1[d],
                                                             initial=0.0, op0=ALU.mult, op1=ALU.add),
                 r=["t1%d" % d, "cf", "AE"], w=["t2%d" % d])
            et = etf if d == 0 else etb
            c3 = t2[d].rearrange("p (a c) -> p a c", a=4)
            P.op("act", lambda e, et=et, c3=c3, t=t: e.activation(out=et[:, 4 * t:4 * t + 4], in_=c3[:, :, 127], func=AF.Exp, scale=-1.0 / 16),
                 r=["t2%d" % d, "AE"], w=["et%d:%d" % (d, t)])
            if d == 0:
                src = t2[0]
            else:
                P.op("dve", lambda e: e.tensor_tensor(out=t5, in0=t1[1], in1=t2[1], op=ALU.subtract), r=["t11", "t21", "AE"], w=["t5"])
                t53 = t5.rearrange("p (a c) -> p a c", a=4)
                P.op("dve", lambda e, t53=t53, c3=c3: e.tensor_tensor(out=t53, in0=t53, in1=c3[:, :, 127:128].to_broadcast([128, 4, 128]), op=ALU.add),
                     r=["t5", "t21"], w=["t5"])
                src = t5
            skey = "t20" if d == 0 else "t5"
            if with_q:
                P.op("act", lambda e, d=d, src=src: e.activation(out=t3[d], in_=src, func=AF.Exp, scale=-1.0 / 16, bias=LNC),
                     r=[skey, "AE"], w=["t3%d" % d])
            P.op("act", lambda e, d=d, src=src: e.activation(out=t4[d], in_=src, func=AF.Exp, scale=1.0 / 16), r=[skey, "AE"], w=["t4%d" % d])
        if with_q:
            for kc in range(KC):
                P.op("pe", lambda e, kc=kc, t=t: e.matmul(banks[0][:, :], lhsT=w_q[:, kc * 128:(kc + 1) * 128], rhs=xn_tile(kc, t),
                                                           start=(kc == 0), stop=(kc == KC - 1)),
                     r=[k_q, "xn:%d:%d" % (kc, t)], w=["b0"])
            for d, qt in ((0, qt_f), (1, qt_b)):
                P.op("dve", lambda e, d=d, qt=qt, sl=sl: e.tensor_tensor(out=qt[:, sl], in0=banks[0][:, :], in1=t3[d], op=ALU.mult),
                     r=["b0", "t3%d" % d, "AE"], w=["qt%d:%d" % (d, t)])
        for kc in range(KC):
            P.op("pe", lambda e, kc=kc, t=t: e.matmul(banks[1][:, :], lhsT=w_k[:, kc * 128:(kc + 1) * 128], rhs=xn_tile(kc, t),
                                                       start=(kc == 0), stop=(kc == KC - 1)),
                 r=[k_k, "xn:%d:%d" % (kc, t)], w=["b1"])
        for d, kt in ((0, kt_f), (1, kt_b)):
            P.op("dve", lambda e, d=d, kt=kt, sl=sl: e.tensor_tensor(out=kt[:, sl], in0=banks[1][:, :], in1=t4[d], op=ALU.mult),
                 r=["b1", "t4%d" % d, "AE"], w=["kt%d:%d" % (d, t)])
    W.done(b_lr)
    W.done(b_k)
    if with_q:
        W.done(b_q)
    w_v0, k_v0, b_v0 = W.get()
    w_v1, k_v1, b_v1 = W.get()
    for n2 in range(NB // 2):
        bk = banks[2 + (n2 % 2)]
        bkey = "b%d" % (2 + (n2 % 2))
        for j in range(2):
            n = 2 * n2 + j
            for half, (wv, kv) in enumerate(((w_v0, k_v0), (w_v1, k_v1))):
                for kc in range(KC):
                    P.op("pe", lambda e, kc=kc, n=n, j=j, half=half, wv=wv, bk=bk: e.matmul(
                        bk[:, j * 256 + half * 128:j * 256 + half * 128 + 128], lhsT=xn_blk(kc, n), rhs=wv[:, kc * 128:(kc + 1) * 128],
                        start=(kc == 0), stop=(kc == KC - 1)),
                         r=[kv, "xn:%d:%d" % (kc, n // 4)], w=[bkey])
        P.op("act", lambda e, n2=n2, bk=bk: e.activation(out=v_sb[:, 2 * n2:2 * n2 + 2, :], in_=bk[:, :].rearrange("p (a c) -> p a c", a=2), func=AF.Copy),
             r=[bkey, "AE"], w=["v:%d" % n2])
    W.done(b_v0)
    W.done(b_v1)
    kh = [ar["kh0"], ar["kh1"]]
    khs = [ar["khs0"], ar["khs1"]]
    tmpU = ar["tmpU"]
    P.op("pool", lambda e: e.memset(U.rearrange("p a n c -> p (a n c)"), 0.0), r=["AE"], w=["Uz"])
    bT = bfview(banks[6])
    for n in range(NB):
        bs = slice(n * 128, (n + 1) * 128)
        for d, (kt, et) in enumerate(((kt_f, etf), (kt_b, etb))):
            P.op("dve", lambda e, d=d, kt=kt, et=et, n=n, bs=bs: e.tensor_scalar(out=kh[d], in0=kt[:, bs], scalar1=et[:, n:n + 1], scalar2=None, op0=ALU.mult),
                 r=["kt%d:%d" % (d, n // 4), "et%d:%d" % (d, n // 4), "AE"], w=["kh%d" % d])
            P.op("pe", lambda e, d=d: e.transpose(out=bT[:, d * 128:(d + 1) * 128], in_=kh[d], identity=C.cbf[:, IDENT:IDENT + 128]),
                 r=["kh%d" % d, "cbf"], w=["b6:%d" % d])
            P.op("act", lambda e, d=d: e.activation(out=khs[d], in_=bT[:, d * 128:(d + 1) * 128], func=AF.Copy), r=["b6:%d" % d, "AE"], w=["khs%d" % d])
            P.op("pe", lambda e, d=d, n=n: e.matmul(banks[7][:, d * 256:(d + 1) * 256], lhsT=khs[d], rhs=v_sb[:, n, :], start=True, stop=True),
                 r=["khs%d" % d, "v:%d" % (n // 2)], w=["b7:%d" % d])
        P.op("dve", lambda e: e.tensor_tensor(out=tmpU.rearrange("p (a c) -> p a c", a=2), in0=banks[7][:, :].rearrange("p (a c) -> p a c", a=2),
                                              in1=C.cf[:, BDM - NCBF:BDM - NCBF + 256].unsqueeze(1).to_broadcast([128, 2, 256]), op=ALU.mult),
             r=["b7:0", "b7:1", "cf", "AE"], w=["tmpU"])
        for d in range(2):
            P.op("dve", lambda e, d=d, n=n: e.tensor_reduce(out=U[:, d, n, 0:64], in_=tmpU[:, d * 256:(d + 1) * 256].rearrange("p (h c) -> p c h", h=4),
                                                             axis=AX.X, op=ALU.add),
                 r=["tmpU", "Uz", "AE"], w=["U:%d:%d" % (d, n)])


def load_consts(C, ar):
    P = C.P
    tmp = ar.f32(NCBF)
    P.op("sp", lambda e: e.dma_start(out=tmp, in_=C.cst_d[:, 0:NCBF]), r=["AE"], w=["cst_tmp"], dma="cst")
    P.op("dve", lambda e: e.tensor_copy(out=C.cbf[:], in_=tmp), r=["cst_tmp"], w=["cbf"])
    C.scr = C.P.sb([128, 8], F32, name="scr")


SUM_CH = 4


def build_sum():
    nc = bass.Bass("TRN2", target_bir_lowering=False)
    st = ExitStack()
    P = Prog(nc, st)
    C = Ctx()
    setup_common(nc, P, C, SUM_CH)
    xin = nc.dram_tensor("xT", [D, NT], F32, kind="ExternalInput").ap()
    sout = nc.dram_tensor("so", [128, 2 * 65], F32, kind="ExternalOutput").ap()
    xn = P.sb([128, KC, NT], BF16, name="xn")
    ysb = P.sb([128, 4, NT], BF16, name="ysb")
    art = P.sb([128, 16384], F32, name="arena")
    A = Arena(art, 16384)
    load_consts(C, A)
    barrier(C)
    A.reset()
    xt = [A.f32(KC * TL), A.f32(KC * TL)]
    ar = dict(sq=A.bf16(KC * TL), rs=A.f32(TL))
    xv = xin.rearrange("(k p) t -> p k t", p=128)
    for t in range(NTL):
        b = t % 2
        xb = xt[b].rearrange("p (k t) -> p k t", k=KC)
        P.op("sp", lambda e, xb=xb, t=t: e.dma_start(out=xb, in_=xv[:, :, t * TL:(t + 1) * TL]), r=["AE"], w=["xt%d" % b], dma="xt%d" % b)
        rms_tile(C, lambda kc, xb=xb: xb[:, kc, :], MIXG, lambda kc, t=t: xn[:, kc, t * TL:(t + 1) * TL], TL,
                 ["xt%d" % b], ["xn:%d:%d" % (kc, t) for kc in range(KC)], ar, C.banks[3], "b3", "s%d" % t)
    barrier(C)
    A.reset()
    g = dict(glra=A.bf16(TL), t1f=A.f32(TL), t1b=A.f32(TL), t2f=A.f32(TL), t2b=A.f32(TL), t3f=None, t3b=None,
             t4f=A.f32(TL), t4b=A.f32(TL), t5=A.f32(TL), kh0=A.bf16(128), kh1=A.bf16(128), khs0=A.bf16(128), khs1=A.bf16(128),
             tmpU=A.f32(512))
    U = A.f32(2 * 16 * 65).rearrange("p (a n c) -> p a n c", a=2, n=16)
    etf = A.f32(16)
    etb = A.f32(16)
    Sf = A.f32(65)
    Sb = A.f32(65)
    v_sb = ysb[:, 2:4, :].rearrange("p c t -> p (c t)").rearrange("p (n c) -> p n c", n=16)
    gla_phase1(C, lambda kc, t: xn[:, kc, t * TL:(t + 1) * TL], lambda kc, n: xn[:, kc, n * 128:(n + 1) * 128], g, False,
               ysb[:, 0, :], ysb[:, 1, :], None, None, v_sb, U, etf, etb)
    ukeys = lambda d: ["U:%d:%d" % (d, n) for n in range(NB)]
    for S, d in ((Sf, 0), (Sb, 1)):
        P.op("pool", lambda e, S=S: e.memset(S[:, 0:64], 0.0), r=["AE"], w=["S%d" % d])
        P.op("pool", lambda e, S=S: e.memset(S[:, 64:65], 1.0), r=["AE"], w=["S%db" % d])
    for n in range(NB):
        P.op("dve", lambda e, n=n: e.scalar_tensor_tensor(out=Sf, in0=Sf, scalar=etf[:, n:n + 1], in1=U[:, 0, n, :], op0=ALU.mult, op1=ALU.add),
             r=["S0", "S0b", "U:0:%d" % n, "Uz", "et0:%d" % (n // 4), "AE"], w=["S0"])
    for n in range(NB - 1, -1, -1):
        P.op("dve", lambda e, n=n: e.scalar_tensor_tensor(out=Sb, in0=Sb, scalar=etb[:, n:n + 1], in1=U[:, 1, n, :], op0=ALU.mult, op1=ALU.add),
             r=["S1", "S1b", "U:1:%d" % n, "Uz", "et1:%d" % (n // 4), "AE"], w=["S1"])
    o1 = P.op("sp", lambda e: e.dma_start(out=sout[:, 0:65], in_=Sf), r=["S0"], dma="o1")
    o2 = P.op("sp", lambda e: e.dma_start(out=sout[:, 65:130], in_=Sb), r=["S1"], dma="o2")
    P.emit(final_waits=[o1, o2])
    return nc, st


def _chunk_cols(Wm, cols):
    sub = Wm[:, cols]
    if sub.shape[1] < 128:
        sub = np.concatenate([sub, np.zeros((sub.shape[0], 128 - sub.shape[1]), np.float32)], 1)
    return np.ascontiguousarray(sub.reshape(KC, 128, 128).transpose(1, 0, 2)).reshape(128, 1024)


def win_chunk(w_in_l, name):
    o = IN_OFF
    if name == "glr":
        cols = np.arange(o["glr"], o["glr"] + 32)
    elif name in ("gq", "gk", "ak", "av"):
        cols = np.arange(o[name], o[name] + 128)
    elif name in ("gv0", "gv1", "gr0", "gr1", "pin0", "pin1"):
        b = o[name[:-1]] + 128 * int(name[-1])
        cols = np.arange(b, b + 128)
    elif name in ("A0", "A1"):
        b = o["glu_a"] + 128 * int(name[-1])
        cols = np.arange(b, b + 128)
    elif name in ("G0", "G1"):
        b = o["glu_g"] + 128 * int(name[-1])
        cols = np.arange(b, b + 128)
    elif name == "QA":
        cols = np.concatenate([np.arange(o["aq"], o["aq"] + 64), np.arange(o["aq"] + 128, o["aq"] + 192)])
    elif name == "QB":
        cols = np.concatenate([np.arange(o["aq"] + 64, o["aq"] + 128), np.arange(o["aq"] + 192, o["aq"] + 256)])
    else:
        raise KeyError(name)
    return _chunk_cols(w_in_l, cols)


def sum_stream(inp, l):
    w = inp["w_in"][l]
    return np.stack([win_chunk(w, n) for n in ("glr", "gk", "gv0", "gv1")], 0)


def small_params(inp, l):
    sp = np.zeros((128, NSP), np.float32)
    sp[:, MIXG:MIXG + 8] = inp["norm_mix_g"][l].reshape(8, 128).T
    sp[:, FFNG:FFNG + 8] = inp["norm_ffn_g"][l].reshape(8, 128).T
    sp[:, FING:FING + 8] = inp["final_norm_g"].reshape(8, 128).T
    sp[:, CONVB:CONVB + 2] = inp["conv_b"][l].reshape(2, 128).T
    sp[:, LNG:LNG + 2] = inp["conv_ln_g"][l].reshape(2, 128).T
    sp[:, LNB:LNB + 2] = inp["conv_ln_b"][l].reshape(2, 128).T
    sp[:, GLAG:GLAG + 2] = inp["gla_norm_g"][l].reshape(2, 128).T
    sp[:, SINK:SINK + 4] = inp["attn_sink"][l][None, :]
    sp[:, PSC:PSC + 2] = inp["pool_scale"][l].reshape(2, 128).T
    cw = inp["conv_w"][l]
    for c in range(2):
        sp[:, CONVW + 31 * c:CONVW + 31 * (c + 1)] = cw[:, c * 128:(c + 1) * 128].T
    wu = inp["gla_w_up"][l]
    bu = inp["gla_b_up"][l]
    sp[0:16, WUP:WUP + 128] = wu[0]
    sp[16:32, WUP + 128:WUP + 256] = wu[1]
    sp[32, WUP:WUP + 128] = bu[0]
    sp[32, WUP + 128:WUP + 256] = bu[1]
    pw = inp["pool_w"][l]
    for c in range(2):
        for gg in range(2):
            g = 2 * c + gg
            sp[gg * 64:(gg + 1) * 64, POOLW + c * 128 + gg * 64:POOLW + c * 128 + (gg + 1) * 64] = pw[g]
    return sp


def const_block():
    c = np.zeros((128, NCST), np.float32)
    i = np.arange(128)
    c[:, IDENT:IDENT + 128] = np.eye(128, dtype=np.float32)
    c[:, ONES:ONES + 128] = 1.0
    c[:, ONESBLK:ONESBLK + 128] = (i[:, None] // 64 == i[None, :] // 64)
    pm = np.zeros((128, 128), np.float32)
    for hb in (0, 64):
        for j in range(8):
            pm[hb + j + 8, hb + j] = -1.0
            pm[hb + j, hb + j + 8] = 1.0
    c[:, PMAT:PMAT + 128] = pm
    c[:, TRIF:TRIF + 128] = (i[:, None] <= i[None, :])
    c[:, TRIB:TRIB + 128] = (i[:, None] >= i[None, :])
    c[:, BDM:BDM + 256] = (i[:, None] // 32 == np.arange(256)[None, :] // 64)
    c[:, HEADM:HEADM + 4] = (i[:, None] // 32 == np.arange(4)[None, :])
    c[:, INVW] = np.where(i < 64, 1.0 / 2, 1.0 / 4)
    c[:, INVW + 1] = np.where(i < 64, 1.0 / 8, 1.0 / 16)
    sm = np.ones(512, np.float32)
    sm[::128] = 0.0
    c[:, SCANM:SCANM + 512] = sm[None, :]
    return c


LAYER_STREAM = (["glr", "gq", "gk", "gv0", "gv1", "gr0", "gr1", "QA", "QB", "ak", "av", "A0", "G0", "A1", "G1", "pin0", "pin1"])
N_INPROJ = len(LAYER_STREAM)
LAYER_NCH = N_INPROJ + 8 * 5 + 8 + 4 * 16


def layer_stream(inp, l):
    w = inp["w_in"][l]
    ch = [win_chunk(w, n) for n in LAYER_STREAM]
    wb = inp["w_branch"][l]
    for f in range(8):
        blk = np.stack([wb[n][:, f * 128:(f + 1) * 128].reshape(2, 128, 128).transpose(1, 0, 2) for n in range(4)], 1)
        ch.append(np.ascontiguousarray(blk).reshape(128, 1024))
        for n in range(4):
            b = IN_OFF["gate"] + n * 1024 + f * 128
            ch.append(_chunk_cols(w, np.arange(b, b + 128)))
    wo = inp["w_out"][l]
    for f in range(8):
        ch.append(_chunk_cols(wo, np.arange(f * 128, (f + 1) * 128)))
    wu = inp["w_ffn_up"][l]
    wd = inp["w_ffn_down"][l]
    for g in range(4):
        for c in range(8):
            cc = g * 8 + c
            ch.append(_chunk_cols(wu, np.arange(cc * 128, (cc + 1) * 128)))
        for f in range(8):
            ch.append(_chunk_cols(wd[g * 1024:(g + 1) * 1024], np.arange(f * 128, (f + 1) * 128)))
    out = np.stack(ch, 0)
    assert out.shape[0] == LAYER_NCH
    return out


def percore_consts(core):
    p = core % 4
    pc = np.zeros((128, NPC), np.float32)
    q = np.arange(128)[:, None]
    s = np.arange(384)[None, :]
    base = ((s >= q) & (s <= q + 256))
    for v in range(3):
        m = base.copy()
        if v == 0 and p == 0:
            m &= (s >= 128)
        if v == 2 and p == 3:
            m &= (s < 256)
        pc[:, AMASK + v * 384:AMASK + (v + 1) * 384] = m
    i = np.arange(128)
    wid = [np.where(i < 64, 2, 4), np.where(i < 64, 8, 16)]
    corr = np.ones((128, 2, 16), np.float32)
    for c in range(2):
        w = wid[c].astype(np.float64)
        for j in range(8):
            if p == 0:
                pos = j
                lo = np.maximum(pos - w / 2, 0)
                hi = pos + w / 2
                corr[:, c, j] = w / (hi - lo)
            if p == 3:
                pos = SEQ - 8 + j
                lo = pos - w / 2
                hi = np.minimum(pos + w / 2, SEQ)
                corr[:, c, 8 + j] = w / (hi - lo)
    pc[:, PCORR:PCORR + 32] = corr.reshape(128, 32)
    cf = np.zeros((4, 3), np.float32)
    cb = np.zeros((4, 3), np.float32)
    for r in range(4):
        cf[r] = (1, 0, 1) if r < p else (0, 1, 0)
        cb[r] = (1, 0, 1) if r > p else (0, 1, 0)
    pc[:, COEFF:COEFF + 12] = cf.reshape(1, 12)
    pc[:, COEFB:COEFB + 12] = cb.reshape(1, 12)
    return pc


def rope_tables(core):
    p = core % 4
    pos = (p * NT - HB + np.arange(NE)).astype(np.float32)
    inv = (1.0 / (np.float32(500000.0) ** (np.arange(0, 16, 2, dtype=np.float32) / np.float32(16)))).astype(np.float32)
    ang = pos[:, None] * inv[None, :]
    cs = np.cos(ang).astype(np.float32).T
    sn = np.sin(ang).astype(np.float32).T
    Cm = np.ones((128, NE), np.float32)
    Sm = np.zeros((128, NE), np.float32)
    for hb in (0, 64):
        Cm[hb:hb + 8] = cs
        Cm[hb + 8:hb + 16] = cs
        Sm[hb:hb + 8] = sn
        Sm[hb + 8:hb + 16] = sn
    return np.stack([Cm, Sm], 0)


def build_layer(debug=False):
    nc = bass.Bass("TRN2", target_bir_lowering=False)
    st = ExitStack()
    P = Prog(nc, st)
    C = Ctx()
    setup_common(nc, P, C, LAYER_NCH)
    W, banks = C.W, C.banks
    xin = nc.dram_tensor("xT", [D, NE], F32, kind="ExternalInput").ap()
    pc_d = nc.dram_tensor("pc", [128, NPC], F32, kind="ExternalInput").ap()
    rope_d = nc.dram_tensor("rope", [2, 128, NE], F32, kind="ExternalInput").ap()
    gsum_d = nc.dram_tensor("gsum", [128, 4 * 130], F32, kind="ExternalInput").ap()
    xout = nc.dram_tensor("xo", [D, NT], F32, kind="ExternalOutput").ap()
    yout = nc.dram_tensor("yo", [D, NT], F32, kind="ExternalOutput").ap()
    if debug:
        dbg = nc.dram_tensor("dbg", [128, KC * NT], BF16, kind="ExternalOutput").ap()

    xT = P.sb([128, KC, NT], F32, name="xT")
    xn = P.sb([128, KC, NT], BF16, name="xn")
    xnh = P.sb([128, KC, 2 * HB], BF16, name="xnh")
    ys = P.sb([128, KC, NT], BF16, name="ys")
    pcf = P.sb([128, NPC - AMASK - 1152], F32, name="pcf")
    amb = P.sb([128, 1152], BF16, name="amb")
    AW = 11328
    art = P.sb([128, AW], F32, name="arena")
    A = Arena(art, AW)
    cbf, cf, spt = C.cbf, C.cf, C.spt
    CF = lambda off, n: cf[:, off - NCBF:off - NCBF + n]

    load_consts(C, A)
    tmpm = A.f32(1152)
    P.op("sp", lambda e: e.dma_start(out=tmpm, in_=pc_d[:, AMASK:AMASK + 1152]), r=["AE"], w=["tmpm"], dma="pc1")
    P.op("dve", lambda e: e.tensor_copy(out=amb[:], in_=tmpm), r=["tmpm"], w=["amb"])
    P.op("sp", lambda e: e.dma_start(out=pcf[:], in_=pc_d[:, PCORR:NPC]), w=["pcf"], dma="pc2")
    PCF = lambda off, n: pcf[:, off - PCORR:off - PCORR + n]
    barrier(C)
    A.reset()

    xv = xin.rearrange("(k p) t -> p k t", p=128)
    for kc in range(KC):
        P.op("sp", lambda e, kc=kc: e.dma_start(out=xT[:, kc, :], in_=xin[kc * 128:(kc + 1) * 128, HB:HB + NT]),
             w=["x:%d:%d" % (kc, t) for t in range(NTL)], dma="xl%d" % kc)
    xh = A.f32(KC * 2 * HB).rearrange("p (k t) -> p k t", k=KC)
    P.op("sp", lambda e: e.dma_start(out=xh[:, :, 0:HB], in_=xv[:, :, 0:HB]), r=["AE"], w=["xh0"], dma="xh0")
    P.op("sp", lambda e: e.dma_start(out=xh[:, :, HB:2 * HB], in_=xv[:, :, HB + NT:NE]), r=["AE"], w=["xh1"], dma="xh1")
    arn = dict(sq=A.bf16(KC * TL), rs=A.f32(TL))

    def norm_all(gcol):
        for t in range(NTL):
            rms_tile(C, lambda kc, t=t: xT[:, kc, t * TL:(t + 1) * TL], gcol, lambda kc, t=t: xn[:, kc, t * TL:(t + 1) * TL], TL,
                     ["x:%d:%d" % (kc, t) for kc in range(KC)], ["xn:%d:%d" % (kc, t) for kc in range(KC)], arn, banks[3], "b3", "n%d" % t)

    norm_all(MIXG)
    rms_tile(C, lambda kc: xh[:, kc, :], MIXG, lambda kc: xnh[:, kc, :], 2 * HB, ["xh0", "xh1"], ["xnh:%d" % kc for kc in range(KC)], arn, banks[3], "b3", "nh")
    barrier(C)
    A.reset()

    xn_tile = lambda kc, t: xn[:, kc, t * TL:(t + 1) * TL]
    xn_blk = lambda kc, n: xn[:, kc, n * 128:(n + 1) * 128]
    ext_ranges = [(lambda kc, t=t: xn[:, kc, t * TL:(t + 1) * TL], HB + t * TL, TL, (lambda kc, t=t: "xn:%d:%d" % (kc, t))) for t in range(NTL)]
    ext_ranges.append((lambda kc: xnh[:, kc, 0:HB], 0, HB, lambda kc: "xnh:%d" % kc))
    ext_ranges.append((lambda kc: xnh[:, kc, HB:2 * HB], HB + NT, HB, lambda kc: "xnh:%d" % kc))

    qt_f, qt_b, kt_f, kt_b = ys[:, 0, :], ys[:, 1, :], ys[:, 4, :], ys[:, 5, :]
    v_sb = ys[:, 6:8, :].rearrange("p c t -> p (c t)").rearrange("p (n c) -> p n c", n=16)
    U = A.f32(2 * 16 * 65).rearrange("p (a n c) -> p a n c", a=2, n=16)
    etf = A.f32(16)
    etb = A.f32(16)
    G = A.f32(4 * 130)
    Sst = A.f32(17 * 64 * 2).rearrange("p (d n c) -> p d n c", d=2, n=17)
    wr = A.f32(8)
    Tr = A.f32(4 * 64)
    amark = A.off
    g = dict(glra=A.bf16(TL), t1f=A.f32(TL), t1b=A.f32(TL), t2f=A.f32(TL), t2b=A.f32(TL), t3f=A.f32(TL), t3b=A.f32(TL),
             t4f=A.f32(TL), t4b=A.f32(TL), t5=A.f32(TL), kh0=A.bf16(128), kh1=A.bf16(128), khs0=A.bf16(128), khs1=A.bf16(128),
             tmpU=A.f32(512))
    gla_phase1(C, xn_tile, xn_blk, g, True, kt_f, kt_b, qt_f, qt_b, v_sb, U, etf, etb)
    G4 = G.rearrange("p (r d c) -> p r d c", r=4, d=2)
    P.op("sp", lambda e: e.dma_start(out=G, in_=gsum_d[:, :]), r=["AE"], w=["G"], dma="gs")
    for d, coff in ((0, COEFF), (1, COEFB)):
        co = PCF(coff, 12).rearrange("p (r c) -> p r c", r=4)
        P.op("dve", lambda e, d=d, co=co: e.tensor_tensor(out=wr[:, 0:4], in0=co[:, :, 0], in1=G4[:, :, d, 64], op=ALU.mult), r=["G", "pcf", "AE"], w=["wr"])
        P.op("dve", lambda e, co=co: e.tensor_tensor(out=wr[:, 0:4], in0=wr[:, 0:4], in1=co[:, :, 1], op=ALU.add), r=["wr", "pcf"], w=["wr"])
        P.op("dve", lambda e, d=d, co=co: e.tensor_tensor(out=Tr.rearrange("p (r c) -> p r c", r=4), in0=G4[:, :, d, 0:64],
                                                          in1=co[:, :, 2:3].to_broadcast([128, 4, 64]), op=ALU.mult), r=["G", "pcf", "AE"], w=["Tr"])
        s0 = Sst[:, 0, 0, :] if d == 0 else Sst[:, 1, 16, :]
        P.op("pool", lambda e, s0=s0: e.memset(s0, 0.0), r=["AE"], w=["S0_%d" % d])
        order = range(4) if d == 0 else range(3, -1, -1)
        for r_ in order:
            P.op("dve", lambda e, r_=r_, s0=s0: e.scalar_tensor_tensor(out=s0, in0=s0, scalar=wr[:, r_:r_ + 1], in1=Tr[:, r_ * 64:(r_ + 1) * 64],
                                                                         op0=ALU.mult, op1=ALU.add), r=["S0_%d" % d, "wr", "Tr"], w=["S0_%d" % d])
    for n in range(NB):
        P.op("dve", lambda e, n=n: e.scalar_tensor_tensor(out=Sst[:, 0, n + 1, :], in0=Sst[:, 0, n, :], scalar=etf[:, n:n + 1], in1=U[:, 0, n, 0:64],
                                                           op0=ALU.mult, op1=ALU.add),
             r=["S0_0" if n == 0 else "Sf:%d" % n, "U:0:%d" % n, "et0:%d" % (n // 4), "AE"], w=["Sf:%d" % (n + 1)])
    for n in range(NB - 1, -1, -1):
        P.op("dve", lambda e, n=n: e.scalar_tensor_tensor(out=Sst[:, 1, n, :], in0=Sst[:, 1, n + 1, :], scalar=etb[:, n:n + 1], in1=U[:, 1, n, 0:64],
                                                           op0=ALU.mult, op1=ALU.add),
             r=["S0_1" if n == NB - 1 else "Sb:%d" % (n + 1), "U:1:%d" % n, "et1:%d" % (n // 4), "AE"], w=["Sb:%d" % n])
    barrier(C)
    A.off = amark
    w_g0, k_g0, b_g0 = W.get()
    w_g1, k_g1, b_g1 = W.get()
    qm = [A.bf16(512), A.bf16(512)]
    scs = [A.bf16(512), A.bf16(512)]
    sbd = [A.bf16(256), A.bf16(256)]
    grs = A.bf16(2 * TL).rearrange("p (c t) -> p c t", c=2)
    sqo = A.bf16(2 * TL).rearrange("p (c t) -> p c t", c=2)
    rso = A.f32(2 * TL).rearrange("p (c t) -> p c t", c=2)
    yt = A.f32(TL)
    hm = CF(HEADM, 4)
    for t in range(NTL):
        for c, (wg, kg) in enumerate(((w_g0, k_g0), (w_g1, k_g1))):
            for kc in range(KC):
                P.op("pe", lambda e, kc=kc, t=t, c=c, wg=wg: e.matmul(banks[6 + c][:, :], lhsT=wg[:, kc * 128:(kc + 1) * 128], rhs=xn_tile(kc, t),
                                                                      start=(kc == 0), stop=(kc == KC - 1)),
                     r=[kg, "xn:%d:%d" % (kc, t)], w=["b%d" % (6 + c)])
            P.op("act", lambda e, c=c: e.activation(out=grs[:, c, :], in_=banks[6 + c][:, :], func=AF.Silu), r=["b%d" % (6 + c), "AE"], w=["grs%d" % c])
        for j in range(4):
            n = 4 * t + j
            bs = slice(n * 128, (n + 1) * 128)
            for d, (qt, kt, tri) in enumerate(((qt_f, kt_f, TRIF), (qt_b, kt_b, TRIB))):
                P.op("dve", lambda e, d=d, qt=qt, bs=bs: e.tensor_tensor(out=qm[d].rearrange("p (h i) -> p h i", h=4),
                                                                         in0=qt[:, bs].unsqueeze(1).to_broadcast([128, 4, 128]),
                                                                         in1=hm.unsqueeze(2).to_broadcast([128, 4, 128]), op=ALU.mult),
                     r=["qt%d:%d" % (d, t), "cf", "AE"], w=["qm%d" % d])
                P.op("pe", lambda e, d=d, kt=kt, bs=bs: e.matmul(banks[d][:, :], lhsT=kt[:, bs], rhs=qm[d], start=True, stop=True),
                     r=["kt%d:%d" % (d, t), "qm%d" % d], w=["b%d" % d])
                P.op("dve", lambda e, d=d, tri=tri: e.tensor_tensor(out=scs[d].rearrange("p (h i) -> p h i", h=4),
                                                                    in0=banks[d][:, :].rearrange("p (h i) -> p h i", h=4),
                                                                    in1=cbf[:, tri:tri + 128].unsqueeze(1).to_broadcast([128, 4, 128]), op=ALU.mult),
                     r=["b%d" % d, "cbf", "AE"], w=["scs%d" % d])
                ssrc = Sst[:, 0, n, :] if d == 0 else Sst[:, 1, n + 1, :]
                skey = ("S0_0" if n == 0 else "Sf:%d" % n) if d == 0 else ("S0_1" if n == NB - 1 else "Sb:%d" % (n + 1))
                P.op("dve", lambda e, d=d, ssrc=ssrc: e.tensor_tensor(out=sbd[d].rearrange("p (h c) -> p h c", h=4),
                                                                      in0=ssrc.unsqueeze(1).to_broadcast([128, 4, 64]),
                                                                      in1=hm.unsqueeze(2).to_broadcast([128, 4, 64]), op=ALU.mult),
                     r=[skey, "cf", "AE"], w=["sbd%d" % d])
            for hp in range(2):
                ob = banks[2 + hp]
                okey = "b%d" % (2 + hp)
                oc = slice(j * 128, (j + 1) * 128)
                P.op("pe", lambda e, hp=hp, ob=ob, oc=oc, bs=bs: e.matmul(ob[:, oc], lhsT=sbd[0][:, hp * 128:(hp + 1) * 128], rhs=qt_f[:, bs], start=True, stop=False),
                     r=["sbd0", "qt0:%d" % t], w=[okey])
                P.op("pe", lambda e, hp=hp, ob=ob, oc=oc, bs=bs: e.matmul(ob[:, oc], lhsT=sbd[1][:, hp * 128:(hp + 1) * 128], rhs=qt_b[:, bs], start=False, stop=False),
                     r=["sbd1", "qt1:%d" % t], w=[okey])
                for hh in range(2):
                    h = 2 * hp + hh
                    for d in range(2):
                        last = (hh == 1 and d == 1)
                        P.op("pe", lambda e, ob=ob, oc=oc, hh=hh, h=h, d=d, n=n, last=last: e.matmul(
                            ob[hh * 64:(hh + 1) * 64, oc], lhsT=v_sb[:, n, h * 64:(h + 1) * 64], rhs=scs[d][:, h * 128:(h + 1) * 128], start=False, stop=last),
                             r=["scs%d" % d, "v:%d" % (n // 2)], w=[okey])
        for hp in range(2):
            ob = banks[2 + hp]
            okey = "b%d" % (2 + hp)
            P.op("act", lambda e, hp=hp, ob=ob: e.activation(out=sqo[:, hp, :], in_=ob[:, :], func=AF.Square), r=[okey, "AE"], w=["sqo%d" % hp])
            P.op("pe", lambda e, hp=hp: e.matmul(banks[4 + hp][:, :], lhsT=cbf[:, ONESBLK:ONESBLK + 128], rhs=sqo[:, hp, :], start=True, stop=True),
                 r=["sqo%d" % hp, "cbf"], w=["b%d" % (4 + hp)])
            P.op("act", lambda e, hp=hp: e.activation(out=rso[:, hp, :], in_=banks[4 + hp][:, :], func=AF.Sqrt, bias=EPS, scale=1.0 / 64),
                 r=["b%d" % (4 + hp), "AE"], w=["rso%d" % hp])
            P.op("dve", lambda e, hp=hp: e.reciprocal(out=rso[:, hp, :], in_=rso[:, hp, :]), r=["rso%d" % hp], w=["rso%d" % hp])
            P.op("dve", lambda e, hp=hp, ob=ob: e.tensor_tensor(out=yt, in0=ob[:, :], in1=rso[:, hp, :], op=ALU.mult), r=[okey, "rso%d" % hp, "AE"], w=["yt"])
            P.op("dve", lambda e, hp=hp, t=t: e.scalar_tensor_tensor(out=ys[:, 2 + hp, t * TL:(t + 1) * TL], in0=yt, scalar=spt[:, GLAG + hp:GLAG + hp + 1],
                                                                      in1=grs[:, hp, :], op0=ALU.mult, op1=ALU.mult),
                 r=["yt", "grs%d" % hp, "spt"], w=["ys:%d:%d" % (2 + hp, t)])
    W.done(b_g0)
    W.done(b_g1)
    barrier(C)
    A.reset()
    C.A, C.xT, C.xn, C.xnh, C.ys, C.amb, C.PCF, C.CF, C.ext_ranges = A, xT, xn, xnh, ys, amb, PCF, CF, ext_ranges
    C.rope_d, C.xout, C.yout, C.norm_all, C.arn_fn = rope_d, xout, yout, norm_all, None
    C.debug = debug
    if debug:
        C.dbg = dbg
    return nc, st, P, C


def proj_ext(C, wt, wkey, rng, bank, bkey, M=128):
    rhs_fn, eoff, ntok, keyf = rng
    for kc in range(KC):
        C.P.op("pe", lambda e, kc=kc: e.matmul(bank[0:M, 0:ntok], lhsT=wt[:, kc * 128:kc * 128 + M], rhs=rhs_fn(kc),
                                                start=(kc == 0), stop=(kc == KC - 1)),
               r=[wkey, keyf(kc)], w=[bkey])


def attention_phase(C):
    P, W, banks, A, ys = C.P, C.W, C.banks, C.A, C.ys
    cbf, spt = C.cbf, C.spt
    q_att = ys[:, 0:2, :]
    k_att = A.bf16(NE)
    v_att = A.bf16(18 * 130).rearrange("p (n c) -> p n c", n=18)
    nsink = A.f32(4)
    amark = A.off
    rC = A.f32(NE)
    rS = A.f32(NE)
    P.op("sp", lambda e: e.dma_start(out=rC, in_=C.rope_d[0]), r=["AE"], w=["rC"], dma="rC")
    P.op("sp", lambda e: e.dma_start(out=rS, in_=C.rope_d[1]), r=["AE"], w=["rS"], dma="rS")
    raw = [A.bf16(TL), A.bf16(TL)]
    r1 = [A.f32(TL), A.f32(TL)]
    r2 = [A.f32(TL), A.f32(TL)]
    P.op("dve", lambda e: e.tensor_scalar(out=nsink, in0=spt[:, SINK:SINK + 4], scalar1=-1.0, scalar2=None, op0=ALU.mult), r=["spt", "AE"], w=["nsink"])
    P.op("pool", lambda e: e.memset(v_att.rearrange("p n c -> p (n c)"), 1.0), r=["AE"], w=["vones"])
    cnt = [0]

    def rope_proj(wt, wkey, rng, dst, dkey, scale):
        rhs_fn, eoff, ntok, keyf = rng
        i = cnt[0] % 2
        cnt[0] += 1
        bk, bkey = banks[i], "b%d" % i
        bp, bpkey = banks[2 + i], "b%d" % (2 + i)
        proj_ext(C, wt, wkey, rng, bk, bkey)
        P.op("act", lambda e: e.activation(out=raw[i][:, 0:ntok], in_=bk[:, 0:ntok], func=AF.Copy, scale=scale), r=[bkey, "AE"], w=["raw%d" % i])
        P.op("pe", lambda e: e.matmul(bp[:, 0:ntok], lhsT=cbf[:, PMAT:PMAT + 128], rhs=raw[i][:, 0:ntok], start=True, stop=True),
             r=["raw%d" % i, "cbf"], w=[bpkey])
        P.op("dve", lambda e: e.tensor_tensor(out=r1[i][:, 0:ntok], in0=raw[i][:, 0:ntok], in1=rC[:, eoff:eoff + ntok], op=ALU.mult),
             r=["raw%d" % i, "rC", "AE"], w=["r1%d" % i])
        P.op("dve", lambda e: e.tensor_tensor(out=r2[i][:, 0:ntok], in0=bp[:, 0:ntok], in1=rS[:, eoff:eoff + ntok], op=ALU.mult),
             r=[bpkey, "rS", "AE"], w=["r2%d" % i])
        P.op("pool", lambda e: e.tensor_tensor(out=dst, in0=r1[i][:, 0:ntok], in1=r2[i][:, 0:ntok], op=ALU.add),
             r=["r1%d" % i, "r2%d" % i, "AE"], w=[dkey])

    for gq in range(2):
        wt, wkey, wb_ = W.get()
        for t in range(NTL):
            rope_proj(wt, wkey, C.ext_ranges[t], q_att[:, gq, t * TL:(t + 1) * TL], "qa:%d:%d" % (gq, t), 0.125)
        W.done(wb_)
    wt, wkey, wb_ = W.get()
    for ri, rng in enumerate(C.ext_ranges):
        rope_proj(wt, wkey, rng, k_att[:, rng[1]:rng[1] + rng[2]], "ka:%d" % ri, 1.0)
    W.done(wb_)
    wt, wkey, wb_ = W.get()
    for eb in range(18):
        if eb == 0:
            lf, kf = (lambda kc: C.xnh[:, kc, 0:HB]), (lambda kc: "xnh:%d" % kc)
        elif eb == 17:
            lf, kf = (lambda kc: C.xnh[:, kc, HB:2 * HB]), (lambda kc: "xnh:%d" % kc)
        else:
            lf, kf = (lambda kc, eb=eb: C.xn[:, kc, (eb - 1) * 128:eb * 128]), (lambda kc, eb=eb: "xn:%d:%d" % (kc, (eb - 1) // 4))
        bk, bkey = banks[4 + eb % 2], "b%d" % (4 + eb % 2)
        for kc in range(KC):
            P.op("pe", lambda e, kc=kc, lf=lf, bk=bk: e.matmul(bk[:, 0:128], lhsT=lf(kc), rhs=wt[:, kc * 128:(kc + 1) * 128],
                                                                start=(kc == 0), stop=(kc == KC - 1)),
                 r=[wkey, kf(kc)], w=[bkey])
        P.op("act", lambda e, eb=eb, bk=bk: e.activation(out=v_att[:, eb, :].rearrange("p (g c) -> p g c", g=2)[:, :, 0:64],
                                                          in_=bk[:, 0:128].rearrange("p (g c) -> p g c", g=2), func=AF.Copy),
             r=[bkey, "vones", "AE"], w=["va:%d" % eb])
    W.done(wb_)
    barrier(C)
    A.off = amark
    mx = A.f32(4)
    negm = A.f32(4)
    es = A.f32(4)
    den = A.f32(4)
    Pb = [A.bf16(768), A.bf16(768)]
    Pm = [A.bf16(768), A.bf16(768)]
    PTs = [A.bf16(768), A.bf16(768)]
    on = A.bf16(256)
    for n in range(NB):
        v = 0 if n == 0 else (2 if n == NB - 1 else 1)
        msk = C.amb[:, v * 384:(v + 1) * 384]
        qs = slice(n * 128, (n + 1) * 128)
        win = slice(n * 128, n * 128 + 384)
        wkeys = ["ka:%d" % ri for ri in range(6)]
        for k in range(2):
            pr = slice(64 * k, 64 * k + 64)
            for g_ in range(2):
                bk, bkey = banks[2 * k + g_], "b%d" % (2 * k + g_)
                P.op("pe", lambda e, bk=bk, g_=g_, pr=pr, qs=qs, win=win: e.matmul(bk[:, 0:384], lhsT=q_att[pr, g_, qs], rhs=k_att[pr, win], start=True, stop=True),
                     r=["qa:%d:%d" % (g_, n // 4)] + wkeys, w=[bkey])
                h = 2 * k + g_
                P.op("dve", lambda e, bk=bk, h=h: e.reduce_max(out=mx[:, h:h + 1], in_=bk[:, 0:384], axis=AX.X), r=[bkey, "AE"], w=["mx%d" % h])
            hs = slice(2 * k, 2 * k + 2)
            P.op("dve", lambda e, hs=hs: e.scalar_tensor_tensor(out=negm[:, hs], in0=mx[:, hs], scalar=-1.0, in1=nsink[:, hs], op0=ALU.mult, op1=ALU.min),
                 r=["mx%d" % (2 * k), "mx%d" % (2 * k + 1), "nsink"], w=["negm%d" % k])
            for g_ in range(2):
                h = 2 * k + g_
                bk, bkey = banks[2 * k + g_], "b%d" % (2 * k + g_)
                P.op("act", lambda e, bk=bk, g_=g_, h=h, k=k: e.activation(out=Pb[k][:, g_ * 384:(g_ + 1) * 384], in_=bk[:, 0:384], func=AF.Exp, bias=negm[:, h:h + 1]),
                     r=[bkey, "negm%d" % k, "AE"], w=["Pb%d:%d" % (k, g_)])
            P.op("pool", lambda e, k=k, msk=msk: e.tensor_tensor(out=Pm[k].rearrange("p (g s) -> p g s", g=2), in0=Pb[k].rearrange("p (g s) -> p g s", g=2),
                                                                 in1=msk.unsqueeze(1).to_broadcast([128, 2, 384]), op=ALU.mult),
                 r=["Pb%d:0" % k, "Pb%d:1" % k, "amb", "AE"], w=["Pm%d" % k])
            bt = bfview(banks[4 + k])
            btkey = "b%d" % (4 + k)
            for j in range(6):
                P.op("pe", lambda e, j=j, k=k, bt=bt: e.transpose(out=bt[:, j * 128:(j + 1) * 128], in_=Pm[k][:, j * 128:(j + 1) * 128], identity=cbf[:, IDENT:IDENT + 128]),
                     r=["Pm%d" % k, "cbf"], w=[btkey])
            P.op("act", lambda e, k=k, bt=bt: e.activation(out=PTs[k], in_=bt[:, 0:768], func=AF.Copy), r=[btkey, "AE"], w=["PTs%d" % k])
            for g_ in range(2):
                h = 2 * k + g_
                for w_ in range(3):
                    P.op("pe", lambda e, g_=g_, h=h, w_=w_, k=k, n=n: e.matmul(banks[6][:, h * 65:(h + 1) * 65], lhsT=PTs[k][:, (g_ * 3 + w_) * 128:(g_ * 3 + w_ + 1) * 128],
                                                                               rhs=v_att[:, n + w_, k * 65:(k + 1) * 65], start=(w_ == 0), stop=(w_ == 2)),
                         r=["PTs%d" % k] + ["va:%d" % (n + w_)], w=["b6"])
        P.op("dve", lambda e: e.tensor_tensor(out=es, in0=negm, in1=spt[:, SINK:SINK + 4], op=ALU.add), r=["negm0", "negm1", "spt"], w=["es"])
        P.op("act", lambda e: e.activation(out=es, in_=es, func=AF.Exp), r=["es"], w=["es"])
        b6v = banks[6][:, 0:260].rearrange("p (h c) -> p h c", h=4)
        P.op("dve", lambda e, b6v=b6v: e.tensor_tensor(out=den, in0=b6v[:, :, 64], in1=es, op=ALU.add), r=["b6", "es"], w=["den"])
        P.op("dve", lambda e: e.reciprocal(out=den, in_=den), r=["den"], w=["den"])
        P.op("dve", lambda e, b6v=b6v: e.tensor_tensor(out=on.rearrange("p (h c) -> p h c", h=4), in0=b6v[:, :, 0:64],
                                                       in1=den.unsqueeze(2).to_broadcast([128, 4, 64]), op=ALU.mult), r=["b6", "den", "AE"], w=["on"])
        b7 = bfview(banks[7])
        for c in range(2):
            P.op("pe", lambda e, c=c: e.transpose(out=b7[:, c * 128:(c + 1) * 128], in_=on[:, c * 128:(c + 1) * 128], identity=cbf[:, IDENT:IDENT + 128]),
                 r=["on", "cbf"], w=["b7"])
        P.op("act", lambda e, qs=qs: e.activation(out=ys[:, 4:6, qs], in_=b7[:, 0:256].rearrange("p (c t) -> p c t", c=2), func=AF.Copy),
             r=["b7", "AE"], w=["ys:4:%d" % (n // 4), "ys:5:%d" % (n // 4)])
    barrier(C)
    A.reset()


def conv_phase(C):
    P, W, banks, A, ys = C.P, C.W, C.banks, C.A, C.ys
    cbf, spt = C.cbf, C.spt
    u_ext = A.bf16(2 * NE).rearrange("p (c t) -> p c t", c=2)
    Dm = A.bf16(2 * 31 * 128).rearrange("p (c k j) -> p c k j", c=2, k=31)
    sg = [A.f32(TL), A.f32(TL)]
    for c in range(2):
        P.op("pool", lambda e, c=c: e.tensor_tensor(out=Dm[:, c, :, :], in0=cbf[:, IDENT:IDENT + 128].unsqueeze(1).to_broadcast([128, 31, 128]),
                                                     in1=spt[:, CONVW + 31 * c:CONVW + 31 * (c + 1)].unsqueeze(2).to_broadcast([128, 31, 128]), op=ALU.mult),
             r=["cbf", "spt", "AE"], w=["Dm%d" % c])
    i = 0
    for c in range(2):
        wa, ka, ba = W.get()
        wg, kg, bg = W.get()
        for ri, rng in enumerate(C.ext_ranges):
            eoff, ntok = rng[1], rng[2]
            j = i % 2
            i += 1
            proj_ext(C, wa, ka, rng, banks[j], "b%d" % j)
            proj_ext(C, wg, kg, rng, banks[2 + j], "b%d" % (2 + j))
            P.op("act", lambda e, j=j, ntok=ntok: e.activation(out=sg[j][:, 0:ntok], in_=banks[2 + j][:, 0:ntok], func=AF.Sigmoid), r=["b%d" % (2 + j), "AE"], w=["sg%d" % j])
            P.op("dve", lambda e, j=j, c=c, eoff=eoff, ntok=ntok: e.tensor_tensor(out=u_ext[:, c, eoff:eoff + ntok], in0=banks[j][:, 0:ntok], in1=sg[j][:, 0:ntok], op=ALU.mult),
                 r=["b%d" % j, "sg%d" % j, "AE"], w=["u:%d:%d" % (c, ri)])
        W.done(ba)
        W.done(bg)
    ysb = A.f32(2 * TL).rearrange("p (c t) -> p c t", c=2)
    ybf = A.bf16(2 * TL).rearrange("p (c t) -> p c t", c=2)
    ysq = A.bf16(2 * TL).rearrange("p (c t) -> p c t", c=2)
    mean = A.f32(TL)
    var = A.f32(TL)
    dd = A.f32(TL)
    msq = dd
    for t in range(NTL):
        for c in range(2):
            bk, bkey = banks[4 + c], "b%d" % (4 + c)
            for k in range(31):
                s0 = HB + t * TL + k - 15
                P.op("pe", lambda e, c=c, k=k, s0=s0, bk=bk: e.matmul(bk[:, :], lhsT=Dm[:, c, k, :], rhs=u_ext[:, c, s0:s0 + TL], start=(k == 0), stop=(k == 30)),
                     r=["Dm%d" % c] + ["u:%d:%d" % (c, ri) for ri in range(6)], w=[bkey])
            P.op("act", lambda e, c=c, bk=bk: e.activation(out=ysb[:, c, :], in_=bk[:, :], func=AF.Identity, bias=spt[:, CONVB + c:CONVB + c + 1]),
                 r=[bkey, "spt", "AE"], w=["ysb%d" % c])
            P.op("act", lambda e, c=c, bk=bk: e.activation(out=ysq[:, c, :], in_=bk[:, :], func=AF.Square, bias=spt[:, CONVB + c:CONVB + c + 1]),
                 r=[bkey, "spt", "AE"], w=["ysq%d" % c])
            P.op("pool", lambda e, c=c: e.tensor_copy(out=ybf[:, c, :], in_=ysb[:, c, :]), r=["ysb%d" % c, "AE"], w=["ybf%d" % c])
        for c in range(2):
            P.op("pe", lambda e, c=c: e.matmul(banks[6][:, :], lhsT=cbf[:, ONES:ONES + 128], rhs=ybf[:, c, :], start=(c == 0), stop=(c == 1)), r=["ybf%d" % c, "cbf"], w=["b6"])
        for c in range(2):
            P.op("pe", lambda e, c=c: e.matmul(banks[7][:, :], lhsT=cbf[:, ONES:ONES + 128], rhs=ysq[:, c, :], start=(c == 0), stop=(c == 1)), r=["ysq%d" % c, "cbf"], w=["b7"])
        P.op("dve", lambda e: e.tensor_scalar(out=mean, in0=banks[6][:, :], scalar1=1.0 / 256, scalar2=None, op0=ALU.mult), r=["b6", "AE"], w=["mean"])
        P.op("dve", lambda e: e.tensor_tensor(out=msq, in0=mean, in1=mean, op=ALU.mult), r=["mean", "AE"], w=["dd"])
        P.op("dve", lambda e: e.scalar_tensor_tensor(out=var, in0=banks[7][:, :], scalar=1.0 / 256, in1=msq, op0=ALU.mult, op1=ALU.subtract), r=["b7", "dd", "AE"], w=["var"])
        P.op("act", lambda e: e.activation(out=var, in_=var, func=AF.Sqrt, bias=EPS), r=["var"], w=["var"])
        P.op("dve", lambda e: e.reciprocal(out=var, in_=var), r=["var"], w=["var"])
        for c in range(2):
            P.op("dve", lambda e, c=c: e.tensor_tensor(out=dd, in0=ysb[:, c, :], in1=mean, op=ALU.subtract), r=["ysb%d" % c, "mean", "AE"], w=["dd"])
            P.op("dve", lambda e: e.tensor_tensor(out=dd, in0=dd, in1=var, op=ALU.mult), r=["dd", "var"], w=["dd"])
            P.op("act", lambda e, c=c, t=t: e.activation(out=ys[:, c, t * TL:(t + 1) * TL], in_=dd, func=AF.Silu, scale=spt[:, LNG + c:LNG + c + 1], bias=spt[:, LNB + c:LNB + c + 1]),
                 r=["dd", "spt", "AE"], w=["ys:%d:%d" % (c, t)])
    barrier(C)
    A.reset()


def pool_phase(C):
    P, W, banks, A, ys = C.P, C.W, C.banks, C.A, C.ys
    spt = C.spt
    pin = A.f32(2 * NE).rearrange("p (c t) -> p c t", c=2)
    B1 = A.f32(NE)
    B2 = A.f32(NE)
    dbf = [A.bf16(TL), A.bf16(TL)]
    pwb = A.bf16(256)
    P.op("dve", lambda e: e.tensor_copy(out=pwb, in_=spt[:, POOLW:POOLW + 256]), r=["spt", "AE"], w=["pwb"])
    i = 0
    for c in range(2):
        wt, wkey, wb_ = W.get()
        for ri, rng in enumerate(C.ext_ranges):
            eoff, ntok = rng[1], rng[2]
            j = i % 2
            i += 1
            proj_ext(C, wt, wkey, rng, banks[j], "b%d" % j)
            P.op("act", lambda e, j=j, c=c, eoff=eoff, ntok=ntok: e.activation(out=pin[:, c, eoff:eoff + ntok], in_=banks[j][:, 0:ntok], func=AF.Copy),
                 r=["b%d" % j, "AE"], w=["pin:%d:%d" % (c, ri)])
        W.done(wb_)
    Wd = NE
    invw = C.CF(INVW, 2)
    corr = C.PCF(PCORR, 32).rearrange("p (c j) -> p c j", c=2)
    for c in range(2):
        u = pin[:, c, :]
        pk = ["pin:%d:%d" % (c, ri) for ri in range(6)]
        P.op("pool", lambda e, u=u: e.tensor_tensor(out=B1[:, 1:Wd], in0=u[:, 0:Wd - 1], in1=u[:, 1:Wd], op=ALU.add), r=pk + ["AE"], w=["B1"])
        P.op("pool", lambda e: e.tensor_tensor(out=B2[:, 2:Wd - 1], in0=B1[:, 1:Wd - 2], in1=B1[:, 3:Wd], op=ALU.add), r=["B1", "AE"], w=["B2"])
        if c == 1:
            P.op("pool", lambda e: e.tensor_tensor(out=B1[:, 4:Wd - 3], in0=B2[:, 2:Wd - 5], in1=B2[:, 6:Wd - 1], op=ALU.add), r=["B2"], w=["B1"])
            P.op("pool", lambda e: e.tensor_tensor(out=B2[:, 8:Wd - 7], in0=B1[:, 4:Wd - 11], in1=B1[:, 12:Wd - 3], op=ALU.add), r=["B1"], w=["B2"])
        for (Bx, bkey, pr) in ((B1, "B1", slice(0, 64)), (B2, "B2", slice(64, 128))):
            P.op("dve", lambda e, Bx=Bx, pr=pr, c=c: e.tensor_tensor(out=Bx[pr, HB:HB + 8], in0=Bx[pr, HB:HB + 8], in1=corr[pr, c, 0:8], op=ALU.mult), r=[bkey, "pcf"], w=[bkey])
            P.op("dve", lambda e, Bx=Bx, pr=pr, c=c: e.tensor_tensor(out=Bx[pr, HB + NT - 8:HB + NT], in0=Bx[pr, HB + NT - 8:HB + NT], in1=corr[pr, c, 8:16], op=ALU.mult), r=[bkey, "pcf"], w=[bkey])
        for t in range(NTL):
            j = t % 2
            es = slice(HB + t * TL, HB + (t + 1) * TL)
            for (Bx, bkey, pr) in ((B1, "B1", slice(0, 64)), (B2, "B2", slice(64, 128))):
                P.op("dve", lambda e, Bx=Bx, pr=pr, c=c, j=j, es=es: e.scalar_tensor_tensor(out=dbf[j][pr, :], in0=Bx[pr, es], scalar=invw[pr, c:c + 1], in1=pin[pr, c, es],
                                                                                             op0=ALU.mult, op1=ALU.subtract),
                     r=[bkey, "cf"] + pk + ["AE"], w=["dbf%d:%d" % (j, pr.start)])
            bk, bkey2 = banks[2 + j], "b%d" % (2 + j)
            P.op("pe", lambda e, c=c, j=j, bk=bk: e.matmul(bk[:, :], lhsT=pwb[:, c * 128:(c + 1) * 128], rhs=dbf[j], start=True, stop=True),
                 r=["dbf%d:0" % j, "dbf%d:64" % j, "pwb"], w=[bkey2])
            P.op("act", lambda e, c=c, t=t, bk=bk: e.activation(out=ys[:, 6 + c, t * TL:(t + 1) * TL], in_=bk[:, :], func=AF.Identity, scale=spt[:, PSC + c:PSC + c + 1]),
                 r=[bkey2, "spt", "AE"], w=["ys:%d:%d" % (6 + c, t)])
    barrier(C)
    A.reset()


def merge_phase(C):
    P, W, banks, A, ys, xn, xT = C.P, C.W, C.banks, C.A, C.ys, C.xn, C.xT
    merged = A.bf16(KC * NT).rearrange("p (k t) -> p k t", k=KC)
    acc = A.f32(NTL * TL).rearrange("p (a t) -> p a t", a=NTL)
    tmp = [A.f32(TL), A.f32(TL)]
    sgv = C.xnh[:, :, :].rearrange("p k t -> p (k t)").bitcast(F32)
    sg = [sgv[:, 0:TL], sgv[:, TL:2 * TL]]
    i = 0
    for f in range(KC):
        wb, kb, bb = W.get()
        wb4 = wb[:, :].rearrange("p (n k j) -> p n k j", n=4, k=2)
        for n in range(4):
            wg, kg, bg = W.get()
            for t in range(NTL):
                j = i % 2
                i += 1
                G, gk = banks[j], "b%d" % j
                Pj, pk = banks[2 + j], "b%d" % (2 + j)
                for kc in range(KC):
                    P.op("pe", lambda e, kc=kc, t=t, wg=wg, G=G: e.matmul(G[:, :], lhsT=wg[:, kc * 128:(kc + 1) * 128], rhs=xn[:, kc, t * TL:(t + 1) * TL],
                                                                          start=(kc == 0), stop=(kc == KC - 1)), r=[kg, "xn:%d:%d" % (kc, t)], w=[gk])
                for k2 in range(2):
                    P.op("pe", lambda e, k2=k2, t=t, n=n, Pj=Pj, wb4=wb4: e.matmul(Pj[:, :], lhsT=wb4[:, n, k2, :], rhs=ys[:, 2 * n + k2, t * TL:(t + 1) * TL],
                                                                                   start=(k2 == 0), stop=(k2 == 1)), r=[kb, "ys:%d:%d" % (2 * n + k2, t)], w=[pk])
                P.op("act", lambda e, j=j, G=G: e.activation(out=sg[j], in_=G[:, :], func=AF.Sigmoid), r=[gk, "xnhfree"], w=["sgm%d" % j])
                if n == 0:
                    P.op("dve", lambda e, j=j, t=t, Pj=Pj: e.tensor_tensor(out=acc[:, t, :], in0=Pj[:, :], in1=sg[j], op=ALU.mult), r=[pk, "sgm%d" % j, "AE"], w=["acc%d" % t])
                else:
                    P.op("dve", lambda e, j=j, Pj=Pj: e.tensor_tensor(out=tmp[j], in0=Pj[:, :], in1=sg[j], op=ALU.mult), r=[pk, "sgm%d" % j, "AE"], w=["tmp%d" % j])
                    if n < 3:
                        P.op("pool", lambda e, j=j, t=t: e.tensor_tensor(out=acc[:, t, :], in0=acc[:, t, :], in1=tmp[j], op=ALU.add), r=["acc%d" % t, "tmp%d" % j], w=["acc%d" % t])
                    else:
                        P.op("pool", lambda e, j=j, t=t, f=f: e.tensor_tensor(out=merged[:, f, t * TL:(t + 1) * TL], in0=acc[:, t, :], in1=tmp[j], op=ALU.add),
                             r=["acc%d" % t, "tmp%d" % j], w=["mg:%d:%d" % (f, t)])
            W.done(bg)
        W.done(bb)
    i = 0
    for f in range(KC):
        wo, ko, bo = W.get()
        for t in range(NTL):
            j = i % 4
            i += 1
            bk, bkey = banks[4 + j], "b%d" % (4 + j)
            for kc in range(KC):
                P.op("pe", lambda e, kc=kc, t=t, bk=bk, wo=wo: e.matmul(bk[:, :], lhsT=wo[:, kc * 128:(kc + 1) * 128], rhs=merged[:, kc, t * TL:(t + 1) * TL],
                                                                        start=(kc == 0), stop=(kc == KC - 1)), r=[ko, "mg:%d:%d" % (kc, t)], w=[bkey])
            P.op("dve", lambda e, f=f, t=t, bk=bk: e.tensor_tensor(out=xT[:, f, t * TL:(t + 1) * TL], in0=bk[:, :], in1=xT[:, f, t * TL:(t + 1) * TL], op=ALU.add),
                 r=[bkey, "x:%d:%d" % (f, t)], w=["x:%d:%d" % (f, t)])
        W.done(bo)
    barrier(C)
    A.reset()


def ffn_phase(C):
    P, W, banks, A, ys, xn, xT = C.P, C.W, C.banks, C.A, C.ys, C.xn, C.xT
    arn = dict(sq=A.bf16(KC * TL), rs=A.f32(TL))
    for t in range(NTL):
        rms_tile(C, lambda kc, t=t: xT[:, kc, t * TL:(t + 1) * TL], FFNG, lambda kc, t=t: xn[:, kc, t * TL:(t + 1) * TL], TL,
                 ["x:%d:%d" % (kc, t) for kc in range(KC)], ["xn:%d:%d" % (kc, t) for kc in range(KC)], arn, banks[3], "b3", "f%d" % t)
    rl = [A.f32(TL), A.f32(TL)]
    act = ys
    i = 0
    for g in range(4):
        for c in range(KC):
            wu, ku, bu = W.get()
            for t in range(NTL):
                j = i % 2
                i += 1
                bk, bkey = banks[j], "b%d" % j
                for kc in range(KC):
                    P.op("pe", lambda e, kc=kc, t=t, bk=bk, wu=wu: e.matmul(bk[:, :], lhsT=wu[:, kc * 128:(kc + 1) * 128], rhs=xn[:, kc, t * TL:(t + 1) * TL],
                                                                            start=(kc == 0), stop=(kc == KC - 1)), r=[ku, "xn:%d:%d" % (kc, t)], w=[bkey])
                P.op("act", lambda e, j=j, bk=bk: e.activation(out=rl[j], in_=bk[:, :], func=AF.Relu), r=[bkey, "AE"], w=["rl%d" % j])
                P.op("dve", lambda e, j=j, c=c, t=t: e.tensor_tensor(out=act[:, c, t * TL:(t + 1) * TL], in0=rl[j], in1=rl[j], op=ALU.mult),
                     r=["rl%d" % j, "AE"], w=["ys:%d:%d" % (c, t)])
            W.done(bu)
        for f in range(KC):
            wd, kd, bd = W.get()
            for t in range(NTL):
                j = i % 4
                i += 1
                bk, bkey = banks[4 + j], "b%d" % (4 + j)
                for kc in range(KC):
                    P.op("pe", lambda e, kc=kc, t=t, bk=bk, wd=wd: e.matmul(bk[:, :], lhsT=wd[:, kc * 128:(kc + 1) * 128], rhs=act[:, kc, t * TL:(t + 1) * TL],
                                                                            start=(kc == 0), stop=(kc == KC - 1)), r=[kd, "ys:%d:%d" % (kc, t)], w=[bkey])
                P.op("dve", lambda e, f=f, t=t, bk=bk: e.tensor_tensor(out=xT[:, f, t * TL:(t + 1) * TL], in0=bk[:, :], in1=xT[:, f, t * TL:(t + 1) * TL], op=ALU.add),
                     r=[bkey, "x:%d:%d" % (f, t)], w=["x:%d:%d" % (f, t)])
            W.done(bd)
    barrier(C)
    A.reset()


def output_phase(C):
    P, banks, A, xT, spt = C.P, C.banks, C.A, C.xT, C.spt
    fins = []
    for kc in range(KC):
        fins.append(P.op("sp", lambda e, kc=kc: e.dma_start(out=C.xout[kc * 128:(kc + 1) * 128, :], in_=xT[:, kc, :]),
                         r=["x:%d:%d" % (kc, t) for t in range(NTL)], dma="xo%d" % kc))
    sq = A.bf16(KC * TL).rearrange("p (k t) -> p k t", k=KC)
    rs = A.f32(TL)
    yo = [A.f32(TL), A.f32(TL), A.f32(TL), A.f32(TL)]
    i = 0
    for t in range(NTL):
        ts = slice(t * TL, (t + 1) * TL)
        for kc in range(KC):
            P.op("act", lambda e, kc=kc, ts=ts: e.activation(out=sq[:, kc, :], in_=xT[:, kc, ts], func=AF.Square), r=["x:%d:%d" % (kc, t), "AE"], w=["sq%d" % kc])
        for kc in range(KC):
            P.op("pe", lambda e, kc=kc: e.matmul(banks[3][:, :], lhsT=C.cbf[:, ONES:ONES + 128], rhs=sq[:, kc, :], start=(kc == 0), stop=(kc == KC - 1)),
                 r=["sq%d" % kc, "cbf"], w=["b3"])
        P.op("act", lambda e: e.activation(out=rs, in_=banks[3][:, :], func=AF.Sqrt, bias=EPS, scale=1.0 / D), r=["b3", "AE"], w=["rs"])
        P.op("dve", lambda e: e.reciprocal(out=rs, in_=rs), r=["rs"], w=["rs"])
        for kc in range(KC):
            j = i % 4
            i += 1
            P.op("dve", lambda e, kc=kc, ts=ts, j=j: e.scalar_tensor_tensor(out=yo[j], in0=xT[:, kc, ts], scalar=spt[:, FING + kc:FING + kc + 1], in1=rs, op0=ALU.mult, op1=ALU.mult),
                 r=["x:%d:%d" % (kc, t), "rs", "spt", "AE"], w=["yo%d" % j])
            fins.append(P.op("sp", lambda e, kc=kc, ts=ts, j=j: e.dma_start(out=C.yout[kc * 128:(kc + 1) * 128, ts], in_=yo[j]), r=["yo%d" % j], dma="yo%d" % j))
    return fins


def build_layer_full(debug=False, upto=99):
    nc, st, P, C = build_layer(debug)
    if upto >= 2:
        attention_phase(C)
    if upto >= 3:
        conv_phase(C)
    if upto >= 4:
        pool_phase(C)
    fins = []
    if debug:
        fins.append(P.op("sp", lambda e: e.dma_start(out=C.dbg[:, :], in_=C.ys[:, :, :].rearrange("p k t -> p (k t)")),
                         r=["ys:%d:%d" % (k, t) for k in range(KC) for t in range(NTL)] + ["AE"], dma="dbg"))
        barrier(C)
    if upto >= 5:
        P.op("pool", lambda e: e.memset(C.scr[:, 0:8], 0.0), w=["xnh:%d" % kc for kc in range(KC)] + ["xnhfree"])
        merge_phase(C)
    if upto >= 6:
        ffn_phase(C)
    fins += output_phase(C)
    P.emit(final_waits=fins)
    return nc, st


_CACHE = {}


def _prog(name):
    if name not in _CACHE:
        if name == "sum":
            _CACHE[name] = build_sum()
        else:
            _CACHE[name] = build_layer_full(debug=False)
    return _CACHE[name][0]


def _ext_from_segments(segs, c):
    p = c % 4
    left = segs[c - 1][:, NT - HB:] if p > 0 else np.zeros((D, HB), np.float32)
    right = segs[c + 1][:, :HB] if p < 3 else np.zeros((D, HB), np.float32)
    return np.ascontiguousarray(np.concatenate([left, segs[c], right], axis=1))


def kernel(**inputs):
    inp = {k: np.asarray(v, dtype=np.float32) for k, v in inputs.items()}
    x = inp["x"]
    B, T, _ = x.shape
    cores = list(range(8))
    cst = const_block()
    pcs = [percore_consts(c) for c in cores]
    ropes = [rope_tables(c) for c in cores]
    segs = [np.ascontiguousarray(x[c // 4, (c % 4) * NT:(c % 4 + 1) * NT, :].T) for c in cores]
    y = None
    for l in range(2):
        sp = small_params(inp, l)
        ws_s = sum_stream(inp, l)
        res = run_bass_kernel_spmd(_prog("sum"), [{"xT": segs[c], "ws": ws_s, "sp": sp, "cst": cst} for c in cores], core_ids=cores)
        sums = [np.asarray(res.results[c]["so"]) for c in cores]
        ws_l = layer_stream(inp, l)
        in_maps = []
        for c in cores:
            b = c // 4
            gs = np.ascontiguousarray(np.stack([sums[b * 4 + r] for r in range(4)], 1).reshape(128, 4 * 130))
            in_maps.append({"xT": _ext_from_segments(segs, c), "ws": ws_l, "sp": sp, "cst": cst, "pc": pcs[c], "rope": ropes[c], "gsum": gs})
        res = run_bass_kernel_spmd(_prog("layer"), in_maps, core_ids=cores)
        segs = [np.asarray(res.results[c]["xo"]) for c in cores]
        y = [np.asarray(res.results[c]["yo"]) for c in cores]
    out = np.empty((B, T, D), np.float32)
    for c in cores:
        out[c // 4, (c % 4) * NT:(c % 4 + 1) * NT, :] = y[c].T
    return out
```

```python
import math
import numpy as np
import concourse.bass as bass
import concourse.mybir as mybir
from concourse.bass_utils import run_bass_kernel_spmd
from contextlib import ExitStack

F32 = mybir.dt.float32
BF16 = mybir.dt.bfloat16
ALU = mybir.AluOpType
AF = mybir.ActivationFunctionType
AX = mybir.AxisListType

D = 1024
KC = 8
NT = 2048
TL = 512
NTL = 4
HB = 128
NE = NT + 2 * HB
NB = 16
SEQ = 8192
EPS = 1e-6
LNC = -0.5 * math.log(32.0)

MIXG, FFNG, FING, CONVB, LNG, LNB, GLAG, SINK, PSC, CONVW, WUP, POOLW, NSP = 0, 8, 16, 24, 26, 28, 30, 32, 36, 38, 100, 356, 612
IDENT, ONES, ONESBLK, PMAT, TRIF, TRIB, NCBF = 0, 128, 256, 384, 512, 640, 768
BDM, HEADM, INVW, SCANM, NCST = 768, 1024, 1028, 1032, 1544
AMASK, PCORR, COEFF, COEFB, NPC = 0, 1152, 1184, 1196, 1208

IN_OFF = dict(glu_a=0, glu_g=256, gq=512, gk=640, gv=768, gr=1024, glr=1280, aq=1312, ak=1568, av=1696, pin=1824, gate=2080)


class Prog:
    ENGS = ("pe", "act", "dve", "pool", "sp")

    def __init__(self, nc, stack):
        self.nc = nc
        self.stack = stack
        self.ops = []
        self.last_w = {}
        self.readers = {}
        self.sems = {}
        self.semcnt = {}
        self.nsb = 0

    def sem(self, key):
        if key not in self.sems:
            self.sems[key] = self.stack.enter_context(self.nc.semaphore("s%d" % len(self.sems)))
            self.semcnt[key] = 0
        return self.sems[key]

    def sb(self, shape, dt, name=None):
        self.nsb += 1
        return self.stack.enter_context(self.nc.sbuf_tensor("s_" + (name or ("sb%d" % self.nsb)), list(shape), dt))

    def ps(self, shape, dt=F32, name=None):
        self.nsb += 1
        return self.stack.enter_context(self.nc.psum_tensor(name or ("ps%d" % self.nsb), list(shape), dt))

    def op(self, eng, fn, r=(), w=(), dma=None):
        idx = len(self.ops)
        deps = set()
        for k in r:
            x = self.last_w.get(k)
            if x is not None:
                deps.add(x)
        for k in w:
            x = self.last_w.get(k)
            if x is not None:
                deps.add(x)
            rl = self.readers.get(k)
            if rl:
                deps.update(rl)
        if dma is not None:
            skey = ("dma", dma)
            self.sem(skey)
            self.semcnt[skey] += 16
        else:
            skey = ("eng", eng)
            self.sem(skey)
            self.semcnt[skey] += 1
        done = (skey, self.semcnt[skey])
        self.ops.append((eng, fn, deps, done, dma is not None))
        for k in w:
            self.last_w[k] = idx
            self.readers[k] = []
        for k in r:
            self.readers.setdefault(k, []).append(idx)
        return idx

    def barrier(self, eng, fn, key):
        last = {}
        dmas = []
        start = getattr(self, "_bar_at", 0)
        for i in range(len(self.ops) - 1, -1, -1):
            o = self.ops[i]
            if o[4]:
                if i >= start:
                    dmas.append(i)
            elif o[0] not in last:
                last[o[0]] = i
            if i < start and len(last) >= 4:
                break
        idx = self.op(eng, fn, w=[key])
        e, f, deps, done, isd = self.ops[idx]
        deps.update(last.values())
        deps.update(dmas)
        deps.discard(idx)
        self._bar_at = idx
        return idx

    def emit(self, final_waits=()):
        nc = self.nc
        known = {e: {} for e in self.ENGS}
        per = {e: [] for e in self.ENGS}
        for (e, fn, deps, done, isdma) in self.ops:
            need = {}
            for d in deps:
                (de, _, _, (sk, val), ddma) = self.ops[d]
                if de == "pe" and e == "pe" and not ddma:
                    continue
                if need.get(sk, 0) < val:
                    need[sk] = val
            waits = []
            kn = known[e]
            for sk, val in need.items():
                if kn.get(sk, 0) >= val:
                    continue
                kn[sk] = val
                waits.append((sk, val))
            per[e].append((waits, fn, done, isdma))
        block = self.stack.enter_context(nc.Block())
        sems = self.sems

        def run(engobj, lst, extra=()):
            for waits, fn, (sk, val), isdma in lst:
                for wk, wv in waits:
                    engobj.wait_ge(sems[wk], wv)
                ins = fn(engobj)
                ins.then_inc(sems[sk], 16 if isdma else 1)
            for wk, wv in extra:
                engobj.wait_ge(sems[wk], wv)

        fin = [self.ops[i][3] for i in final_waits]

        @block.tensor
        def _(e):
            run(e, per["pe"])

        @block.scalar
        def _(e):
            run(e, per["act"])

        @block.vector
        def _(e):
            run(e, per["dve"])

        @block.gpsimd
        def _(e):
            run(e, per["pool"])

        @block.sync
        def _(e):
            run(e, per["sp"], fin)


class Arena:
    def __init__(self, t, nwords):
        self.t = t
        self.n = nwords
        self.off = 0

    def reset(self):
        self.off = 0

    def f32(self, n):
        n8 = (n + 7) // 8 * 8
        assert self.off + n8 <= self.n, ("arena overflow", self.off, n8, self.n)
        v = self.t[:, self.off:self.off + n]
        self.off += n8
        return v

    def bf16(self, n):
        w = (n + 1) // 2
        w = (w + 7) // 8 * 8
        assert self.off + w <= self.n, ("arena overflow", self.off, w, self.n)
        v = self.t[:, self.off:self.off + w].bitcast(BF16)[:, 0:n]
        self.off += w
        return v


class WStream:
    def __init__(self, P, dram, nch, wst, wbf, look=1):
        self.P, self.dram, self.nch, self.wst, self.wbf, self.look = P, dram, nch, wst, wbf, look
        self.issued = 0
        self.nxt = 0
        self.live = {}

    def _issue(self, i):
        s = i % len(self.wst)
        b = i % len(self.wbf)
        assert self.live.get(b) is None, ("weight slot still live", i, b, self.live.get(b))
        self.live[b] = i
        wst, wbf, dram = self.wst, self.wbf, self.dram
        self.P.op("sp", lambda e: e.dma_start(out=wst[s][:], in_=dram[i]), w=["wst%d" % s], dma="wst%d" % s)
        self.P.op("pool", lambda e: e.tensor_copy(out=wbf[b][:], in_=wst[s][:]), r=["wst%d" % s], w=["wbf%d" % b])

    def get(self):
        i = self.nxt
        self.nxt += 1
        while self.issued < min(self.nch, i + 1 + self.look):
            self._issue(self.issued)
            self.issued += 1
        b = i % len(self.wbf)
        return self.wbf[b], "wbf%d" % b, b

    def done(self, b):
        self.live[b] = None


class Ctx:
    pass


def setup_common(nc, P, C, nch, look=1, nbf=6):
    C.nc, C.P = nc, P
    C.ws_d = nc.dram_tensor("ws", [nch, 128, 1024], F32, kind="ExternalInput").ap()
    C.sp_d = nc.dram_tensor("sp", [128, NSP], F32, kind="ExternalInput").ap()
    C.cst_d = nc.dram_tensor("cst", [128, NCST], F32, kind="ExternalInput").ap()
    C.banks = [P.ps([128, 512], F32, name="bank%d" % i) for i in range(8)]
    C.wst = [P.sb([128, 1024], F32, name="wst%d" % i) for i in range(2)]
    C.wbf = [P.sb([128, 1024], BF16, name="wbf%d" % i) for i in range(nbf)]
    C.W = WStream(P, C.ws_d, nch, C.wst, C.wbf, look=look)
    C.spt = P.sb([128, NSP], F32, name="spt")
    C.cbf = P.sb([128, NCBF], BF16, name="cbf")
    C.cf = P.sb([128, NCST - NCBF], F32, name="cf")
    C.wupb = P.sb([128, 256], BF16, name="wupb")
    P.op("sp", lambda e: e.dma_start(out=C.spt[:], in_=C.sp_d[:, :]), w=["spt"], dma="spt")
    P.op("sp", lambda e: e.dma_start(out=C.cf[:], in_=C.cst_d[:, NCBF:NCST]), w=["cf"], dma="cf")
    P.op("pool", lambda e: e.tensor_copy(out=C.wupb[:], in_=C.spt[:, WUP:WUP + 256]), r=["spt"], w=["wupb"])


def bfview(bank):
    return bank[:].bitcast(BF16)


def rms_tile(C, xsrc, gcol, dst, ntok, rkeys, wkeys, ar, bank, bkey, tag):
    P = C.P
    sq = ar["sq"][:, 0:KC * ntok].rearrange("p (k t) -> p k t", k=KC)
    rs = ar["rs"][:, 0:ntok]
    for kc in range(KC):
        P.op("act", lambda e, kc=kc: e.activation(out=sq[:, kc, :], in_=xsrc(kc), func=AF.Square),
             r=rkeys + ["AE"], w=["sq%d" % kc])
    for kc in range(KC):
        P.op("pe", lambda e, kc=kc: e.matmul(bank[:, 0:ntok], lhsT=C.cbf[:, ONES:ONES + 128], rhs=sq[:, kc, :],
                                              start=(kc == 0), stop=(kc == KC - 1)),
             r=["sq%d" % kc, "cbf"], w=[bkey])
    P.op("act", lambda e: e.activation(out=rs, in_=bank[:, 0:ntok], func=AF.Sqrt, bias=EPS, scale=1.0 / D),
         r=[bkey, "AE"], w=["rs"])
    P.op("dve", lambda e: e.reciprocal(out=rs, in_=rs), r=["rs"], w=["rs"])
    for kc in range(KC):
        P.op("dve", lambda e, kc=kc: e.scalar_tensor_tensor(out=dst(kc), in0=xsrc(kc), scalar=C.spt[:, gcol + kc:gcol + kc + 1],
                                                             in1=rs, op0=ALU.mult, op1=ALU.mult),
             r=rkeys + ["rs", "spt"], w=[wkeys[kc]])


def barrier(C):
    C.P.barrier("pool", lambda e: e.memset(C.scr[:, 0:8], 0.0), "AE")


def gla_phase1(C, xn_tile, xn_blk, ar, with_q, kt_f, kt_b, qt_f, qt_b, v_sb, U, etf, etb):
    P, W, banks = C.P, C.W, C.banks
    w_lr, k_lr, b_lr = W.get()
    if with_q:
        w_q, k_q, b_q = W.get()
    w_k, k_k, b_k = W.get()
    glra = ar["glra"]
    t1 = [ar["t1f"], ar["t1b"]]
    t2 = [ar["t2f"], ar["t2b"]]
    t3 = [ar["t3f"], ar["t3b"]]
    t4 = [ar["t4f"], ar["t4b"]]
    t5 = ar["t5"]
    P.op("pool", lambda e: e.memset(glra[32:33, :], 1.0), r=["AE"], w=["glra1"])
    for t in range(NTL):
        sl = slice(t * TL, (t + 1) * TL)
        for kc in range(KC):
            P.op("pe", lambda e, kc=kc, t=t: e.matmul(banks[6][0:32, :], lhsT=w_lr[:, kc * 128:kc * 128 + 32], rhs=xn_tile(kc, t),
                                                       start=(kc == 0), stop=(kc == KC - 1)),
                 r=[k_lr, "xn:%d:%d" % (kc, t)], w=["b6"])
        P.op("act", lambda e: e.activation(out=glra[0:32, :], in_=banks[6][0:32, :], func=AF.Copy), r=["b6", "AE"], w=["glra"])
        for d in range(2):
            bk = banks[4 + d]
            P.op("pe", lambda e, d=d, bk=bk: e.matmul(bk[:, :], lhsT=C.wupb[0:33, d * 128:(d + 1) * 128], rhs=glra[0:33, :],
                                                       start=True, stop=True),
                 r=["glra", "glra1", "wupb"], w=["b%d" % (4 + d)])
            P.op("act", lambda e, d=d, bk=bk: e.activation(out=t1[d], in_=bk[:, :], func=AF.Exp, scale=-1.0),
                 r=["b%d" % (4 + d), "AE"], w=["t1%d" % d])
            P.op("act", lambda e, d=d: e.activation(out=t1[d], in_=t1[d], func=AF.Ln, bias=1.0), r=["t1%d" % d], w=["t1%d" % d])
            P.op("dve", lambda e, d=d: e.tensor_tensor_scan(out=t2[d], data0=C.cf[:, SCANM - NCBF:SCANM - NCBF + 512], data1=t1[d],
                                                             initial=0.0, op0=ALU.mult, op1=ALU.add),
                 r=["t1%d" % d, "cf", "AE"], w=["t2%d" % d])
            et = etf if d == 0 else etb
            c3 = t2[d].rearrange("p (a c) -> p a c", a=4)
            P.op("act", lambda e, et=et, c3=c3, t=t: e.activation(out=et[:, 4 * t:4 * t + 4], in_=c3[:, :, 127], func=AF.Exp, scale=-1.0 / 16),
                 r=["t2%d" % d, "AE"], w=["et%d:%d" % (d, t)])
            if d == 0:
                src = t2[0]
            else:
                P.op("dve", lambda e: e.tensor_tensor(out=t5, in0=t1[1], in1=t2[1], op=ALU.subtract), r=["t11", "t21", "AE"], w=["t5"])
                t53 = t5.rearrange("p (a c) -> p a c", a=4)
                P.op("dve", lambda e, t53=t53, c3=c3: e.tensor_tensor(out=t53, in0=t53, in1=c3[:, :, 127:128].to_broadcast([128, 4, 128]), op=ALU.add),
                     r=["t5", "t21"], w=["t5"])
                src = t5
            skey = "t20" if d == 0 else "t5"
            if with_q:
                P.op("act", lambda e, d=d, src=src: e.activation(out=t3[d], in_=src, func=AF.Exp, scale=-1.0 / 16, bias=LNC),
                     r=[skey, "AE"], w=["t3%d" % d])
            P.op("act", lambda e, d=d, src=src: e.activation(out=t4[d], in_=src, func=AF.Exp, scale=1.0 / 16), r=[skey, "AE"], w=["t4%d" % d])
        if with_q:
            for kc in range(KC):
                P.op("pe", lambda e, kc=kc, t=t: e.matmul(banks[0][:, :], lhsT=w_q[:, kc * 128:(kc + 1) * 128], rhs=xn_tile(kc, t),
                                                           start=(kc == 0), stop=(kc == KC - 1)),
                     r=[k_q, "xn:%d:%d" % (kc, t)], w=["b0"])
            for d, qt in ((0, qt_f), (1, qt_b)):
                P.op("dve", lambda e, d=d, qt=qt, sl=sl: e.tensor_tensor(out=qt[:, sl], in0=banks[0][:, :], in1=t3[d], op=ALU.mult),
                     r=["b0", "t3%d" % d, "AE"], w=["qt%d:%d" % (d, t)])
        for kc in range(KC):
            P.op("pe", lambda e, kc=kc, t=t: e.matmul(banks[1][:, :], lhsT=w_k[:, kc * 128:(kc + 1) * 128], rhs=xn_tile(kc, t),
                                                       start=(kc == 0), stop=(kc == KC - 1)),
                 r=[k_k, "xn:%d:%d" % (kc, t)], w=["b1"])
        for d, kt in ((0, kt_f), (1, kt_b)):
            P.op("dve", lambda e, d=d, kt=kt, sl=sl: e.tensor_tensor(out=kt[:, sl], in0=banks[1][:, :], in1=t4[d], op=ALU.mult),
                 r=["b1", "t4%d" % d, "AE"], w=["kt%d:%d" % (d, t)])
    W.done(b_lr)
    W.done(b_k)
    if with_q:
        W.done(b_q)
    w_v0, k_v0, b_v0 = W.get()
    w_v1, k_v1, b_v1 = W.get()
    for n2 in range(NB // 2):
        bk = banks[2 + (n2 % 2)]
        bkey = "b%d" % (2 + (n2 % 2))
        for j in range(2):
            n = 2 * n2 + j
            for half, (wv, kv) in enumerate(((w_v0, k_v0), (w_v1, k_v1))):
                for kc in range(KC):
                    P.op("pe", lambda e, kc=kc, n=n, j=j, half=half, wv=wv, bk=bk: e.matmul(
                        bk[:, j * 256 + half * 128:j * 256 + half * 128 + 128], lhsT=xn_blk(kc, n), rhs=wv[:, kc * 128:(kc + 1) * 128],
                        start=(kc == 0), stop=(kc == KC - 1)),
                         r=[kv, "xn:%d:%d" % (kc, n // 4)], w=[bkey])
        P.op("act", lambda e, n2=n2, bk=bk: e.activation(out=v_sb[:, 2 * n2:2 * n2 + 2, :], in_=bk[:, :].rearrange("p (a c) -> p a c", a=2), func=AF.Copy),
             r=[bkey, "AE"], w=["v:%d" % n2])
    W.done(b_v0)
    W.done(b_v1)
    kh = [ar["kh0"], ar["kh1"]]
    khs = [ar["khs0"], ar["khs1"]]
    tmpU = ar["tmpU"]
    P.op("pool", lambda e: e.memset(U.rearrange("p a n c -> p (a n c)"), 0.0), r=["AE"], w=["Uz"])
    bTs = [bfview(banks[4]), bfview(banks[5])]
    for n in range(NB):
        bs = slice(n * 128, (n + 1) * 128)
        for d, (kt, et) in enumerate(((kt_f, etf), (kt_b, etb))):
            P.op("dve", lambda e, d=d, kt=kt, et=et, n=n, bs=bs: e.tensor_scalar(out=kh[d], in0=kt[:, bs], scalar1=et[:, n:n + 1], scalar2=None, op0=ALU.mult),
                 r=["kt%d:%d" % (d, n // 4), "et%d:%d" % (d, n // 4), "AE"], w=["kh%d" % d])
            P.op("pe", lambda e, d=d: e.transpose(out=bTs[d][:, 0:128], in_=kh[d], identity=C.cbf[:, IDENT:IDENT + 128]),
                 r=["kh%d" % d, "cbf"], w=["b%d" % (4 + d)])
            P.op("act", lambda e, d=d: e.activation(out=khs[d], in_=bTs[d][:, 0:128], func=AF.Copy), r=["b%d" % (4 + d), "AE"], w=["khs%d" % d])
            P.op("pe", lambda e, d=d, n=n: e.matmul(banks[7][:, d * 256:(d + 1) * 256], lhsT=khs[d], rhs=v_sb[:, n, :], start=True, stop=True),
                 r=["khs%d" % d, "v:%d" % (n // 2)], w=["b7"])
        P.op("dve", lambda e: e.tensor_tensor(out=tmpU.rearrange("p (a c) -> p a c", a=2), in0=banks[7][:, :].rearrange("p (a c) -> p a c", a=2),
                                              in1=C.cf[:, BDM - NCBF:BDM - NCBF + 256].unsqueeze(1).to_broadcast([128, 2, 256]), op=ALU.mult),
             r=["b7", "cf", "AE"], w=["tmpU"])
        for d in range(2):
            P.op("dve", lambda e, d=d, n=n: e.tensor_reduce(out=U[:, d, n, 0:64], in_=tmpU[:, d * 256:(d + 1) * 256].rearrange("p (h c) -> p c h", h=4),
                                                             axis=AX.X, op=ALU.add),
                 r=["tmpU", "Uz", "AE"], w=["U:%d:%d" % (d, n)])


def load_consts(C, ar):
    P = C.P
    tmp = ar.f32(NCBF)
    P.op("sp", lambda e: e.dma_start(out=tmp, in_=C.cst_d[:, 0:NCBF]), r=["AE"], w=["cst_tmp"], dma="cst")
    P.op("dve", lambda e: e.tensor_copy(out=C.cbf[:], in_=tmp), r=["cst_tmp"], w=["cbf"])
    C.scr = C.P.sb([128, 8], F32, name="scr")


SUM_CH = 4


def build_sum():
    nc = bass.Bass("TRN2", target_bir_lowering=False)
    st = ExitStack()
    P = Prog(nc, st)
    C = Ctx()
    setup_common(nc, P, C, SUM_CH)
    xin = nc.dram_tensor("xT", [D, NT], F32, kind="ExternalInput").ap()
    sout = nc.dram_tensor("so", [128, 2 * 65], F32, kind="ExternalOutput").ap()
    xn = P.sb([128, KC, NT], BF16, name="xn")
    ysb = P.sb([128, 4, NT], BF16, name="ysb")
    art = P.sb([128, 16384], F32, name="arena")
    A = Arena(art, 16384)
    load_consts(C, A)
    barrier(C)
    A.reset()
    xt = [A.f32(KC * TL), A.f32(KC * TL)]
    ar = dict(sq=A.bf16(KC * TL), rs=A.f32(TL))
    xv = xin.rearrange("(k p) t -> p k t", p=128)
    for t in range(NTL):
        b = t % 2
        xb = xt[b].rearrange("p (k t) -> p k t", k=KC)
        P.op("sp" if t % 2 == 0 else "pool", lambda e, xb=xb, t=t: e.dma_start(out=xb, in_=xv[:, :, t * TL:(t + 1) * TL]), r=["AE"], w=["xt%d" % b], dma="xt%d" % b)
        rms_tile(C, lambda kc, xb=xb: xb[:, kc, :], MIXG, lambda kc, t=t: xn[:, kc, t * TL:(t + 1) * TL], TL,
                 ["xt%d" % b], ["xn:%d:%d" % (kc, t) for kc in range(KC)], ar, C.banks[3], "b3", "s%d" % t)
    barrier(C)
    A.reset()
    g = dict(glra=A.bf16(TL), t1f=A.f32(TL), t1b=A.f32(TL), t2f=A.f32(TL), t2b=A.f32(TL), t3f=None, t3b=None,
             t4f=A.f32(TL), t4b=A.f32(TL), t5=A.f32(TL), kh0=A.bf16(128), kh1=A.bf16(128), khs0=A.bf16(128), khs1=A.bf16(128),
             tmpU=A.f32(512))
    U = A.f32(2 * 16 * 65).rearrange("p (a n c) -> p a n c", a=2, n=16)
    etf = A.f32(16)
    etb = A.f32(16)
    Sf = A.f32(65)
    Sb = A.f32(65)
    v_sb = ysb[:, 2:4, :].rearrange("p c t -> p (c t)").rearrange("p (n c) -> p n c", n=16)
    gla_phase1(C, lambda kc, t: xn[:, kc, t * TL:(t + 1) * TL], lambda kc, n: xn[:, kc, n * 128:(n + 1) * 128], g, False,
               ysb[:, 0, :], ysb[:, 1, :], None, None, v_sb, U, etf, etb)
    ukeys = lambda d: ["U:%d:%d" % (d, n) for n in range(NB)]
    for S, d in ((Sf, 0), (Sb, 1)):
        P.op("pool", lambda e, S=S: e.memset(S[:, 0:64], 0.0), r=["AE"], w=["S%d" % d])
        P.op("pool", lambda e, S=S: e.memset(S[:, 64:65], 1.0), r=["AE"], w=["S%db" % d])
    for n in range(NB):
        P.op("dve", lambda e, n=n: e.scalar_tensor_tensor(out=Sf, in0=Sf, scalar=etf[:, n:n + 1], in1=U[:, 0, n, :], op0=ALU.mult, op1=ALU.add),
             r=["S0", "S0b", "U:0:%d" % n, "Uz", "et0:%d" % (n // 4), "AE"], w=["S0"])
    for n in range(NB - 1, -1, -1):
        P.op("dve", lambda e, n=n: e.scalar_tensor_tensor(out=Sb, in0=Sb, scalar=etb[:, n:n + 1], in1=U[:, 1, n, :], op0=ALU.mult, op1=ALU.add),
             r=["S1", "S1b", "U:1:%d" % n, "Uz", "et1:%d" % (n // 4), "AE"], w=["S1"])
    o1 = P.op("sp", lambda e: e.dma_start(out=sout[:, 0:65], in_=Sf), r=["S0"], dma="o1")
    o2 = P.op("sp", lambda e: e.dma_start(out=sout[:, 65:130], in_=Sb), r=["S1"], dma="o2")
    P.emit(final_waits=[o1, o2])
    return nc, st


def _chunk_cols(Wm, cols):
    sub = Wm[:, cols]
    if sub.shape[1] < 128:
        sub = np.concatenate([sub, np.zeros((sub.shape[0], 128 - sub.shape[1]), np.float32)], 1)
    return np.ascontiguousarray(sub.reshape(KC, 128, 128).transpose(1, 0, 2)).reshape(128, 1024)


def win_chunk(w_in_l, name):
    o = IN_OFF
    if name == "glr":
        cols = np.arange(o["glr"], o["glr"] + 32)
    elif name in ("gq", "gk", "ak", "av"):
        cols = np.arange(o[name], o[name] + 128)
    elif name in ("gv0", "gv1", "gr0", "gr1", "pin0", "pin1"):
        b = o[name[:-1]] + 128 * int(name[-1])
        cols = np.arange(b, b + 128)
    elif name in ("A0", "A1"):
        b = o["glu_a"] + 128 * int(name[-1])
        cols = np.arange(b, b + 128)
    elif name in ("G0", "G1"):
        b = o["glu_g"] + 128 * int(name[-1])
        cols = np.arange(b, b + 128)
    elif name == "QA":
        cols = np.concatenate([np.arange(o["aq"], o["aq"] + 64), np.arange(o["aq"] + 128, o["aq"] + 192)])
    elif name == "QB":
        cols = np.concatenate([np.arange(o["aq"] + 64, o["aq"] + 128), np.arange(o["aq"] + 192, o["aq"] + 256)])
    else:
        raise KeyError(name)
    return _chunk_cols(w_in_l, cols)


def sum_stream(inp, l):
    w = inp["w_in"][l]
    return np.stack([win_chunk(w, n) for n in ("glr", "gk", "gv0", "gv1")], 0)


def small_params(inp, l):
    sp = np.zeros((128, NSP), np.float32)
    sp[:, MIXG:MIXG + 8] = inp["norm_mix_g"][l].reshape(8, 128).T
    sp[:, FFNG:FFNG + 8] = inp["norm_ffn_g"][l].reshape(8, 128).T
    sp[:, FING:FING + 8] = inp["final_norm_g"].reshape(8, 128).T
    sp[:, CONVB:CONVB + 2] = inp["conv_b"][l].reshape(2, 128).T
    sp[:, LNG:LNG + 2] = inp["conv_ln_g"][l].reshape(2, 128).T
    sp[:, LNB:LNB + 2] = inp["conv_ln_b"][l].reshape(2, 128).T
    sp[:, GLAG:GLAG + 2] = inp["gla_norm_g"][l].reshape(2, 128).T
    sp[:, SINK:SINK + 4] = inp["attn_sink"][l][None, :]
    sp[:, PSC:PSC + 2] = inp["pool_scale"][l].reshape(2, 128).T
    cw = inp["conv_w"][l]
    for c in range(2):
        sp[:, CONVW + 31 * c:CONVW + 31 * (c + 1)] = cw[:, c * 128:(c + 1) * 128].T
    wu = inp["gla_w_up"][l]
    bu = inp["gla_b_up"][l]
    sp[0:16, WUP:WUP + 128] = wu[0]
    sp[16:32, WUP + 128:WUP + 256] = wu[1]
    sp[32, WUP:WUP + 128] = bu[0]
    sp[32, WUP + 128:WUP + 256] = bu[1]
    pw = inp["pool_w"][l]
    for c in range(2):
        for gg in range(2):
            g = 2 * c + gg
            sp[gg * 64:(gg + 1) * 64, POOLW + c * 128 + gg * 64:POOLW + c * 128 + (gg + 1) * 64] = pw[g]
    return sp


def const_block():
    c = np.zeros((128, NCST), np.float32)
    i = np.arange(128)
    c[:, IDENT:IDENT + 128] = np.eye(128, dtype=np.float32)
    c[:, ONES:ONES + 128] = 1.0
    c[:, ONESBLK:ONESBLK + 128] = (i[:, None] // 64 == i[None, :] // 64)
    pm = np.zeros((128, 128), np.float32)
    for hb in (0, 64):
        for j in range(8):
            pm[hb + j + 8, hb + j] = -1.0
            pm[hb + j, hb + j + 8] = 1.0
    c[:, PMAT:PMAT + 128] = pm
    c[:, TRIF:TRIF + 128] = (i[:, None] <= i[None, :])
    c[:, TRIB:TRIB + 128] = (i[:, None] >= i[None, :])
    c[:, BDM:BDM + 256] = (i[:, None] // 32 == np.arange(256)[None, :] // 64)
    c[:, HEADM:HEADM + 4] = (i[:, None] // 32 == np.arange(4)[None, :])
    c[:, INVW] = np.where(i < 64, 1.0 / 2, 1.0 / 4)
    c[:, INVW + 1] = np.where(i < 64, 1.0 / 8, 1.0 / 16)
    sm = np.ones(512, np.float32)
    sm[::128] = 0.0
    c[:, SCANM:SCANM + 512] = sm[None, :]
    return c


LAYER_STREAM = (["glr", "gq", "gk", "gv0", "gv1", "gr0", "gr1", "QA", "QB", "ak", "av", "A0", "G0", "A1", "G1", "pin0", "pin1"])
N_INPROJ = len(LAYER_STREAM)
LAYER_NCH = N_INPROJ + 8 * 5 + 8 + 4 * 16


def layer_stream(inp, l):
    w = inp["w_in"][l]
    ch = [win_chunk(w, n) for n in LAYER_STREAM]
    wb = inp["w_branch"][l]
    for f in range(8):
        blk = np.stack([wb[n][:, f * 128:(f + 1) * 128].reshape(2, 128, 128).transpose(1, 0, 2) for n in range(4)], 1)
        ch.append(np.ascontiguousarray(blk).reshape(128, 1024))
        for n in range(4):
            b = IN_OFF["gate"] + n * 1024 + f * 128
            ch.append(_chunk_cols(w, np.arange(b, b + 128)))
    wo = inp["w_out"][l]
    for f in range(8):
        ch.append(_chunk_cols(wo, np.arange(f * 128, (f + 1) * 128)))
    wu = inp["w_ffn_up"][l]
    wd = inp["w_ffn_down"][l]
    for g in range(4):
        for c in range(8):
            cc = g * 8 + c
            ch.append(_chunk_cols(wu, np.arange(cc * 128, (cc + 1) * 128)))
        for f in range(8):
            ch.append(_chunk_cols(wd[g * 1024:(g + 1) * 1024], np.arange(f * 128, (f + 1) * 128)))
    out = np.stack(ch, 0)
    assert out.shape[0] == LAYER_NCH
    return out


def percore_consts(core):
    p = core % 4
    pc = np.zeros((128, NPC), np.float32)
    q = np.arange(128)[:, None]
    s = np.arange(384)[None, :]
    base = ((s >= q) & (s <= q + 256))
    for v in range(3):
        m = base.copy()
        if v == 0 and p == 0:
            m &= (s >= 128)
        if v == 2 and p == 3:
            m &= (s < 256)
        pc[:, AMASK + v * 384:AMASK + (v + 1) * 384] = m
    i = np.arange(128)
    wid = [np.where(i < 64, 2, 4), np.where(i < 64, 8, 16)]
    corr = np.ones((128, 2, 16), np.float32)
    for c in range(2):
        w = wid[c].astype(np.float64)
        for j in range(8):
            if p == 0:
                pos = j
                lo = np.maximum(pos - w / 2, 0)
                hi = pos + w / 2
                corr[:, c, j] = w / (hi - lo)
            if p == 3:
                pos = SEQ - 8 + j
                lo = pos - w / 2
                hi = np.minimum(pos + w / 2, SEQ)
                corr[:, c, 8 + j] = w / (hi - lo)
    pc[:, PCORR:PCORR + 32] = corr.reshape(128, 32)
    cf = np.zeros((4, 3), np.float32)
    cb = np.zeros((4, 3), np.float32)
    for r in range(4):
        cf[r] = (1, 0, 1) if r < p else (0, 1, 0)
        cb[r] = (1, 0, 1) if r > p else (0, 1, 0)
    pc[:, COEFF:COEFF + 12] = cf.reshape(1, 12)
    pc[:, COEFB:COEFB + 12] = cb.reshape(1, 12)
    return pc


def rope_tables(core):
    p = core % 4
    pos = (p * NT - HB + np.arange(NE)).astype(np.float32)
    inv = (1.0 / (np.float32(500000.0) ** (np.arange(0, 16, 2, dtype=np.float32) / np.float32(16)))).astype(np.float32)
    ang = pos[:, None] * inv[None, :]
    cs = np.cos(ang).astype(np.float32).T
    sn = np.sin(ang).astype(np.float32).T
    Cm = np.ones((128, NE), np.float32)
    Sm = np.zeros((128, NE), np.float32)
    for hb in (0, 64):
        Cm[hb:hb + 8] = cs
        Cm[hb + 8:hb + 16] = cs
        Sm[hb:hb + 8] = sn
        Sm[hb + 8:hb + 16] = sn
    return np.stack([Cm, Sm], 0)


def build_layer(debug=False):
    nc = bass.Bass("TRN2", target_bir_lowering=False)
    st = ExitStack()
    P = Prog(nc, st)
    C = Ctx()
    setup_common(nc, P, C, LAYER_NCH)
    W, banks = C.W, C.banks
    xin = nc.dram_tensor("xT", [D, NE], F32, kind="ExternalInput").ap()
    pc_d = nc.dram_tensor("pc", [128, NPC], F32, kind="ExternalInput").ap()
    rope_d = nc.dram_tensor("rope", [2, 128, NE], F32, kind="ExternalInput").ap()
    gsum_d = nc.dram_tensor("gsum", [128, 4 * 130], F32, kind="ExternalInput").ap()
    xout = nc.dram_tensor("xo", [D, NT], F32, kind="ExternalOutput").ap()
    yout = nc.dram_tensor("yo", [D, NT], F32, kind="ExternalOutput").ap()
    if debug:
        dbg = nc.dram_tensor("dbg", [128, KC * NT], BF16, kind="ExternalOutput").ap()

    xT = P.sb([128, KC, NT], F32, name="xT")
    xn = P.sb([128, KC, NT], BF16, name="xn")
    xnh = P.sb([128, KC, 2 * HB], BF16, name="xnh")
    ys = P.sb([128, KC, NT], BF16, name="ys")
    pcf = P.sb([128, NPC - AMASK - 1152], F32, name="pcf")
    amb = P.sb([128, 1152], BF16, name="amb")
    AW = 11328
    art = P.sb([128, AW], F32, name="arena")
    A = Arena(art, AW)
    cbf, cf, spt = C.cbf, C.cf, C.spt
    CF = lambda off, n: cf[:, off - NCBF:off - NCBF + n]

    load_consts(C, A)
    tmpm = A.f32(1152)
    P.op("sp", lambda e: e.dma_start(out=tmpm, in_=pc_d[:, AMASK:AMASK + 1152]), r=["AE"], w=["tmpm"], dma="pc1")
    P.op("dve", lambda e: e.tensor_copy(out=amb[:], in_=tmpm), r=["tmpm"], w=["amb"])
    P.op("sp", lambda e: e.dma_start(out=pcf[:], in_=pc_d[:, PCORR:NPC]), w=["pcf"], dma="pc2")
    PCF = lambda off, n: pcf[:, off - PCORR:off - PCORR + n]
    barrier(C)
    A.reset()

    xv = xin.rearrange("(k p) t -> p k t", p=128)
    for kc in range(KC):
        P.op("sp" if kc % 2 == 0 else "pool", lambda e, kc=kc: e.dma_start(out=xT[:, kc, :], in_=xin[kc * 128:(kc + 1) * 128, HB:HB + NT]),
             w=["x:%d:%d" % (kc, t) for t in range(NTL)], dma="xl%d" % kc)
    xh = A.f32(KC * 2 * HB).rearrange("p (k t) -> p k t", k=KC)
    P.op("sp", lambda e: e.dma_start(out=xh[:, :, 0:HB], in_=xv[:, :, 0:HB]), r=["AE"], w=["xh0"], dma="xh0")
    P.op("sp", lambda e: e.dma_start(out=xh[:, :, HB:2 * HB], in_=xv[:, :, HB + NT:NE]), r=["AE"], w=["xh1"], dma="xh1")
    arn = dict(sq=A.bf16(KC * TL), rs=A.f32(TL))

    def norm_all(gcol):
        for t in range(NTL):
            rms_tile(C, lambda kc, t=t: xT[:, kc, t * TL:(t + 1) * TL], gcol, lambda kc, t=t: xn[:, kc, t * TL:(t + 1) * TL], TL,
                     ["x:%d:%d" % (kc, t) for kc in range(KC)], ["xn:%d:%d" % (kc, t) for kc in range(KC)], arn, banks[3], "b3", "n%d" % t)

    norm_all(MIXG)
    rms_tile(C, lambda kc: xh[:, kc, :], MIXG, lambda kc: xnh[:, kc, :], 2 * HB, ["xh0", "xh1"], ["xnh:%d" % kc for kc in range(KC)], arn, banks[3], "b3", "nh")
    barrier(C)
    A.reset()

    xn_tile = lambda kc, t: xn[:, kc, t * TL:(t + 1) * TL]
    xn_blk = lambda kc, n: xn[:, kc, n * 128:(n + 1) * 128]
    ext_ranges = [(lambda kc, t=t: xn[:, kc, t * TL:(t + 1) * TL], HB + t * TL, TL, (lambda kc, t=t: "xn:%d:%d" % (kc, t))) for t in range(NTL)]
    ext_ranges.append((lambda kc: xnh[:, kc, 0:HB], 0, HB, lambda kc: "xnh:%d" % kc))
    ext_ranges.append((lambda kc: xnh[:, kc, HB:2 * HB], HB + NT, HB, lambda kc: "xnh:%d" % kc))

    qt_f, qt_b, kt_f, kt_b = ys[:, 0, :], ys[:, 1, :], ys[:, 4, :], ys[:, 5, :]
    v_sb = ys[:, 6:8, :].rearrange("p c t -> p (c t)").rearrange("p (n c) -> p n c", n=16)
    U = A.f32(2 * 16 * 65).rearrange("p (a n c) -> p a n c", a=2, n=16)
    etf = A.f32(16)
    etb = A.f32(16)
    G = A.f32(4 * 130)
    Sst = A.f32(17 * 64 * 2).rearrange("p (d n c) -> p d n c", d=2, n=17)
    wr = A.f32(8)
    Tr = A.f32(4 * 64)
    amark = A.off
    g = dict(glra=A.bf16(TL), t1f=A.f32(TL), t1b=A.f32(TL), t2f=A.f32(TL), t2b=A.f32(TL), t3f=A.f32(TL), t3b=A.f32(TL),
             t4f=A.f32(TL), t4b=A.f32(TL), t5=A.f32(TL), kh0=A.bf16(128), kh1=A.bf16(128), khs0=A.bf16(128), khs1=A.bf16(128),
             tmpU=A.f32(512))
    gla_phase1(C, xn_tile, xn_blk, g, True, kt_f, kt_b, qt_f, qt_b, v_sb, U, etf, etb)
    G4 = G.rearrange("p (r d c) -> p r d c", r=4, d=2)
    P.op("sp", lambda e: e.dma_start(out=G, in_=gsum_d[:, :]), r=["AE"], w=["G"], dma="gs")
    for d, coff in ((0, COEFF), (1, COEFB)):
        co = PCF(coff, 12).rearrange("p (r c) -> p r c", r=4)
        P.op("dve", lambda e, d=d, co=co: e.tensor_tensor(out=wr[:, 0:4], in0=co[:, :, 0], in1=G4[:, :, d, 64], op=ALU.mult), r=["G", "pcf", "AE"], w=["wr"])
        P.op("dve", lambda e, co=co: e.tensor_tensor(out=wr[:, 0:4], in0=wr[:, 0:4], in1=co[:, :, 1], op=ALU.add), r=["wr", "pcf"], w=["wr"])
        P.op("dve", lambda e, d=d, co=co: e.tensor_tensor(out=Tr.rearrange("p (r c) -> p r c", r=4), in0=G4[:, :, d, 0:64],
                                                          in1=co[:, :, 2:3].to_broadcast([128, 4, 64]), op=ALU.mult), r=["G", "pcf", "AE"], w=["Tr"])
        s0 = Sst[:, 0, 0, :] if d == 0 else Sst[:, 1, 16, :]
        P.op("pool", lambda e, s0=s0: e.memset(s0, 0.0), r=["AE"], w=["S0_%d" % d])
        order = range(4) if d == 0 else range(3, -1, -1)
        for r_ in order:
            P.op("dve", lambda e, r_=r_, s0=s0: e.scalar_tensor_tensor(out=s0, in0=s0, scalar=wr[:, r_:r_ + 1], in1=Tr[:, r_ * 64:(r_ + 1) * 64],
                                                                         op0=ALU.mult, op1=ALU.add), r=["S0_%d" % d, "wr", "Tr"], w=["S0_%d" % d])
    for n in range(NB):
        P.op("dve", lambda e, n=n: e.scalar_tensor_tensor(out=Sst[:, 0, n + 1, :], in0=Sst[:, 0, n, :], scalar=etf[:, n:n + 1], in1=U[:, 0, n, 0:64],
                                                           op0=ALU.mult, op1=ALU.add),
             r=["S0_0" if n == 0 else "Sf:%d" % n, "U:0:%d" % n, "et0:%d" % (n // 4), "AE"], w=["Sf:%d" % (n + 1)])
    for n in range(NB - 1, -1, -1):
        P.op("dve", lambda e, n=n: e.scalar_tensor_tensor(out=Sst[:, 1, n, :], in0=Sst[:, 1, n + 1, :], scalar=etb[:, n:n + 1], in1=U[:, 1, n, 0:64],
                                                           op0=ALU.mult, op1=ALU.add),
             r=["S0_1" if n == NB - 1 else "Sb:%d" % (n + 1), "U:1:%d" % n, "et1:%d" % (n // 4), "AE"], w=["Sb:%d" % n])
    barrier(C)
    A.off = amark
    w_g0, k_g0, b_g0 = W.get()
    w_g1, k_g1, b_g1 = W.get()
    qm = [[A.bf16(512), A.bf16(512)], [A.bf16(512), A.bf16(512)]]
    scs = [[A.bf16(512), A.bf16(512)], [A.bf16(512), A.bf16(512)]]
    sbd = [[A.bf16(256), A.bf16(256)], [A.bf16(256), A.bf16(256)]]
    grs = A.bf16(2 * TL).rearrange("p (c t) -> p c t", c=2)
    sqo = A.bf16(2 * TL).rearrange("p (c t) -> p c t", c=2)
    rso = A.f32(2 * TL).rearrange("p (c t) -> p c t", c=2)
    yt = A.f32(TL)
    hm = CF(HEADM, 4)
    dirs = ((qt_f, kt_f, TRIF), (qt_b, kt_b, TRIB))

    def g_a(n):
        pb, t = n % 2, n // 4
        bs = slice(n * 128, (n + 1) * 128)
        for d, (qt, kt, tri) in enumerate(dirs):
            P.op("dve", lambda e, d=d, qt=qt: e.tensor_tensor(out=qm[pb][d].rearrange("p (h i) -> p h i", h=4),
                                                              in0=qt[:, bs].unsqueeze(1).to_broadcast([128, 4, 128]),
                                                              in1=hm.unsqueeze(2).to_broadcast([128, 4, 128]), op=ALU.mult),
                 r=["qt%d:%d" % (d, t), "cf", "AE"], w=["qm%d:%d" % (pb, d)])
        for d, (qt, kt, tri) in enumerate(dirs):
            P.op("pe", lambda e, d=d, kt=kt: e.matmul(banks[2 * pb + d][:, :], lhsT=kt[:, bs], rhs=qm[pb][d], start=True, stop=True),
                 r=["kt%d:%d" % (d, t), "qm%d:%d" % (pb, d)], w=["b%d" % (2 * pb + d)])

    def g_b(n):
        pb, t, j = n % 2, n // 4, n % 4
        bs = slice(n * 128, (n + 1) * 128)
        for d, (qt, kt, tri) in enumerate(dirs):
            P.op("dve", lambda e, d=d, tri=tri: e.tensor_tensor(out=scs[pb][d].rearrange("p (h i) -> p h i", h=4),
                                                                in0=banks[2 * pb + d][:, :].rearrange("p (h i) -> p h i", h=4),
                                                                in1=cbf[:, tri:tri + 128].unsqueeze(1).to_broadcast([128, 4, 128]), op=ALU.mult),
                 r=["b%d" % (2 * pb + d), "cbf", "AE"], w=["scs%d:%d" % (pb, d)])
            ssrc = Sst[:, 0, n, :] if d == 0 else Sst[:, 1, n + 1, :]
            skey = ("S0_0" if n == 0 else "Sf:%d" % n) if d == 0 else ("S0_1" if n == NB - 1 else "Sb:%d" % (n + 1))
            P.op("dve", lambda e, d=d, ssrc=ssrc: e.tensor_tensor(out=sbd[pb][d].rearrange("p (h c) -> p h c", h=4),
                                                                  in0=ssrc.unsqueeze(1).to_broadcast([128, 4, 64]),
                                                                  in1=hm.unsqueeze(2).to_broadcast([128, 4, 64]), op=ALU.mult),
                 r=[skey, "cf", "AE"], w=["sbd%d:%d" % (pb, d)])
        for hp in range(2):
            ob = banks[4 + hp]
            okey = "b%d" % (4 + hp)
            oc = slice(j * 128, (j + 1) * 128)
            P.op("pe", lambda e, hp=hp, ob=ob, oc=oc: e.matmul(ob[:, oc], lhsT=sbd[pb][0][:, hp * 128:(hp + 1) * 128], rhs=qt_f[:, bs], start=True, stop=False),
                 r=["sbd%d:0" % pb, "qt0:%d" % t], w=[okey])
            P.op("pe", lambda e, hp=hp, ob=ob, oc=oc: e.matmul(ob[:, oc], lhsT=sbd[pb][1][:, hp * 128:(hp + 1) * 128], rhs=qt_b[:, bs], start=False, stop=False),
                 r=["sbd%d:1" % pb, "qt1:%d" % t], w=[okey])
            for hh in range(2):
                h = 2 * hp + hh
                for d in range(2):
                    last = (hh == 1 and d == 1)
                    P.op("pe", lambda e, ob=ob, oc=oc, hh=hh, h=h, d=d, last=last: e.matmul(
                        ob[hh * 64:(hh + 1) * 64, oc], lhsT=v_sb[:, n, h * 64:(h + 1) * 64], rhs=scs[pb][d][:, h * 128:(h + 1) * 128], start=False, stop=last),
                         r=["scs%d:%d" % (pb, d), "v:%d" % (n // 2)], w=[okey])

    def g_gr(t):
        for c, (wg, kg) in enumerate(((w_g0, k_g0), (w_g1, k_g1))):
            for kc in range(KC):
                P.op("pe", lambda e, kc=kc, c=c, wg=wg: e.matmul(banks[6 + c][:, :], lhsT=wg[:, kc * 128:(kc + 1) * 128], rhs=xn_tile(kc, t),
                                                                 start=(kc == 0), stop=(kc == KC - 1)),
                     r=[kg, "xn:%d:%d" % (kc, t)], w=["b%d" % (6 + c)])
            P.op("act", lambda e, c=c: e.activation(out=grs[:, c, :], in_=banks[6 + c][:, :], func=AF.Silu), r=["b%d" % (6 + c), "AE"], w=["grs%d" % c])

    def g_post(t):
        for hp in range(2):
            ob = banks[4 + hp]
            okey = "b%d" % (4 + hp)
            P.op("act", lambda e, hp=hp, ob=ob: e.activation(out=sqo[:, hp, :], in_=ob[:, :], func=AF.Square), r=[okey, "AE"], w=["sqo%d" % hp])
            P.op("pe", lambda e, hp=hp: e.matmul(banks[6 + hp][:, :], lhsT=cbf[:, ONESBLK:ONESBLK + 128], rhs=sqo[:, hp, :], start=True, stop=True),
                 r=["sqo%d" % hp, "cbf", "grs%d" % hp], w=["b%d" % (6 + hp)])
            P.op("act", lambda e, hp=hp: e.activation(out=rso[:, hp, :], in_=banks[6 + hp][:, :], func=AF.Sqrt, bias=EPS, scale=1.0 / 64),
                 r=["b%d" % (6 + hp), "AE"], w=["rso%d" % hp])
            P.op("dve", lambda e, hp=hp: e.reciprocal(out=rso[:, hp, :], in_=rso[:, hp, :]), r=["rso%d" % hp], w=["rso%d" % hp])
            P.op("dve", lambda e, hp=hp, ob=ob: e.tensor_tensor(out=yt, in0=ob[:, :], in1=rso[:, hp, :], op=ALU.mult), r=[okey, "rso%d" % hp, "AE"], w=["yt"])
            P.op("dve", lambda e, hp=hp: e.scalar_tensor_tensor(out=ys[:, 2 + hp, t * TL:(t + 1) * TL], in0=yt, scalar=spt[:, GLAG + hp:GLAG + hp + 1],
                                                                 in1=grs[:, hp, :], op0=ALU.mult, op1=ALU.mult),
                 r=["yt", "grs%d" % hp, "spt"], w=["ys:%d:%d" % (2 + hp, t)])

    g_gr(0)
    g_a(0)
    for n in range(NB):
        if n + 1 < NB:
            g_a(n + 1)
        g_b(n)
        if n % 4 == 3:
            g_post(n // 4)
            if n + 1 < NB:
                g_gr(n // 4 + 1)
    W.done(b_g0)
    W.done(b_g1)
    barrier(C)
    A.reset()
    C.A, C.xT, C.xn, C.xnh, C.ys, C.amb, C.PCF, C.CF, C.ext_ranges = A, xT, xn, xnh, ys, amb, PCF, CF, ext_ranges
    C.rope_d, C.xout, C.yout, C.norm_all, C.arn_fn = rope_d, xout, yout, norm_all, None
    C.debug = debug
    if debug:
        C.dbg = dbg
    return nc, st, P, C


def proj_ext(C, wt, wkey, rng, bank, bkey, M=128):
    rhs_fn, eoff, ntok, keyf = rng
    for kc in range(KC):
        C.P.op("pe", lambda e, kc=kc: e.matmul(bank[0:M, 0:ntok], lhsT=wt[:, kc * 128:kc * 128 + M], rhs=rhs_fn(kc),
                                                start=(kc == 0), stop=(kc == KC - 1)),
               r=[wkey, keyf(kc)], w=[bkey])


def attention_phase(C):
    P, W, banks, A, ys = C.P, C.W, C.banks, C.A, C.ys
    cbf, spt = C.cbf, C.spt
    q_att = ys[:, 0:2, :]
    k_att = A.bf16(NE)
    v_att = A.bf16(18 * 130).rearrange("p (n c) -> p n c", n=18)
    nsink = A.f32(4)
    amark = A.off
    rC = A.f32(NE)
    rS = A.f32(NE)
    P.op("sp", lambda e: e.dma_start(out=rC, in_=C.rope_d[0]), r=["AE"], w=["rC"], dma="rC")
    P.op("sp", lambda e: e.dma_start(out=rS, in_=C.rope_d[1]), r=["AE"], w=["rS"], dma="rS")
    raw = [A.bf16(TL), A.bf16(TL)]
    r1 = [A.f32(TL), A.f32(TL)]
    r2 = [A.f32(TL), A.f32(TL)]
    P.op("dve", lambda e: e.tensor_scalar(out=nsink, in0=spt[:, SINK:SINK + 4], scalar1=-1.0, scalar2=None, op0=ALU.mult), r=["spt", "AE"], w=["nsink"])
    P.op("pool", lambda e: e.memset(v_att.rearrange("p n c -> p (n c)"), 1.0), r=["AE"], w=["vones"])
    cnt = [0]

    def rope_proj(wt, wkey, rng, dst, dkey, scale):
        rhs_fn, eoff, ntok, keyf = rng
        i = cnt[0] % 2
        cnt[0] += 1
        bk, bkey = banks[i], "b%d" % i
        bp, bpkey = banks[2 + i], "b%d" % (2 + i)
        proj_ext(C, wt, wkey, rng, bk, bkey)
        P.op("act", lambda e: e.activation(out=raw[i][:, 0:ntok], in_=bk[:, 0:ntok], func=AF.Copy, scale=scale), r=[bkey, "AE"], w=["raw%d" % i])
        P.op("pe", lambda e: e.matmul(bp[:, 0:ntok], lhsT=cbf[:, PMAT:PMAT + 128], rhs=raw[i][:, 0:ntok], start=True, stop=True),
             r=["raw%d" % i, "cbf"], w=[bpkey])
        P.op("dve", lambda e: e.tensor_tensor(out=r1[i][:, 0:ntok], in0=raw[i][:, 0:ntok], in1=rC[:, eoff:eoff + ntok], op=ALU.mult),
             r=["raw%d" % i, "rC", "AE"], w=["r1%d" % i])
        P.op("dve", lambda e: e.tensor_tensor(out=r2[i][:, 0:ntok], in0=bp[:, 0:ntok], in1=rS[:, eoff:eoff + ntok], op=ALU.mult),
             r=[bpkey, "rS", "AE"], w=["r2%d" % i])
        P.op("pool", lambda e: e.tensor_tensor(out=dst, in0=r1[i][:, 0:ntok], in1=r2[i][:, 0:ntok], op=ALU.add),
             r=["r1%d" % i, "r2%d" % i, "AE"], w=[dkey])

    for gq in range(2):
        wt, wkey, wb_ = W.get()
        for t in range(NTL):
            rope_proj(wt, wkey, C.ext_ranges[t], q_att[:, gq, t * TL:(t + 1) * TL], "qa:%d:%d" % (gq, t), 0.125)
        W.done(wb_)
    wt, wkey, wb_ = W.get()
    for ri, rng in enumerate(C.ext_ranges):
        rope_proj(wt, wkey, rng, k_att[:, rng[1]:rng[1] + rng[2]], "ka:%d" % ri, 1.0)
    W.done(wb_)
    wt, wkey, wb_ = W.get()
    for eb in range(18):
        if eb == 0:
            lf, kf = (lambda kc: C.xnh[:, kc, 0:HB]), (lambda kc: "xnh:%d" % kc)
        elif eb == 17:
            lf, kf = (lambda kc: C.xnh[:, kc, HB:2 * HB]), (lambda kc: "xnh:%d" % kc)
        else:
            lf, kf = (lambda kc, eb=eb: C.xn[:, kc, (eb - 1) * 128:eb * 128]), (lambda kc, eb=eb: "xn:%d:%d" % (kc, (eb - 1) // 4))
        bk, bkey = banks[4 + eb % 2], "b%d" % (4 + eb % 2)
        for kc in range(KC):
            P.op("pe", lambda e, kc=kc, lf=lf, bk=bk: e.matmul(bk[:, 0:128], lhsT=lf(kc), rhs=wt[:, kc * 128:(kc + 1) * 128],
                                                                start=(kc == 0), stop=(kc == KC - 1)),
                 r=[wkey, kf(kc)], w=[bkey])
        P.op("act", lambda e, eb=eb, bk=bk: e.activation(out=v_att[:, eb, :].rearrange("p (g c) -> p g c", g=2)[:, :, 0:64],
                                                          in_=bk[:, 0:128].rearrange("p (g c) -> p g c", g=2), func=AF.Copy),
             r=[bkey, "vones", "AE"], w=["va:%d" % eb])
    W.done(wb_)
    barrier(C)
    A.off = amark
    mx = [A.f32(4), A.f32(4)]
    negm = [A.f32(4), A.f32(4)]
    es = [A.f32(4), A.f32(4)]
    den = A.f32(4)
    Pb = [[A.bf16(768), A.bf16(768)], [A.bf16(768), A.bf16(768)]]
    Pm = [[A.bf16(768), A.bf16(768)], [A.bf16(768), A.bf16(768)]]
    PTs = [A.bf16(768), A.bf16(768)]
    on = [A.bf16(256), A.bf16(256)]
    wkeys = ["ka:%d" % ri for ri in range(6)]

    def stage_a(n):
        pb = n % 2
        v = 0 if n == 0 else (2 if n == NB - 1 else 1)
        msk = C.amb[:, v * 384:(v + 1) * 384]
        qs = slice(n * 128, (n + 1) * 128)
        win = slice(n * 128, n * 128 + 384)
        for k in range(2):
            pr = slice(64 * k, 64 * k + 64)
            for g_ in range(2):
                bk, bkey = banks[2 * k + g_], "b%d" % (2 * k + g_)
                P.op("pe", lambda e, bk=bk, g_=g_, pr=pr: e.matmul(bk[:, 0:384], lhsT=q_att[pr, g_, qs], rhs=k_att[pr, win], start=True, stop=True),
                     r=["qa:%d:%d" % (g_, n // 4)] + wkeys, w=[bkey])
        for h in range(4):
            P.op("dve", lambda e, h=h: e.reduce_max(out=mx[pb][:, h:h + 1], in_=banks[h][:, 0:384], axis=AX.X), r=["b%d" % h, "AE"], w=["mx%d:%d" % (pb, h)])
        P.op("dve", lambda e: e.scalar_tensor_tensor(out=negm[pb], in0=mx[pb], scalar=-1.0, in1=nsink, op0=ALU.mult, op1=ALU.min),
             r=["mx%d:%d" % (pb, h) for h in range(4)] + ["nsink"], w=["negm%d" % pb])

    def stage_a2(n):
        pb = n % 2
        v = 0 if n == 0 else (2 if n == NB - 1 else 1)
        msk = C.amb[:, v * 384:(v + 1) * 384]
        for k in range(2):
            for g_ in range(2):
                h = 2 * k + g_
                P.op("act", lambda e, g_=g_, h=h, k=k: e.activation(out=Pb[pb][k][:, g_ * 384:(g_ + 1) * 384], in_=banks[h][:, 0:384], func=AF.Exp, bias=negm[pb][:, h:h + 1]),
                     r=["b%d" % h, "negm%d" % pb, "AE"], w=["Pb%d:%d:%d" % (pb, k, g_)])
            P.op("pool", lambda e, k=k: e.tensor_tensor(out=Pm[pb][k].rearrange("p (g s) -> p g s", g=2), in0=Pb[pb][k].rearrange("p (g s) -> p g s", g=2),
                                                        in1=msk.unsqueeze(1).to_broadcast([128, 2, 384]), op=ALU.mult),
                 r=["Pb%d:%d:0" % (pb, k), "Pb%d:%d:1" % (pb, k), "amb", "AE"], w=["Pm%d:%d" % (pb, k)])
        P.op("dve", lambda e: e.tensor_tensor(out=es[pb], in0=negm[pb], in1=spt[:, SINK:SINK + 4], op=ALU.add), r=["negm%d" % pb, "spt", "AE"], w=["es%d" % pb])
        P.op("act", lambda e: e.activation(out=es[pb], in_=es[pb], func=AF.Exp), r=["es%d" % pb], w=["es%d" % pb])

    def stage_b(n):
        pb = n % 2
        qs = slice(n * 128, (n + 1) * 128)
        for k in range(2):
            bt = bfview(banks[4 + k])
            btkey = "b%d" % (4 + k)
            for j in range(6):
                P.op("pe", lambda e, j=j, k=k, bt=bt: e.transpose(out=bt[:, j * 128:(j + 1) * 128], in_=Pm[pb][k][:, j * 128:(j + 1) * 128], identity=cbf[:, IDENT:IDENT + 128]),
                     r=["Pm%d:%d" % (pb, k), "cbf"], w=[btkey])
            P.op("act", lambda e, k=k, bt=bt: e.activation(out=PTs[k], in_=bt[:, 0:768], func=AF.Copy), r=[btkey, "AE"], w=["PTs%d" % k])

    def stage_b2(n):
        pb = n % 2
        qs = slice(n * 128, (n + 1) * 128)
        for k in range(2):
            for g_ in range(2):
                h = 2 * k + g_
                for w_ in range(3):
                    P.op("pe", lambda e, g_=g_, h=h, w_=w_, k=k: e.matmul(banks[6][:, h * 65:(h + 1) * 65], lhsT=PTs[k][:, (g_ * 3 + w_) * 128:(g_ * 3 + w_ + 1) * 128],
                                                                          rhs=v_att[:, n + w_, k * 65:(k + 1) * 65], start=(w_ == 0), stop=(w_ == 2)),
                         r=["PTs%d" % k] + ["va:%d" % (n + w_)], w=["b6"])
        b6v = banks[6][:, 0:260].rearrange("p (h c) -> p h c", h=4)
        P.op("dve", lambda e: e.tensor_tensor(out=den, in0=b6v[:, :, 64], in1=es[pb], op=ALU.add), r=["b6", "es%d" % pb, "AE"], w=["den"])
        P.op("dve", lambda e: e.reciprocal(out=den, in_=den), r=["den"], w=["den"])
        P.op("dve", lambda e: e.tensor_tensor(out=on[pb].rearrange("p (h c) -> p h c", h=4), in0=b6v[:, :, 0:64],
                                              in1=den.unsqueeze(2).to_broadcast([128, 4, 64]), op=ALU.mult), r=["b6", "den", "AE"], w=["on%d" % pb])
        b7 = bfview(banks[7])
        for c in range(2):
            P.op("pe", lambda e, c=c: e.transpose(out=b7[:, c * 128:(c + 1) * 128], in_=on[pb][:, c * 128:(c + 1) * 128], identity=cbf[:, IDENT:IDENT + 128]),
                 r=["on%d" % pb, "cbf"], w=["b7"])
        P.op("act", lambda e: e.activation(out=ys[:, 4:6, qs], in_=b7[:, 0:256].rearrange("p (c t) -> p c t", c=2), func=AF.Copy),
             r=["b7", "AE"], w=["ys:4:%d" % (n // 4), "ys:5:%d" % (n // 4)])

    stage_a(0)
    stage_a2(0)
    for n in range(NB):
        if n + 1 < NB:
            stage_a(n + 1)
        stage_b(n)
        if n + 1 < NB:
            stage_a2(n + 1)
        stage_b2(n)
    barrier(C)
    A.reset()


def conv_phase(C):
    P, W, banks, A, ys = C.P, C.W, C.banks, C.A, C.ys
    cbf, spt = C.cbf, C.spt
    u_ext = A.bf16(2 * NE).rearrange("p (c t) -> p c t", c=2)
    Dm = A.bf16(2 * 31 * 128).rearrange("p (c k j) -> p c k j", c=2, k=31)
    sg = [A.f32(TL), A.f32(TL)]
    for c in range(2):
        P.op("pool", lambda e, c=c: e.tensor_tensor(out=Dm[:, c, :, :], in0=cbf[:, IDENT:IDENT + 128].unsqueeze(1).to_broadcast([128, 31, 128]),
                                                     in1=spt[:, CONVW + 31 * c:CONVW + 31 * (c + 1)].unsqueeze(2).to_broadcast([128, 31, 128]), op=ALU.mult),
             r=["cbf", "spt", "AE"], w=["Dm%d" % c])
    i = 0
    for c in range(2):
        wa, ka, ba = W.get()
        wg, kg, bg = W.get()
        for ri, rng in enumerate(C.ext_ranges):
            eoff, ntok = rng[1], rng[2]
            j = i % 2
            i += 1
            proj_ext(C, wa, ka, rng, banks[j], "b%d" % j)
            proj_ext(C, wg, kg, rng, banks[2 + j], "b%d" % (2 + j))
            P.op("act", lambda e, j=j, ntok=ntok: e.activation(out=sg[j][:, 0:ntok], in_=banks[2 + j][:, 0:ntok], func=AF.Sigmoid), r=["b%d" % (2 + j), "AE"], w=["sg%d" % j])
            P.op("dve", lambda e, j=j, c=c, eoff=eoff, ntok=ntok: e.tensor_tensor(out=u_ext[:, c, eoff:eoff + ntok], in0=banks[j][:, 0:ntok], in1=sg[j][:, 0:ntok], op=ALU.mult),
                 r=["b%d" % j, "sg%d" % j, "AE"], w=["u:%d:%d" % (c, ri)])
        W.done(ba)
        W.done(bg)
    ysb = A.f32(2 * TL).rearrange("p (c t) -> p c t", c=2)
    ybf = A.bf16(2 * TL).rearrange("p (c t) -> p c t", c=2)
    ysq = A.bf16(2 * TL).rearrange("p (c t) -> p c t", c=2)
    mean = A.f32(TL)
    var = A.f32(TL)
    dd = A.f32(TL)
    msq = dd
    for t in range(NTL):
        for c in range(2):
            bk, bkey = banks[4 + c], "b%d" % (4 + c)
            for k in range(31):
                s0 = HB + t * TL + k - 15
                P.op("pe", lambda e, c=c, k=k, s0=s0, bk=bk: e.matmul(bk[:, :], lhsT=Dm[:, c, k, :], rhs=u_ext[:, c, s0:s0 + TL], start=(k == 0), stop=(k == 30)),
                     r=["Dm%d" % c] + ["u:%d:%d" % (c, ri) for ri in range(6)], w=[bkey])
            P.op("act", lambda e, c=c, bk=bk: e.activation(out=ysb[:, c, :], in_=bk[:, :], func=AF.Identity, bias=spt[:, CONVB + c:CONVB + c + 1]),
                 r=[bkey, "spt", "AE"], w=["ysb%d" % c])
            P.op("act", lambda e, c=c, bk=bk: e.activation(out=ysq[:, c, :], in_=bk[:, :], func=AF.Square, bias=spt[:, CONVB + c:CONVB + c + 1]),
                 r=[bkey, "spt", "AE"], w=["ysq%d" % c])
            P.op("pool", lambda e, c=c: e.tensor_copy(out=ybf[:, c, :], in_=ysb[:, c, :]), r=["ysb%d" % c, "AE"], w=["ybf%d" % c])
        for c in range(2):
            P.op("pe", lambda e, c=c: e.matmul(banks[6][:, :], lhsT=cbf[:, ONES:ONES + 128], rhs=ybf[:, c, :], start=(c == 0), stop=(c == 1)), r=["ybf%d" % c, "cbf"], w=["b6"])
        for c in range(2):
            P.op("pe", lambda e, c=c: e.matmul(banks[7][:, :], lhsT=cbf[:, ONES:ONES + 128], rhs=ysq[:, c, :], start=(c == 0), stop=(c == 1)), r=["ysq%d" % c, "cbf"], w=["b7"])
        P.op("dve", lambda e: e.tensor_scalar(out=mean, in0=banks[6][:, :], scalar1=1.0 / 256, scalar2=None, op0=ALU.mult), r=["b6", "AE"], w=["mean"])
        P.op("dve", lambda e: e.tensor_tensor(out=msq, in0=mean, in1=mean, op=ALU.mult), r=["mean", "AE"], w=["dd"])
        P.op("dve", lambda e: e.scalar_tensor_tensor(out=var, in0=banks[7][:, :], scalar=1.0 / 256, in1=msq, op0=ALU.mult, op1=ALU.subtract), r=["b7", "dd", "AE"], w=["var"])
        P.op("act", lambda e: e.activation(out=var, in_=var, func=AF.Sqrt, bias=EPS), r=["var"], w=["var"])
        P.op("dve", lambda e: e.reciprocal(out=var, in_=var), r=["var"], w=["var"])
        for c in range(2):
            P.op("dve", lambda e, c=c: e.tensor_tensor(out=dd, in0=ysb[:, c, :], in1=mean, op=ALU.subtract), r=["ysb%d" % c, "mean", "AE"], w=["dd"])
            P.op("dve", lambda e: e.tensor_tensor(out=dd, in0=dd, in1=var, op=ALU.mult), r=["dd", "var"], w=["dd"])
            P.op("act", lambda e, c=c, t=t: e.activation(out=ys[:, c, t * TL:(t + 1) * TL], in_=dd, func=AF.Silu, scale=spt[:, LNG + c:LNG + c + 1], bias=spt[:, LNB + c:LNB + c + 1]),
                 r=["dd", "spt", "AE"], w=["ys:%d:%d" % (c, t)])
    barrier(C)
    A.reset()


def pool_phase(C):
    P, W, banks, A, ys = C.P, C.W, C.banks, C.A, C.ys
    spt = C.spt
    pin = A.f32(2 * NE).rearrange("p (c t) -> p c t", c=2)
    B1 = A.f32(NE)
    B2 = A.f32(NE)
    dbf = [A.bf16(TL), A.bf16(TL)]
    pwb = A.bf16(256)
    P.op("dve", lambda e: e.tensor_copy(out=pwb, in_=spt[:, POOLW:POOLW + 256]), r=["spt", "AE"], w=["pwb"])
    i = 0
    for c in range(2):
        wt, wkey, wb_ = W.get()
        for ri, rng in enumerate(C.ext_ranges):
            eoff, ntok = rng[1], rng[2]
            j = i % 2
            i += 1
            proj_ext(C, wt, wkey, rng, banks[j], "b%d" % j)
            P.op("act", lambda e, j=j, c=c, eoff=eoff, ntok=ntok: e.activation(out=pin[:, c, eoff:eoff + ntok], in_=banks[j][:, 0:ntok], func=AF.Copy),
                 r=["b%d" % j, "AE"], w=["pin:%d:%d" % (c, ri)])
        W.done(wb_)
    Wd = NE
    invw = C.CF(INVW, 2)
    corr = C.PCF(PCORR, 32).rearrange("p (c j) -> p c j", c=2)
    for c in range(2):
        u = pin[:, c, :]
        pk = ["pin:%d:%d" % (c, ri) for ri in range(6)]
        P.op("pool", lambda e, u=u: e.tensor_tensor(out=B1[:, 1:Wd], in0=u[:, 0:Wd - 1], in1=u[:, 1:Wd], op=ALU.add), r=pk + ["AE"], w=["B1"])
        P.op("pool", lambda e: e.tensor_tensor(out=B2[:, 2:Wd - 1], in0=B1[:, 1:Wd - 2], in1=B1[:, 3:Wd], op=ALU.add), r=["B1", "AE"], w=["B2"])
        if c == 1:
            P.op("pool", lambda e: e.tensor_tensor(out=B1[:, 4:Wd - 3], in0=B2[:, 2:Wd - 5], in1=B2[:, 6:Wd - 1], op=ALU.add), r=["B2"], w=["B1"])
            P.op("pool", lambda e: e.tensor_tensor(out=B2[:, 8:Wd - 7], in0=B1[:, 4:Wd - 11], in1=B1[:, 12:Wd - 3], op=ALU.add), r=["B1"], w=["B2"])
        for (Bx, bkey, pr) in ((B1, "B1", slice(0, 64)), (B2, "B2", slice(64, 128))):
            P.op("dve", lambda e, Bx=Bx, pr=pr, c=c: e.tensor_tensor(out=Bx[pr, HB:HB + 8], in0=Bx[pr, HB:HB + 8], in1=corr[pr, c, 0:8], op=ALU.mult), r=[bkey, "pcf"], w=[bkey])
            P.op("dve", lambda e, Bx=Bx, pr=pr, c=c: e.tensor_tensor(out=Bx[pr, HB + NT - 8:HB + NT], in0=Bx[pr, HB + NT - 8:HB + NT], in1=corr[pr, c, 8:16], op=ALU.mult), r=[bkey, "pcf"], w=[bkey])
        for t in range(NTL):
            j = t % 2
            es = slice(HB + t * TL, HB + (t + 1) * TL)
            for (Bx, bkey, pr) in ((B1, "B1", slice(0, 64)), (B2, "B2", slice(64, 128))):
                P.op("dve", lambda e, Bx=Bx, pr=pr, c=c, j=j, es=es: e.scalar_tensor_tensor(out=dbf[j][pr, :], in0=Bx[pr, es], scalar=invw[pr, c:c + 1], in1=pin[pr, c, es],
                                                                                             op0=ALU.mult, op1=ALU.subtract),
                     r=[bkey, "cf"] + pk + ["AE"], w=["dbf%d:%d" % (j, pr.start)])
            bk, bkey2 = banks[2 + j], "b%d" % (2 + j)
            P.op("pe", lambda e, c=c, j=j, bk=bk: e.matmul(bk[:, :], lhsT=pwb[:, c * 128:(c + 1) * 128], rhs=dbf[j], start=True, stop=True),
                 r=["dbf%d:0" % j, "dbf%d:64" % j, "pwb"], w=[bkey2])
            P.op("act", lambda e, c=c, t=t, bk=bk: e.activation(out=ys[:, 6 + c, t * TL:(t + 1) * TL], in_=bk[:, :], func=AF.Identity, scale=spt[:, PSC + c:PSC + c + 1]),
                 r=[bkey2, "spt", "AE"], w=["ys:%d:%d" % (6 + c, t)])
    barrier(C)
    A.reset()


def merge_phase(C):
    P, W, banks, A, ys, xn, xT = C.P, C.W, C.banks, C.A, C.ys, C.xn, C.xT
    merged = A.bf16(KC * NT).rearrange("p (k t) -> p k t", k=KC)
    acc = A.f32(NTL * TL).rearrange("p (a t) -> p a t", a=NTL)
    tmp = [A.f32(TL), A.f32(TL)]
    sgv = C.xnh[:, :, :].rearrange("p k t -> p (k t)").bitcast(F32)
    sg = [sgv[:, 0:TL], sgv[:, TL:2 * TL]]
    i = 0
    for f in range(KC):
        wb, kb, bb = W.get()
        wb4 = wb[:, :].rearrange("p (n k j) -> p n k j", n=4, k=2)
        for n in range(4):
            wg, kg, bg = W.get()
            for t in range(NTL):
                j = i % 2
                i += 1
                G, gk = banks[j], "b%d" % j
                Pj, pk = banks[2 + j], "b%d" % (2 + j)
                for kc in range(KC):
                    P.op("pe", lambda e, kc=kc, t=t, wg=wg, G=G: e.matmul(G[:, :], lhsT=wg[:, kc * 128:(kc + 1) * 128], rhs=xn[:, kc, t * TL:(t + 1) * TL],
                                                                          start=(kc == 0), stop=(kc == KC - 1)), r=[kg, "xn:%d:%d" % (kc, t)], w=[gk])
                for k2 in range(2):
                    P.op("pe", lambda e, k2=k2, t=t, n=n, Pj=Pj, wb4=wb4: e.matmul(Pj[:, :], lhsT=wb4[:, n, k2, :], rhs=ys[:, 2 * n + k2, t * TL:(t + 1) * TL],
                                                                                   start=(k2 == 0), stop=(k2 == 1)), r=[kb, "ys:%d:%d" % (2 * n + k2, t)], w=[pk])
                P.op("act", lambda e, j=j, G=G: e.activation(out=sg[j], in_=G[:, :], func=AF.Sigmoid), r=[gk, "xnhfree"], w=["sgm%d" % j])
                if n == 0:
                    P.op("dve", lambda e, j=j, t=t, Pj=Pj: e.tensor_tensor(out=acc[:, t, :], in0=Pj[:, :], in1=sg[j], op=ALU.mult), r=[pk, "sgm%d" % j, "AE"], w=["acc%d" % t])
                else:
                    P.op("dve", lambda e, j=j, Pj=Pj: e.tensor_tensor(out=tmp[j], in0=Pj[:, :], in1=sg[j], op=ALU.mult), r=[pk, "sgm%d" % j, "AE"], w=["tmp%d" % j])
                    if n < 3:
                        P.op("pool", lambda e, j=j, t=t: e.tensor_tensor(out=acc[:, t, :], in0=acc[:, t, :], in1=tmp[j], op=ALU.add), r=["acc%d" % t, "tmp%d" % j], w=["acc%d" % t])
                    else:
                        P.op("pool", lambda e, j=j, t=t, f=f: e.tensor_tensor(out=merged[:, f, t * TL:(t + 1) * TL], in0=acc[:, t, :], in1=tmp[j], op=ALU.add),
                             r=["acc%d" % t, "tmp%d" % j], w=["mg:%d:%d" % (f, t)])
            W.done(bg)
        W.done(bb)
    i = 0
    for f in range(KC):
        wo, ko, bo = W.get()
        for t in range(NTL):
            j = i % 4
            i += 1
            bk, bkey = banks[4 + j], "b%d" % (4 + j)
            for kc in range(KC):
                P.op("pe", lambda e, kc=kc, t=t, bk=bk, wo=wo: e.matmul(bk[:, :], lhsT=wo[:, kc * 128:(kc + 1) * 128], rhs=merged[:, kc, t * TL:(t + 1) * TL],
                                                                        start=(kc == 0), stop=(kc == KC - 1)), r=[ko, "mg:%d:%d" % (kc, t)], w=[bkey])
            P.op("dve", lambda e, f=f, t=t, bk=bk: e.tensor_tensor(out=xT[:, f, t * TL:(t + 1) * TL], in0=bk[:, :], in1=xT[:, f, t * TL:(t + 1) * TL], op=ALU.add),
                 r=[bkey, "x:%d:%d" % (f, t)], w=["x:%d:%d" % (f, t)])
        W.done(bo)
    barrier(C)
    A.reset()


def ffn_phase(C):
    P, W, banks, A, ys, xn, xT = C.P, C.W, C.banks, C.A, C.ys, C.xn, C.xT
    arn = dict(sq=A.bf16(KC * TL), rs=A.f32(TL))
    for t in range(NTL):
        rms_tile(C, lambda kc, t=t: xT[:, kc, t * TL:(t + 1) * TL], FFNG, lambda kc, t=t: xn[:, kc, t * TL:(t + 1) * TL], TL,
                 ["x:%d:%d" % (kc, t) for kc in range(KC)], ["xn:%d:%d" % (kc, t) for kc in range(KC)], arn, banks[3], "b3", "f%d" % t)
    rl = [A.f32(TL), A.f32(TL)]
    act = ys
    i = 0
    for g in range(4):
        for c in range(KC):
            wu, ku, bu = W.get()
            for t in range(NTL):
                j = i % 2
                i += 1
                bk, bkey = banks[j], "b%d" % j
                for kc in range(KC):
                    P.op("pe", lambda e, kc=kc, t=t, bk=bk, wu=wu: e.matmul(bk[:, :], lhsT=wu[:, kc * 128:(kc + 1) * 128], rhs=xn[:, kc, t * TL:(t + 1) * TL],
                                                                            start=(kc == 0), stop=(kc == KC - 1)), r=[ku, "xn:%d:%d" % (kc, t)], w=[bkey])
                P.op("act", lambda e, j=j, bk=bk: e.activation(out=rl[j], in_=bk[:, :], func=AF.Relu), r=[bkey, "AE"], w=["rl%d" % j])
                P.op("dve", lambda e, j=j, c=c, t=t: e.tensor_tensor(out=act[:, c, t * TL:(t + 1) * TL], in0=rl[j], in1=rl[j], op=ALU.mult),
                     r=["rl%d" % j, "AE"], w=["ys:%d:%d" % (c, t)])
            W.done(bu)
        for f in range(KC):
            wd, kd, bd = W.get()
            for t in range(NTL):
                j = i % 4
                i += 1
                bk, bkey = banks[4 + j], "b%d" % (4 + j)
                for kc in range(KC):
                    P.op("pe", lambda e, kc=kc, t=t, bk=bk, wd=wd: e.matmul(bk[:, :], lhsT=wd[:, kc * 128:(kc + 1) * 128], rhs=act[:, kc, t * TL:(t + 1) * TL],
                                                                            start=(kc == 0), stop=(kc == KC - 1)), r=[kd, "ys:%d:%d" % (kc, t)], w=[bkey])
                P.op("dve", lambda e, f=f, t=t, bk=bk: e.tensor_tensor(out=xT[:, f, t * TL:(t + 1) * TL], in0=bk[:, :], in1=xT[:, f, t * TL:(t + 1) * TL], op=ALU.add),
                     r=[bkey, "x:%d:%d" % (f, t)], w=["x:%d:%d" % (f, t)])
            W.done(bd)
    barrier(C)
    A.reset()


def output_phase(C):
    P, banks, A, xT, spt = C.P, C.banks, C.A, C.xT, C.spt
    fins = []
    for kc in range(KC):
        fins.append(P.op("sp", lambda e, kc=kc: e.dma_start(out=C.xout[kc * 128:(kc + 1) * 128, :], in_=xT[:, kc, :]),
                         r=["x:%d:%d" % (kc, t) for t in range(NTL)], dma="xo%d" % kc))
    sq = A.bf16(KC * TL).rearrange("p (k t) -> p k t", k=KC)
    rs = A.f32(TL)
    yo = [A.f32(TL), A.f32(TL), A.f32(TL), A.f32(TL)]
    i = 0
    for t in range(NTL):
        ts = slice(t * TL, (t + 1) * TL)
        for kc in range(KC):
            P.op("act", lambda e, kc=kc, ts=ts: e.activation(out=sq[:, kc, :], in_=xT[:, kc, ts], func=AF.Square), r=["x:%d:%d" % (kc, t), "AE"], w=["sq%d" % kc])
        for kc in range(KC):
            P.op("pe", lambda e, kc=kc: e.matmul(banks[3][:, :], lhsT=C.cbf[:, ONES:ONES + 128], rhs=sq[:, kc, :], start=(kc == 0), stop=(kc == KC - 1)),
                 r=["sq%d" % kc, "cbf"], w=["b3"])
        P.op("act", lambda e: e.activation(out=rs, in_=banks[3][:, :], func=AF.Sqrt, bias=EPS, scale=1.0 / D), r=["b3", "AE"], w=["rs"])
        P.op("dve", lambda e: e.reciprocal(out=rs, in_=rs), r=["rs"], w=["rs"])
        for kc in range(KC):
            j = i % 4
            i += 1
            P.op("dve", lambda e, kc=kc, ts=ts, j=j: e.scalar_tensor_tensor(out=yo[j], in0=xT[:, kc, ts], scalar=spt[:, FING + kc:FING + kc + 1], in1=rs, op0=ALU.mult, op1=ALU.mult),
                 r=["x:%d:%d" % (kc, t), "rs", "spt", "AE"], w=["yo%d" % j])
            fins.append(P.op("sp", lambda e, kc=kc, ts=ts, j=j: e.dma_start(out=C.yout[kc * 128:(kc + 1) * 128, ts], in_=yo[j]), r=["yo%d" % j], dma="yo%d" % j))
    return fins


def build_layer_full(debug=False, upto=99):
    nc, st, P, C = build_layer(debug)
    if upto >= 2:
        attention_phase(C)
    if upto >= 3:
        conv_phase(C)
    if upto >= 4:
        pool_phase(C)
    fins = []
    if debug:
        fins.append(P.op("sp", lambda e: e.dma_start(out=C.dbg[:, :], in_=C.ys[:, :, :].rearrange("p k t -> p (k t)")),
                         r=["ys:%d:%d" % (k, t) for k in range(KC) for t in range(NTL)] + ["AE"], dma="dbg"))
        barrier(C)
    if upto >= 5:
        P.op("pool", lambda e: e.memset(C.scr[:, 0:8], 0.0), w=["xnh:%d" % kc for kc in range(KC)] + ["xnhfree"])
        merge_phase(C)
    if upto >= 6:
        ffn_phase(C)
    fins += output_phase(C)
    P.emit(final_waits=fins)
    return nc, st


_CACHE = {}


def _prog(name):
    if name not in _CACHE:
        if name == "sum":
            _CACHE[name] = build_sum()
        else:
            _CACHE[name] = build_layer_full(debug=False)
    return _CACHE[name][0]


def _ext_from_segments(segs, c):
    p = c % 4
    left = segs[c - 1][:, NT - HB:] if p > 0 else np.zeros((D, HB), np.float32)
    right = segs[c + 1][:, :HB] if p < 3 else np.zeros((D, HB), np.float32)
    return np.ascontiguousarray(np.concatenate([left, segs[c], right], axis=1))


def kernel(**inputs):
    inp = {k: np.asarray(v, dtype=np.float32) for k, v in inputs.items()}
    x = inp["x"]
    B, T, _ = x.shape
    cores = list(range(8))
    cst = const_block()
    pcs = [percore_consts(c) for c in cores]
    ropes = [rope_tables(c) for c in cores]
    segs = [np.ascontiguousarray(x[c // 4, (c % 4) * NT:(c % 4 + 1) * NT, :].T) for c in cores]
    y = None
    for l in range(2):
        sp = small_params(inp, l)
        ws_s = sum_stream(inp, l)
        res = run_bass_kernel_spmd(_prog("sum"), [{"xT": segs[c], "ws": ws_s, "sp": sp, "cst": cst} for c in cores], core_ids=cores)
        sums = [np.asarray(res.results[c]["so"]) for c in cores]
        ws_l = layer_stream(inp, l)
        in_maps = []
        for c in cores:
            b = c // 4
            gs = np.ascontiguousarray(np.stack([sums[b * 4 + r] for r in range(4)], 1).reshape(128, 4 * 130))
            in_maps.append({"xT": _ext_from_segments(segs, c), "ws": ws_l, "sp": sp, "cst": cst, "pc": pcs[c], "rope": ropes[c], "gsum": gs})
        res = run_bass_kernel_spmd(_prog("layer"), in_maps, core_ids=cores)
        segs = [np.asarray(res.results[c]["xo"]) for c in cores]
        y = [np.asarray(res.results[c]["yo"]) for c in cores]
    out = np.empty((B, T, D), np.float32)
    for c in cores:
        out[c // 4, (c % 4) * NT:(c % 4 + 1) * NT, :] = y[c].T
    return out
```

```python
import math
import numpy as np
import concourse.bass as bass
import concourse.mybir as mybir
from concourse.bass_utils import run_bass_kernel_spmd
from contextlib import ExitStack

F32 = mybir.dt.float32
BF16 = mybir.dt.bfloat16
ALU = mybir.AluOpType
AF = mybir.ActivationFunctionType
AX = mybir.AxisListType

D = 1024
KC = 8
NT = 2048
TL = 512
NTL = 4
HB = 128
NE = NT + 2 * HB
NB = 16
SEQ = 8192
EPS = 1e-6
LNC = -0.5 * math.log(32.0)

MIXG, FFNG, FING, CONVB, LNG, LNB, GLAG, SINK, PSC, CONVW, WUP, POOLW, NSP = 0, 8, 16, 24, 26, 28, 30, 32, 36, 38, 100, 356, 612
IDENT, ONES, ONESBLK, PMAT, TRIF, TRIB, NCBF = 0, 128, 256, 384, 512, 640, 768
BDM, HEADM, INVW, SCANM, NCST = 768, 1024, 1028, 1032, 1544
AMASK, PCORR, COEFF, COEFB, NPC = 0, 1152, 1184, 1196, 1208

IN_OFF = dict(glu_a=0, glu_g=256, gq=512, gk=640, gv=768, gr=1024, glr=1280, aq=1312, ak=1568, av=1696, pin=1824, gate=2080)


class Prog:
    ENGS = ("pe", "act", "dve", "pool", "sp")

    def __init__(self, nc, stack):
        self.nc = nc
        self.stack = stack
        self.ops = []
        self.last_w = {}
        self.readers = {}
        self.sems = {}
        self.semcnt = {}
        self.nsb = 0

    def sem(self, key):
        if key not in self.sems:
            self.sems[key] = self.stack.enter_context(self.nc.semaphore("s%d" % len(self.sems)))
            self.semcnt[key] = 0
        return self.sems[key]

    def sb(self, shape, dt, name=None):
        self.nsb += 1
        return self.stack.enter_context(self.nc.sbuf_tensor("s_" + (name or ("sb%d" % self.nsb)), list(shape), dt))

    def ps(self, shape, dt=F32, name=None):
        self.nsb += 1
        return self.stack.enter_context(self.nc.psum_tensor(name or ("ps%d" % self.nsb), list(shape), dt))

    def op(self, eng, fn, r=(), w=(), dma=None):
        idx = len(self.ops)
        deps = set()
        for k in r:
            x = self.last_w.get(k)
            if x is not None:
                deps.add(x)
        for k in w:
            x = self.last_w.get(k)
            if x is not None:
                deps.add(x)
            rl = self.readers.get(k)
            if rl:
                deps.update(rl)
        if dma is not None:
            skey = ("dma", dma)
            self.sem(skey)
            self.semcnt[skey] += 16
        else:
            skey = ("eng", eng)
            self.sem(skey)
            self.semcnt[skey] += 1
        done = (skey, self.semcnt[skey])
        self.ops.append((eng, fn, deps, done, dma is not None))
        for k in w:
            self.last_w[k] = idx
            self.readers[k] = []
        for k in r:
            self.readers.setdefault(k, []).append(idx)
        return idx

    def barrier(self, eng, fn, key):
        last = {}
        dmas = []
        start = getattr(self, "_bar_at", 0)
        for i in range(len(self.ops) - 1, -1, -1):
            o = self.ops[i]
            if o[4]:
                if i >= start:
                    dmas.append(i)
            elif o[0] not in last:
                last[o[0]] = i
            if i < start and len(last) >= 4:
                break
        idx = self.op(eng, fn, w=[key])
        e, f, deps, done, isd = self.ops[idx]
        deps.update(last.values())
        deps.update(dmas)
        deps.discard(idx)
        self._bar_at = idx
        return idx

    def emit(self, final_waits=()):
        nc = self.nc
        known = {e: {} for e in self.ENGS}
        per = {e: [] for e in self.ENGS}
        for (e, fn, deps, done, isdma) in self.ops:
            need = {}
            for d in deps:
                (de, _, _, (sk, val), ddma) = self.ops[d]
                if de == "pe" and e == "pe" and not ddma:
                    continue
                if need.get(sk, 0) < val:
                    need[sk] = val
            waits = []
            kn = known[e]
            for sk, val in need.items():
                if kn.get(sk, 0) >= val:
                    continue
                kn[sk] = val
                waits.append((sk, val))
            per[e].append((waits, fn, done, isdma))
        block = self.stack.enter_context(nc.Block())
        sems = self.sems

        def run(engobj, lst, extra=()):
            for waits, fn, (sk, val), isdma in lst:
                for wk, wv in waits:
                    engobj.wait_ge(sems[wk], wv)
                ins = fn(engobj)
                ins.then_inc(sems[sk], 16 if isdma else 1)
            for wk, wv in extra:
                engobj.wait_ge(sems[wk], wv)

        fin = [self.ops[i][3] for i in final_waits]

        @block.tensor
        def _(e):
            run(e, per["pe"])

        @block.scalar
        def _(e):
            run(e, per["act"])

        @block.vector
        def _(e):
            run(e, per["dve"])

        @block.gpsimd
        def _(e):
            run(e, per["pool"])

        @block.sync
        def _(e):
            run(e, per["sp"], fin)


class Arena:
    def __init__(self, t, nwords):
        self.t = t
        self.n = nwords
        self.off = 0

    def reset(self):
        self.off = 0

    def f32(self, n):
        n8 = (n + 7) // 8 * 8
        assert self.off + n8 <= self.n, ("arena overflow", self.off, n8, self.n)
        v = self.t[:, self.off:self.off + n]
        self.off += n8
        return v

    def bf16(self, n):
        w = (n + 1) // 2
        w = (w + 7) // 8 * 8
        assert self.off + w <= self.n, ("arena overflow", self.off, w, self.n)
        v = self.t[:, self.off:self.off + w].bitcast(BF16)[:, 0:n]
        self.off += w
        return v


class WStream:
    def __init__(self, P, dram, nch, wst, wbf, look=1):
        self.P, self.dram, self.nch, self.wst, self.wbf, self.look = P, dram, nch, wst, wbf, look
        self.issued = 0
        self.nxt = 0
        self.live = {}

    def _issue(self, i):
        s = i % len(self.wst)
        b = i % len(self.wbf)
        assert self.live.get(b) is None, ("weight slot still live", i, b, self.live.get(b))
        self.live[b] = i
        wst, wbf, dram = self.wst, self.wbf, self.dram
        self.P.op("sp", lambda e: e.dma_start(out=wst[s][:], in_=dram[i]), w=["wst%d" % s], dma="wst%d" % s)
        self.P.op("pool", lambda e: e.tensor_copy(out=wbf[b][:], in_=wst[s][:]), r=["wst%d" % s], w=["wbf%d" % b])

    def get(self):
        i = self.nxt
        self.nxt += 1
        while self.issued < min(self.nch, i + 1 + self.look):
            self._issue(self.issued)
            self.issued += 1
        b = i % len(self.wbf)
        return self.wbf[b], "wbf%d" % b, b

    def done(self, b):
        self.live[b] = None


class Ctx:
    pass


def setup_common(nc, P, C, nch, look=1, nbf=6):
    C.nc, C.P = nc, P
    C.ws_d = nc.dram_tensor("ws", [nch, 128, 1024], F32, kind="ExternalInput").ap()
    C.sp_d = nc.dram_tensor("sp", [128, NSP], F32, kind="ExternalInput").ap()
    C.cst_d = nc.dram_tensor("cst", [128, NCST], F32, kind="ExternalInput").ap()
    C.banks = [P.ps([128, 512], F32, name="bank%d" % i) for i in range(8)]
    C.wst = [P.sb([128, 1024], F32, name="wst%d" % i) for i in range(2)]
    C.wbf = [P.sb([128, 1024], BF16, name="wbf%d" % i) for i in range(nbf)]
    C.W = WStream(P, C.ws_d, nch, C.wst, C.wbf, look=look)
    C.spt = P.sb([128, NSP], F32, name="spt")
    C.cbf = P.sb([128, NCBF], BF16, name="cbf")
    C.cf = P.sb([128, NCST - NCBF], F32, name="cf")
    C.wupb = P.sb([128, 256], BF16, name="wupb")
    P.op("sp", lambda e: e.dma_start(out=C.spt[:], in_=C.sp_d[:, :]), w=["spt"], dma="spt")
    P.op("sp", lambda e: e.dma_start(out=C.cf[:], in_=C.cst_d[:, NCBF:NCST]), w=["cf"], dma="cf")
    P.op("pool", lambda e: e.tensor_copy(out=C.wupb[:], in_=C.spt[:, WUP:WUP + 256]), r=["spt"], w=["wupb"])


def bfview(bank):
    return bank[:].bitcast(BF16)


def rms_tile(C, xsrc, gcol, dst, ntok, rkeys, wkeys, ar, bank, bkey, tag):
    P = C.P
    sq = ar["sq"][:, 0:KC * ntok].rearrange("p (k t) -> p k t", k=KC)
    ar["cnt"] = ar.get("cnt", 0) + 1
    par = ar["cnt"] % 2
    rsl = ar["rs"] if isinstance(ar["rs"], list) else [ar["rs"]]
    rs = rsl[par % len(rsl)][:, 0:ntok]
    rskey = "rs%d" % (par % len(rsl))
    if isinstance(bank, list):
        bkey = bkey[par % len(bank)]
        bank = bank[par % len(bank)]
    for kc in range(KC):
        P.op("act", lambda e, kc=kc: e.activation(out=sq[:, kc, :], in_=xsrc(kc), func=AF.Square),
             r=rkeys + ["AE"], w=["sq%d" % kc])
    for kc in range(KC):
        P.op("pe", lambda e, kc=kc: e.matmul(bank[:, 0:ntok], lhsT=C.cbf[:, ONES:ONES + 128], rhs=sq[:, kc, :],
                                              start=(kc == 0), stop=(kc == KC - 1)),
             r=["sq%d" % kc, "cbf"], w=[bkey])
    P.op("act", lambda e: e.activation(out=rs, in_=bank[:, 0:ntok], func=AF.Sqrt, bias=EPS, scale=1.0 / D),
         r=[bkey, "AE"], w=[rskey])
    P.op("dve", lambda e: e.reciprocal(out=rs, in_=rs), r=[rskey], w=[rskey])
    for kc in range(KC):
        P.op("dve", lambda e, kc=kc: e.scalar_tensor_tensor(out=dst(kc), in0=xsrc(kc), scalar=C.spt[:, gcol + kc:gcol + kc + 1],
                                                             in1=rs, op0=ALU.mult, op1=ALU.mult),
             r=rkeys + [rskey, "spt"], w=[wkeys[kc]])


def barrier(C):
    C.P.barrier("pool", lambda e: e.memset(C.scr[:, 0:8], 0.0), "AE")


def gla_phase1(C, xn_tile, xn_blk, ar, with_q, kt_f, kt_b, qt_f, qt_b, v_sb, U, etf, etb):
    P, W, banks = C.P, C.W, C.banks
    w_lr, k_lr, b_lr = W.get()
    if with_q:
        w_q, k_q, b_q = W.get()
    w_k, k_k, b_k = W.get()
    glra = ar["glra"]
    t1 = [ar["t1f"], ar["t1b"]]
    t2 = [ar["t2f"], ar["t2b"]]
    t3 = [ar["t3f"], ar["t3b"]]
    t4 = [ar["t4f"], ar["t4b"]]
    t5 = ar["t5"]
    P.op("pool", lambda e: e.memset(glra[32:33, :], 1.0), r=["AE"], w=["glra1"])
    for t in range(NTL):
        sl = slice(t * TL, (t + 1) * TL)
        for kc in range(KC):
            P.op("pe", lambda e, kc=kc, t=t: e.matmul(banks[6][0:32, :], lhsT=w_lr[:, kc * 128:kc * 128 + 32], rhs=xn_tile(kc, t),
                                                       start=(kc == 0), stop=(kc == KC - 1)),
                 r=[k_lr, "xn:%d:%d" % (kc, t)], w=["b6"])
        P.op("act", lambda e: e.activation(out=glra[0:32, :], in_=banks[6][0:32, :], func=AF.Copy), r=["b6", "AE"], w=["glra"])
        for d in range(2):
            bk = banks[4 + d]
            P.op("pe", lambda e, d=d, bk=bk: e.matmul(bk[:, :], lhsT=C.wupb[0:33, d * 128:(d + 1) * 128], rhs=glra[0:33, :],
                                                       start=True, stop=True),
                 r=["glra", "glra1", "wupb"], w=["b%d" % (4 + d)])
            P.op("act", lambda e, d=d, bk=bk: e.activation(out=t1[d], in_=bk[:, :], func=AF.Exp, scale=-1.0),
                 r=["b%d" % (4 + d), "AE"], w=["t1%d" % d])
            P.op("act", lambda e, d=d: e.activation(out=t1[d], in_=t1[d], func=AF.Ln, bias=1.0), r=["t1%d" % d], w=["t1%d" % d])
            P.op("dve", lambda e, d=d: e.tensor_tensor_scan(out=t2[d], data0=C.cf[:, SCANM - NCBF:SCANM - NCBF + 512], data1=t1[d],
                                                             initial=0.0, op0=ALU.mult, op1=ALU.add),
                 r=["t1%d" % d, "cf", "AE"], w=["t2%d" % d])
            et = etf if d == 0 else etb
            c3 = t2[d].rearrange("p (a c) -> p a c", a=4)
            P.op("act", lambda e, et=et, c3=c3, t=t: e.activation(out=et[:, 4 * t:4 * t + 4], in_=c3[:, :, 127], func=AF.Exp, scale=-1.0 / 16),
                 r=["t2%d" % d, "AE"], w=["et%d:%d" % (d, t)])
            if d == 0:
                src = t2[0]
            else:
                P.op("dve", lambda e: e.tensor_tensor(out=t5, in0=t1[1], in1=t2[1], op=ALU.subtract), r=["t11", "t21", "AE"], w=["t5"])
                t53 = t5.rearrange("p (a c) -> p a c", a=4)
                P.op("dve", lambda e, t53=t53, c3=c3: e.tensor_tensor(out=t53, in0=t53, in1=c3[:, :, 127:128].to_broadcast([128, 4, 128]), op=ALU.add),
                     r=["t5", "t21"], w=["t5"])
                src = t5
            skey = "t20" if d == 0 else "t5"
            if with_q:
                P.op("act", lambda e, d=d, src=src: e.activation(out=t3[d], in_=src, func=AF.Exp, scale=-1.0 / 16, bias=LNC),
                     r=[skey, "AE"], w=["t3%d" % d])
            P.op("act", lambda e, d=d, src=src: e.activation(out=t4[d], in_=src, func=AF.Exp, scale=1.0 / 16), r=[skey, "AE"], w=["t4%d" % d])
        if with_q:
            for kc in range(KC):
                P.op("pe", lambda e, kc=kc, t=t: e.matmul(banks[0][:, :], lhsT=w_q[:, kc * 128:(kc + 1) * 128], rhs=xn_tile(kc, t),
                                                           start=(kc == 0), stop=(kc == KC - 1)),
                     r=[k_q, "xn:%d:%d" % (kc, t)], w=["b0"])
            for d, qt in ((0, qt_f), (1, qt_b)):
                P.op("dve", lambda e, d=d, qt=qt, sl=sl: e.tensor_tensor(out=qt[:, sl], in0=banks[0][:, :], in1=t3[d], op=ALU.mult),
                     r=["b0", "t3%d" % d, "AE"], w=["qt%d:%d" % (d, t)])
        for kc in range(KC):
            P.op("pe", lambda e, kc=kc, t=t: e.matmul(banks[1][:, :], lhsT=w_k[:, kc * 128:(kc + 1) * 128], rhs=xn_tile(kc, t),
                                                       start=(kc == 0), stop=(kc == KC - 1)),
                 r=[k_k, "xn:%d:%d" % (kc, t)], w=["b1"])
        for d, kt in ((0, kt_f), (1, kt_b)):
            P.op("dve", lambda e, d=d, kt=kt, sl=sl: e.tensor_tensor(out=kt[:, sl], in0=banks[1][:, :], in1=t4[d], op=ALU.mult),
                 r=["b1", "t4%d" % d, "AE"], w=["kt%d:%d" % (d, t)])
    W.done(b_lr)
    W.done(b_k)
    if with_q:
        W.done(b_q)
    w_v0, k_v0, b_v0 = W.get()
    w_v1, k_v1, b_v1 = W.get()
    for n2 in range(NB // 2):
        bk = banks[2 + (n2 % 2)]
        bkey = "b%d" % (2 + (n2 % 2))
        for j in range(2):
            n = 2 * n2 + j
            for half, (wv, kv) in enumerate(((w_v0, k_v0), (w_v1, k_v1))):
                for kc in range(KC):
                    P.op("pe", lambda e, kc=kc, n=n, j=j, half=half, wv=wv, bk=bk: e.matmul(
                        bk[:, j * 256 + half * 128:j * 256 + half * 128 + 128], lhsT=xn_blk(kc, n), rhs=wv[:, kc * 128:(kc + 1) * 128],
                        start=(kc == 0), stop=(kc == KC - 1)),
                         r=[kv, "xn:%d:%d" % (kc, n // 4)], w=[bkey])
        P.op("act", lambda e, n2=n2, bk=bk: e.activation(out=v_sb[:, 2 * n2:2 * n2 + 2, :], in_=bk[:, :].rearrange("p (a c) -> p a c", a=2), func=AF.Copy),
             r=[bkey, "AE"], w=["v:%d" % n2])
    W.done(b_v0)
    W.done(b_v1)
    kh = [[ar["kh0"], ar["kh1"]], [ar["kh2"], ar["kh3"]]]
    khs = [[ar["khs0"], ar["khs1"]], [ar["khs2"], ar["khs3"]]]
    tmpU = ar["tmpU"]
    P.op("pool", lambda e: e.memset(U.rearrange("p a n c -> p (a n c)"), 0.0), r=["AE"], w=["Uz"])
    bTs = [bfview(banks[4]), bfview(banks[5])]
    dirs = ((kt_f, etf), (kt_b, etb))

    def u_a(n):
        pb = n % 2
        bs = slice(n * 128, (n + 1) * 128)
        for d, (kt, et) in enumerate(dirs):
            P.op("dve", lambda e, d=d, kt=kt, et=et: e.tensor_scalar(out=kh[pb][d], in0=kt[:, bs], scalar1=et[:, n:n + 1], scalar2=None, op0=ALU.mult),
                 r=["kt%d:%d" % (d, n // 4), "et%d:%d" % (d, n // 4), "AE"], w=["kh%d:%d" % (pb, d)])
        for d in range(2):
            P.op("pe", lambda e, d=d: e.transpose(out=bTs[d][:, 0:128], in_=kh[pb][d], identity=C.cbf[:, IDENT:IDENT + 128]),
                 r=["kh%d:%d" % (pb, d), "cbf"], w=["b%d" % (4 + d)])
            P.op("act", lambda e, d=d: e.activation(out=khs[pb][d], in_=bTs[d][:, 0:128], func=AF.Copy), r=["b%d" % (4 + d), "AE"], w=["khs%d:%d" % (pb, d)])

    def u_b(n):
        pb = n % 2
        ub, ukey = banks[6 + pb], "b%d" % (6 + pb)
        for d in range(2):
            P.op("pe", lambda e, d=d: e.matmul(ub[:, d * 256:(d + 1) * 256], lhsT=khs[pb][d], rhs=v_sb[:, n, :], start=True, stop=True),
                 r=["khs%d:%d" % (pb, d), "v:%d" % (n // 2)], w=[ukey])
        P.op("dve", lambda e: e.tensor_tensor(out=tmpU.rearrange("p (a c) -> p a c", a=2), in0=ub[:, :].rearrange("p (a c) -> p a c", a=2),
                                              in1=C.cf[:, BDM - NCBF:BDM - NCBF + 256].unsqueeze(1).to_broadcast([128, 2, 256]), op=ALU.mult),
             r=[ukey, "cf", "AE"], w=["tmpU"])
        for d in range(2):
            P.op("dve", lambda e, d=d: e.tensor_reduce(out=U[:, d, n, 0:64], in_=tmpU[:, d * 256:(d + 1) * 256].rearrange("p (h c) -> p c h", h=4),
                                                       axis=AX.X, op=ALU.add),
                 r=["tmpU", "Uz", "AE"], w=["U:%d:%d" % (d, n)])

    u_a(0)
    for n in range(NB):
        if n + 1 < NB:
            u_a(n + 1)
        u_b(n)


def load_consts(C, ar):
    P = C.P
    tmp = ar.f32(NCBF)
    P.op("sp", lambda e: e.dma_start(out=tmp, in_=C.cst_d[:, 0:NCBF]), r=["AE"], w=["cst_tmp"], dma="cst")
    P.op("dve", lambda e: e.tensor_copy(out=C.cbf[:], in_=tmp), r=["cst_tmp"], w=["cbf"])
    C.scr = C.P.sb([128, 8], F32, name="scr")


SUM_CH = 4


def build_sum():
    nc = bass.Bass("TRN2", target_bir_lowering=False)
    st = ExitStack()
    P = Prog(nc, st)
    C = Ctx()
    setup_common(nc, P, C, SUM_CH)
    xin = nc.dram_tensor("xT", [D, NT], F32, kind="ExternalInput").ap()
    sout = nc.dram_tensor("so", [128, 2 * 65], F32, kind="ExternalOutput").ap()
    xn = P.sb([128, KC, NT], BF16, name="xn")
    ysb = P.sb([128, 4, NT], BF16, name="ysb")
    art = P.sb([128, 16384], F32, name="arena")
    A = Arena(art, 16384)
    load_consts(C, A)
    barrier(C)
    A.reset()
    xt = [A.f32(KC * TL), A.f32(KC * TL)]
    ar = dict(sq=A.bf16(KC * TL), rs=[A.f32(TL), A.f32(TL)])
    xv = xin.rearrange("(k p) t -> p k t", p=128)
    for t in range(NTL):
        b = t % 2
        xb = xt[b].rearrange("p (k t) -> p k t", k=KC)
        P.op("sp" if t % 2 == 0 else "pool", lambda e, xb=xb, t=t: e.dma_start(out=xb, in_=xv[:, :, t * TL:(t + 1) * TL]), r=["AE"], w=["xt%d" % b], dma="xt%d" % b)
        rms_tile(C, lambda kc, xb=xb: xb[:, kc, :], MIXG, lambda kc, t=t: xn[:, kc, t * TL:(t + 1) * TL], TL,
                 ["xt%d" % b], ["xn:%d:%d" % (kc, t) for kc in range(KC)], ar, [C.banks[2], C.banks[3]], ["b2", "b3"], "s%d" % t)
    barrier(C)
    A.reset()
    g = dict(glra=A.bf16(TL), t1f=A.f32(TL), t1b=A.f32(TL), t2f=A.f32(TL), t2b=A.f32(TL), t3f=None, t3b=None,
             t4f=A.f32(TL), t4b=A.f32(TL), t5=A.f32(TL), kh0=A.bf16(128), kh1=A.bf16(128), khs0=A.bf16(128), khs1=A.bf16(128), kh2=A.bf16(128), kh3=A.bf16(128), khs2=A.bf16(128), khs3=A.bf16(128),
             tmpU=A.f32(512))
    U = A.f32(2 * 16 * 65).rearrange("p (a n c) -> p a n c", a=2, n=16)
    etf = A.f32(16)
    etb = A.f32(16)
    Sf = A.f32(65)
    Sb = A.f32(65)
    v_sb = ysb[:, 2:4, :].rearrange("p c t -> p (c t)").rearrange("p (n c) -> p n c", n=16)
    gla_phase1(C, lambda kc, t: xn[:, kc, t * TL:(t + 1) * TL], lambda kc, n: xn[:, kc, n * 128:(n + 1) * 128], g, False,
               ysb[:, 0, :], ysb[:, 1, :], None, None, v_sb, U, etf, etb)
    ukeys = lambda d: ["U:%d:%d" % (d, n) for n in range(NB)]
    for S, d in ((Sf, 0), (Sb, 1)):
        P.op("pool", lambda e, S=S: e.memset(S[:, 0:64], 0.0), r=["AE"], w=["S%d" % d])
        P.op("pool", lambda e, S=S: e.memset(S[:, 64:65], 1.0), r=["AE"], w=["S%db" % d])
    for n in range(NB):
        P.op("dve", lambda e, n=n: e.scalar_tensor_tensor(out=Sf, in0=Sf, scalar=etf[:, n:n + 1], in1=U[:, 0, n, :], op0=ALU.mult, op1=ALU.add),
             r=["S0", "S0b", "U:0:%d" % n, "Uz", "et0:%d" % (n // 4), "AE"], w=["S0"])
    for n in range(NB - 1, -1, -1):
        P.op("dve", lambda e, n=n: e.scalar_tensor_tensor(out=Sb, in0=Sb, scalar=etb[:, n:n + 1], in1=U[:, 1, n, :], op0=ALU.mult, op1=ALU.add),
             r=["S1", "S1b", "U:1:%d" % n, "Uz", "et1:%d" % (n // 4), "AE"], w=["S1"])
    o1 = P.op("sp", lambda e: e.dma_start(out=sout[:, 0:65], in_=Sf), r=["S0"], dma="o1")
    o2 = P.op("sp", lambda e: e.dma_start(out=sout[:, 65:130], in_=Sb), r=["S1"], dma="o2")
    P.emit(final_waits=[o1, o2])
    return nc, st


def _chunk_cols(Wm, cols):
    sub = Wm[:, cols]
    if sub.shape[1] < 128:
        sub = np.concatenate([sub, np.zeros((sub.shape[0], 128 - sub.shape[1]), np.float32)], 1)
    return np.ascontiguousarray(sub.reshape(KC, 128, 128).transpose(1, 0, 2)).reshape(128, 1024)


def win_chunk(w_in_l, name):
    o = IN_OFF
    if name == "glr":
        cols = np.arange(o["glr"], o["glr"] + 32)
    elif name in ("gq", "gk", "ak", "av"):
        cols = np.arange(o[name], o[name] + 128)
    elif name in ("gv0", "gv1", "gr0", "gr1", "pin0", "pin1"):
        b = o[name[:-1]] + 128 * int(name[-1])
        cols = np.arange(b, b + 128)
    elif name in ("A0", "A1"):
        b = o["glu_a"] + 128 * int(name[-1])
        cols = np.arange(b, b + 128)
    elif name in ("G0", "G1"):
        b = o["glu_g"] + 128 * int(name[-1])
        cols = np.arange(b, b + 128)
    elif name == "QA":
        cols = np.concatenate([np.arange(o["aq"], o["aq"] + 64), np.arange(o["aq"] + 128, o["aq"] + 192)])
    elif name == "QB":
        cols = np.concatenate([np.arange(o["aq"] + 64, o["aq"] + 128), np.arange(o["aq"] + 192, o["aq"] + 256)])
    else:
        raise KeyError(name)
    return _chunk_cols(w_in_l, cols)


def sum_stream(inp, l):
    w = inp["w_in"][l]
    return np.stack([win_chunk(w, n) for n in ("glr", "gk", "gv0", "gv1")], 0)


def small_params(inp, l):
    sp = np.zeros((128, NSP), np.float32)
    sp[:, MIXG:MIXG + 8] = inp["norm_mix_g"][l].reshape(8, 128).T
    sp[:, FFNG:FFNG + 8] = inp["norm_ffn_g"][l].reshape(8, 128).T
    sp[:, FING:FING + 8] = inp["final_norm_g"].reshape(8, 128).T
    sp[:, CONVB:CONVB + 2] = inp["conv_b"][l].reshape(2, 128).T
    sp[:, LNG:LNG + 2] = inp["conv_ln_g"][l].reshape(2, 128).T
    sp[:, LNB:LNB + 2] = inp["conv_ln_b"][l].reshape(2, 128).T
    sp[:, GLAG:GLAG + 2] = inp["gla_norm_g"][l].reshape(2, 128).T
    sp[:, SINK:SINK + 4] = inp["attn_sink"][l][None, :]
    sp[:, PSC:PSC + 2] = inp["pool_scale"][l].reshape(2, 128).T
    cw = inp["conv_w"][l]
    for c in range(2):
        sp[:, CONVW + 31 * c:CONVW + 31 * (c + 1)] = cw[:, c * 128:(c + 1) * 128].T
    wu = inp["gla_w_up"][l]
    bu = inp["gla_b_up"][l]
    sp[0:16, WUP:WUP + 128] = wu[0]
    sp[16:32, WUP + 128:WUP + 256] = wu[1]
    sp[32, WUP:WUP + 128] = bu[0]
    sp[32, WUP + 128:WUP + 256] = bu[1]
    pw = inp["pool_w"][l]
    for c in range(2):
        for gg in range(2):
            g = 2 * c + gg
            sp[gg * 64:(gg + 1) * 64, POOLW + c * 128 + gg * 64:POOLW + c * 128 + (gg + 1) * 64] = pw[g]
    return sp


def const_block():
    c = np.zeros((128, NCST), np.float32)
    i = np.arange(128)
    c[:, IDENT:IDENT + 128] = np.eye(128, dtype=np.float32)
    c[:, ONES:ONES + 128] = 1.0
    c[:, ONESBLK:ONESBLK + 128] = (i[:, None] // 64 == i[None, :] // 64)
    pm = np.zeros((128, 128), np.float32)
    for hb in (0, 64):
        for j in range(8):
            pm[hb + j + 8, hb + j] = -1.0
            pm[hb + j, hb + j + 8] = 1.0
    c[:, PMAT:PMAT + 128] = pm
    c[:, TRIF:TRIF + 128] = (i[:, None] <= i[None, :])
    c[:, TRIB:TRIB + 128] = (i[:, None] >= i[None, :])
    c[:, BDM:BDM + 256] = (i[:, None] // 32 == np.arange(256)[None, :] // 64)
    c[:, HEADM:HEADM + 4] = (i[:, None] // 32 == np.arange(4)[None, :])
    c[:, INVW] = np.where(i < 64, 1.0 / 2, 1.0 / 4)
    c[:, INVW + 1] = np.where(i < 64, 1.0 / 8, 1.0 / 16)
    sm = np.ones(512, np.float32)
    sm[::128] = 0.0
    c[:, SCANM:SCANM + 512] = sm[None, :]
    return c


LAYER_STREAM = (["glr", "gq", "gk", "gv0", "gv1", "gr0", "gr1", "QA", "QB", "ak", "av", "A0", "G0", "A1", "G1", "pin0", "pin1"])
N_INPROJ = len(LAYER_STREAM)
LAYER_NCH = N_INPROJ + 8 * 5 + 8 + 4 * 16


def layer_stream(inp, l):
    w = inp["w_in"][l]
    ch = [win_chunk(w, n) for n in LAYER_STREAM]
    wb = inp["w_branch"][l]
    for f in range(8):
        blk = np.stack([wb[n][:, f * 128:(f + 1) * 128].reshape(2, 128, 128).transpose(1, 0, 2) for n in range(4)], 1)
        ch.append(np.ascontiguousarray(blk).reshape(128, 1024))
        for n in range(4):
            b = IN_OFF["gate"] + n * 1024 + f * 128
            ch.append(_chunk_cols(w, np.arange(b, b + 128)))
    wo = inp["w_out"][l]
    for f in range(8):
        ch.append(_chunk_cols(wo, np.arange(f * 128, (f + 1) * 128)))
    wu = inp["w_ffn_up"][l]
    wd = inp["w_ffn_down"][l]
    for g in range(4):
        for c in range(8):
            cc = g * 8 + c
            ch.append(_chunk_cols(wu, np.arange(cc * 128, (cc + 1) * 128)))
        for f in range(8):
            ch.append(_chunk_cols(wd[g * 1024:(g + 1) * 1024], np.arange(f * 128, (f + 1) * 128)))
    out = np.stack(ch, 0)
    assert out.shape[0] == LAYER_NCH
    return out


def percore_consts(core):
    p = core % 4
    pc = np.zeros((128, NPC), np.float32)
    q = np.arange(128)[:, None]
    s = np.arange(384)[None, :]
    base = ((s >= q) & (s <= q + 256))
    for v in range(3):
        m = base.copy()
        if v == 0 and p == 0:
            m &= (s >= 128)
        if v == 2 and p == 3:
            m &= (s < 256)
        pc[:, AMASK + v * 384:AMASK + (v + 1) * 384] = m
    i = np.arange(128)
    wid = [np.where(i < 64, 2, 4), np.where(i < 64, 8, 16)]
    corr = np.ones((128, 2, 16), np.float32)
    for c in range(2):
        w = wid[c].astype(np.float64)
        for j in range(8):
            if p == 0:
                pos = j
                lo = np.maximum(pos - w / 2, 0)
                hi = pos + w / 2
                corr[:, c, j] = w / (hi - lo)
            if p == 3:
                pos = SEQ - 8 + j
                lo = pos - w / 2
                hi = np.minimum(pos + w / 2, SEQ)
                corr[:, c, 8 + j] = w / (hi - lo)
    pc[:, PCORR:PCORR + 32] = corr.reshape(128, 32)
    cf = np.zeros((4, 3), np.float32)
    cb = np.zeros((4, 3), np.float32)
    for r in range(4):
        cf[r] = (1, 0, 1) if r < p else (0, 1, 0)
        cb[r] = (1, 0, 1) if r > p else (0, 1, 0)
    pc[:, COEFF:COEFF + 12] = cf.reshape(1, 12)
    pc[:, COEFB:COEFB + 12] = cb.reshape(1, 12)
    return pc


def rope_tables(core):
    p = core % 4
    pos = (p * NT - HB + np.arange(NE)).astype(np.float32)
    inv = (1.0 / (np.float32(500000.0) ** (np.arange(0, 16, 2, dtype=np.float32) / np.float32(16)))).astype(np.float32)
    ang = pos[:, None] * inv[None, :]
    cs = np.cos(ang).astype(np.float32).T
    sn = np.sin(ang).astype(np.float32).T
    Cm = np.ones((128, NE), np.float32)
    Sm = np.zeros((128, NE), np.float32)
    for hb in (0, 64):
        Cm[hb:hb + 8] = cs
        Cm[hb + 8:hb + 16] = cs
        Sm[hb:hb + 8] = sn
        Sm[hb + 8:hb + 16] = sn
    return np.stack([Cm, Sm], 0)


def build_layer(debug=False):
    nc = bass.Bass("TRN2", target_bir_lowering=False)
    st = ExitStack()
    P = Prog(nc, st)
    C = Ctx()
    setup_common(nc, P, C, LAYER_NCH)
    W, banks = C.W, C.banks
    xin = nc.dram_tensor("xT", [D, NE], F32, kind="ExternalInput").ap()
    pc_d = nc.dram_tensor("pc", [128, NPC], F32, kind="ExternalInput").ap()
    rope_d = nc.dram_tensor("rope", [2, 128, NE], F32, kind="ExternalInput").ap()
    gsum_d = nc.dram_tensor("gsum", [128, 4 * 130], F32, kind="ExternalInput").ap()
    xout = nc.dram_tensor("xo", [D, NT], F32, kind="ExternalOutput").ap()
    yout = nc.dram_tensor("yo", [D, NT], F32, kind="ExternalOutput").ap()
    if debug:
        dbg = nc.dram_tensor("dbg", [128, KC * NT], BF16, kind="ExternalOutput").ap()

    xT = P.sb([128, KC, NT], F32, name="xT")
    xn = P.sb([128, KC, NT], BF16, name="xn")
    xnh = P.sb([128, KC, 2 * HB], BF16, name="xnh")
    ys = P.sb([128, KC, NT], BF16, name="ys")
    pcf = P.sb([128, NPC - AMASK - 1152], F32, name="pcf")
    amb = P.sb([128, 1152], BF16, name="amb")
    AW = 11328
    art = P.sb([128, AW], F32, name="arena")
    A = Arena(art, AW)
    cbf, cf, spt = C.cbf, C.cf, C.spt
    CF = lambda off, n: cf[:, off - NCBF:off - NCBF + n]

    load_consts(C, A)
    tmpm = A.f32(1152)
    P.op("sp", lambda e: e.dma_start(out=tmpm, in_=pc_d[:, AMASK:AMASK + 1152]), r=["AE"], w=["tmpm"], dma="pc1")
    P.op("dve", lambda e: e.tensor_copy(out=amb[:], in_=tmpm), r=["tmpm"], w=["amb"])
    P.op("sp", lambda e: e.dma_start(out=pcf[:], in_=pc_d[:, PCORR:NPC]), w=["pcf"], dma="pc2")
    PCF = lambda off, n: pcf[:, off - PCORR:off - PCORR + n]
    barrier(C)
    A.reset()

    xv = xin.rearrange("(k p) t -> p k t", p=128)
    for kc in range(KC):
        P.op("sp" if kc % 2 == 0 else "pool", lambda e, kc=kc: e.dma_start(out=xT[:, kc, :], in_=xin[kc * 128:(kc + 1) * 128, HB:HB + NT]),
             w=["x:%d:%d" % (kc, t) for t in range(NTL)], dma="xl%d" % kc)
    xh = A.f32(KC * 2 * HB).rearrange("p (k t) -> p k t", k=KC)
    P.op("sp", lambda e: e.dma_start(out=xh[:, :, 0:HB], in_=xv[:, :, 0:HB]), r=["AE"], w=["xh0"], dma="xh0")
    P.op("sp", lambda e: e.dma_start(out=xh[:, :, HB:2 * HB], in_=xv[:, :, HB + NT:NE]), r=["AE"], w=["xh1"], dma="xh1")
    arn = dict(sq=A.bf16(KC * TL), rs=[A.f32(TL), A.f32(TL)])

    def norm_all(gcol):
        for t in range(NTL):
            rms_tile(C, lambda kc, t=t: xT[:, kc, t * TL:(t + 1) * TL], gcol, lambda kc, t=t: xn[:, kc, t * TL:(t + 1) * TL], TL,
                     ["x:%d:%d" % (kc, t) for kc in range(KC)], ["xn:%d:%d" % (kc, t) for kc in range(KC)], arn, [banks[2], banks[3]], ["b2", "b3"], "n%d" % t)

    norm_all(MIXG)
    rms_tile(C, lambda kc: xh[:, kc, :], MIXG, lambda kc: xnh[:, kc, :], 2 * HB, ["xh0", "xh1"], ["xnh:%d" % kc for kc in range(KC)], arn, [banks[2], banks[3]], ["b2", "b3"], "nh")
    barrier(C)
    A.reset()

    xn_tile = lambda kc, t: xn[:, kc, t * TL:(t + 1) * TL]
    xn_blk = lambda kc, n: xn[:, kc, n * 128:(n + 1) * 128]
    ext_ranges = [(lambda kc, t=t: xn[:, kc, t * TL:(t + 1) * TL], HB + t * TL, TL, (lambda kc, t=t: "xn:%d:%d" % (kc, t))) for t in range(NTL)]
    ext_ranges.append((lambda kc: xnh[:, kc, 0:HB], 0, HB, lambda kc: "xnh:%d" % kc))
    ext_ranges.append((lambda kc: xnh[:, kc, HB:2 * HB], HB + NT, HB, lambda kc: "xnh:%d" % kc))

    qt_f, qt_b, kt_f, kt_b = ys[:, 0, :], ys[:, 1, :], ys[:, 4, :], ys[:, 5, :]
    v_sb = ys[:, 6:8, :].rearrange("p c t -> p (c t)").rearrange("p (n c) -> p n c", n=16)
    U = A.f32(2 * 16 * 65).rearrange("p (a n c) -> p a n c", a=2, n=16)
    etf = A.f32(16)
    etb = A.f32(16)
    G = A.f32(4 * 130)
    Sst = A.f32(17 * 64 * 2).rearrange("p (d n c) -> p d n c", d=2, n=17)
    wr = A.f32(8)
    Tr = A.f32(4 * 64)
    amark = A.off
    g = dict(glra=A.bf16(TL), t1f=A.f32(TL), t1b=A.f32(TL), t2f=A.f32(TL), t2b=A.f32(TL), t3f=A.f32(TL), t3b=A.f32(TL),
             t4f=A.f32(TL), t4b=A.f32(TL), t5=A.f32(TL), kh0=A.bf16(128), kh1=A.bf16(128), khs0=A.bf16(128), khs1=A.bf16(128), kh2=A.bf16(128), kh3=A.bf16(128), khs2=A.bf16(128), khs3=A.bf16(128),
             tmpU=A.f32(512))
    gla_phase1(C, xn_tile, xn_blk, g, True, kt_f, kt_b, qt_f, qt_b, v_sb, U, etf, etb)
    G4 = G.rearrange("p (r d c) -> p r d c", r=4, d=2)
    P.op("sp", lambda e: e.dma_start(out=G, in_=gsum_d[:, :]), r=["AE"], w=["G"], dma="gs")
    for d, coff in ((0, COEFF), (1, COEFB)):
        co = PCF(coff, 12).rearrange("p (r c) -> p r c", r=4)
        P.op("dve", lambda e, d=d, co=co: e.tensor_tensor(out=wr[:, 0:4], in0=co[:, :, 0], in1=G4[:, :, d, 64], op=ALU.mult), r=["G", "pcf", "AE"], w=["wr"])
        P.op("dve", lambda e, co=co: e.tensor_tensor(out=wr[:, 0:4], in0=wr[:, 0:4], in1=co[:, :, 1], op=ALU.add), r=["wr", "pcf"], w=["wr"])
        P.op("dve", lambda e, d=d, co=co: e.tensor_tensor(out=Tr.rearrange("p (r c) -> p r c", r=4), in0=G4[:, :, d, 0:64],
                                                          in1=co[:, :, 2:3].to_broadcast([128, 4, 64]), op=ALU.mult), r=["G", "pcf", "AE"], w=["Tr"])
        s0 = Sst[:, 0, 0, :] if d == 0 else Sst[:, 1, 16, :]
        P.op("pool", lambda e, s0=s0: e.memset(s0, 0.0), r=["AE"], w=["S0_%d" % d])
        order = range(4) if d == 0 else range(3, -1, -1)
        for r_ in order:
            P.op("dve", lambda e, r_=r_, s0=s0: e.scalar_tensor_tensor(out=s0, in0=s0, scalar=wr[:, r_:r_ + 1], in1=Tr[:, r_ * 64:(r_ + 1) * 64],
                                                                         op0=ALU.mult, op1=ALU.add), r=["S0_%d" % d, "wr", "Tr"], w=["S0_%d" % d])
    for n in range(NB):
        P.op("dve", lambda e, n=n: e.scalar_tensor_tensor(out=Sst[:, 0, n + 1, :], in0=Sst[:, 0, n, :], scalar=etf[:, n:n + 1], in1=U[:, 0, n, 0:64],
                                                           op0=ALU.mult, op1=ALU.add),
             r=["S0_0" if n == 0 else "Sf:%d" % n, "U:0:%d" % n, "et0:%d" % (n // 4), "AE"], w=["Sf:%d" % (n + 1)])
    for n in range(NB - 1, -1, -1):
        P.op("dve", lambda e, n=n: e.scalar_tensor_tensor(out=Sst[:, 1, n, :], in0=Sst[:, 1, n + 1, :], scalar=etb[:, n:n + 1], in1=U[:, 1, n, 0:64],
                                                           op0=ALU.mult, op1=ALU.add),
             r=["S0_1" if n == NB - 1 else "Sb:%d" % (n + 1), "U:1:%d" % n, "et1:%d" % (n // 4), "AE"], w=["Sb:%d" % n])
    barrier(C)
    A.off = amark
    w_g0, k_g0, b_g0 = W.get()
    w_g1, k_g1, b_g1 = W.get()
    qm = [[A.bf16(512), A.bf16(512)], [A.bf16(512), A.bf16(512)]]
    scs = [[A.bf16(512), A.bf16(512)], [A.bf16(512), A.bf16(512)]]
    sbd = [[A.bf16(256), A.bf16(256)], [A.bf16(256), A.bf16(256)]]
    grs = A.bf16(2 * TL).rearrange("p (c t) -> p c t", c=2)
    sqo = A.bf16(2 * TL).rearrange("p (c t) -> p c t", c=2)
    rso = A.f32(2 * TL).rearrange("p (c t) -> p c t", c=2)
    yt = A.f32(TL)
    hm = CF(HEADM, 4)
    dirs = ((qt_f, kt_f, TRIF), (qt_b, kt_b, TRIB))

    def g_a(n):
        pb, t = n % 2, n // 4
        bs = slice(n * 128, (n + 1) * 128)
        for d, (qt, kt, tri) in enumerate(dirs):
            P.op("dve", lambda e, d=d, qt=qt: e.tensor_tensor(out=qm[pb][d].rearrange("p (h i) -> p h i", h=4),
                                                              in0=qt[:, bs].unsqueeze(1).to_broadcast([128, 4, 128]),
                                                              in1=hm.unsqueeze(2).to_broadcast([128, 4, 128]), op=ALU.mult),
                 r=["qt%d:%d" % (d, t), "cf", "AE"], w=["qm%d:%d" % (pb, d)])
        for d, (qt, kt, tri) in enumerate(dirs):
            P.op("pe", lambda e, d=d, kt=kt: e.matmul(banks[2 * pb + d][:, :], lhsT=kt[:, bs], rhs=qm[pb][d], start=True, stop=True),
                 r=["kt%d:%d" % (d, t), "qm%d:%d" % (pb, d)], w=["b%d" % (2 * pb + d)])

    def g_b(n):
        pb, t, j = n % 2, n // 4, n % 4
        bs = slice(n * 128, (n + 1) * 128)
        for d, (qt, kt, tri) in enumerate(dirs):
            P.op("dve", lambda e, d=d, tri=tri: e.tensor_tensor(out=scs[pb][d].rearrange("p (h i) -> p h i", h=4),
                                                                in0=banks[2 * pb + d][:, :].rearrange("p (h i) -> p h i", h=4),
                                                                in1=cbf[:, tri:tri + 128].unsqueeze(1).to_broadcast([128, 4, 128]), op=ALU.mult),
                 r=["b%d" % (2 * pb + d), "cbf", "AE"], w=["scs%d:%d" % (pb, d)])
            ssrc = Sst[:, 0, n, :] if d == 0 else Sst[:, 1, n + 1, :]
            skey = ("S0_0" if n == 0 else "Sf:%d" % n) if d == 0 else ("S0_1" if n == NB - 1 else "Sb:%d" % (n + 1))
            P.op("dve", lambda e, d=d, ssrc=ssrc: e.tensor_tensor(out=sbd[pb][d].rearrange("p (h c) -> p h c", h=4),
                                                                  in0=ssrc.unsqueeze(1).to_broadcast([128, 4, 64]),
                                                                  in1=hm.unsqueeze(2).to_broadcast([128, 4, 64]), op=ALU.mult),
                 r=[skey, "cf", "AE"], w=["sbd%d:%d" % (pb, d)])
        for hp in range(2):
            ob = banks[4 + hp]
            okey = "b%d" % (4 + hp)
            oc = slice(j * 128, (j + 1) * 128)
            P.op("pe", lambda e, hp=hp, ob=ob, oc=oc: e.matmul(ob[:, oc], lhsT=sbd[pb][0][:, hp * 128:(hp + 1) * 128], rhs=qt_f[:, bs], start=True, stop=False),
                 r=["sbd%d:0" % pb, "qt0:%d" % t], w=[okey])
            P.op("pe", lambda e, hp=hp, ob=ob, oc=oc: e.matmul(ob[:, oc], lhsT=sbd[pb][1][:, hp * 128:(hp + 1) * 128], rhs=qt_b[:, bs], start=False, stop=False),
                 r=["sbd%d:1" % pb, "qt1:%d" % t], w=[okey])
            for hh in range(2):
                h = 2 * hp + hh
                for d in range(2):
                    last = (hh == 1 and d == 1)
                    P.op("pe", lambda e, ob=ob, oc=oc, hh=hh, h=h, d=d, last=last: e.matmul(
                        ob[hh * 64:(hh + 1) * 64, oc], lhsT=v_sb[:, n, h * 64:(h + 1) * 64], rhs=scs[pb][d][:, h * 128:(h + 1) * 128], start=False, stop=last),
                         r=["scs%d:%d" % (pb, d), "v:%d" % (n // 2)], w=[okey])

    def g_gr(t):
        for c, (wg, kg) in enumerate(((w_g0, k_g0), (w_g1, k_g1))):
            for kc in range(KC):
                P.op("pe", lambda e, kc=kc, c=c, wg=wg: e.matmul(banks[6 + c][:, :], lhsT=wg[:, kc * 128:(kc + 1) * 128], rhs=xn_tile(kc, t),
                                                                 start=(kc == 0), stop=(kc == KC - 1)),
                     r=[kg, "xn:%d:%d" % (kc, t)], w=["b%d" % (6 + c)])
            P.op("act", lambda e, c=c: e.activation(out=grs[:, c, :], in_=banks[6 + c][:, :], func=AF.Silu), r=["b%d" % (6 + c), "AE"], w=["grs%d" % c])

    def g_post(t):
        for hp in range(2):
            ob = banks[4 + hp]
            okey = "b%d" % (4 + hp)
            P.op("act", lambda e, hp=hp, ob=ob: e.activation(out=sqo[:, hp, :], in_=ob[:, :], func=AF.Square), r=[okey, "AE"], w=["sqo%d" % hp])
            P.op("pe", lambda e, hp=hp: e.matmul(banks[6 + hp][:, :], lhsT=cbf[:, ONESBLK:ONESBLK + 128], rhs=sqo[:, hp, :], start=True, stop=True),
                 r=["sqo%d" % hp, "cbf", "grs%d" % hp], w=["b%d" % (6 + hp)])
            P.op("act", lambda e, hp=hp: e.activation(out=rso[:, hp, :], in_=banks[6 + hp][:, :], func=AF.Sqrt, bias=EPS, scale=1.0 / 64),
                 r=["b%d" % (6 + hp), "AE"], w=["rso%d" % hp])
            P.op("dve", lambda e, hp=hp: e.reciprocal(out=rso[:, hp, :], in_=rso[:, hp, :]), r=["rso%d" % hp], w=["rso%d" % hp])
            P.op("dve", lambda e, hp=hp, ob=ob: e.tensor_tensor(out=yt, in0=ob[:, :], in1=rso[:, hp, :], op=ALU.mult), r=[okey, "rso%d" % hp, "AE"], w=["yt"])
            P.op("dve", lambda e, hp=hp: e.scalar_tensor_tensor(out=ys[:, 2 + hp, t * TL:(t + 1) * TL], in0=yt, scalar=spt[:, GLAG + hp:GLAG + hp + 1],
                                                                 in1=grs[:, hp, :], op0=ALU.mult, op1=ALU.mult),
                 r=["yt", "grs%d" % hp, "spt"], w=["ys:%d:%d" % (2 + hp, t)])

    g_gr(0)
    g_a(0)
    for n in range(NB):
        if n + 1 < NB:
            g_a(n + 1)
        g_b(n)
        if n % 4 == 3:
            g_post(n // 4)
            if n + 1 < NB:
                g_gr(n // 4 + 1)
    W.done(b_g0)
    W.done(b_g1)
    barrier(C)
    A.reset()
    C.A, C.xT, C.xn, C.xnh, C.ys, C.amb, C.PCF, C.CF, C.ext_ranges = A, xT, xn, xnh, ys, amb, PCF, CF, ext_ranges
    C.rope_d, C.xout, C.yout, C.norm_all, C.arn_fn = rope_d, xout, yout, norm_all, None
    C.debug = debug
    if debug:
        C.dbg = dbg
    return nc, st, P, C


def proj_ext(C, wt, wkey, rng, bank, bkey, M=128):
    rhs_fn, eoff, ntok, keyf = rng
    for kc in range(KC):
        C.P.op("pe", lambda e, kc=kc: e.matmul(bank[0:M, 0:ntok], lhsT=wt[:, kc * 128:kc * 128 + M], rhs=rhs_fn(kc),
                                                start=(kc == 0), stop=(kc == KC - 1)),
               r=[wkey, keyf(kc)], w=[bkey])


def attention_phase(C):
    P, W, banks, A, ys = C.P, C.W, C.banks, C.A, C.ys
    cbf, spt = C.cbf, C.spt
    q_att = ys[:, 0:2, :]
    k_att = A.bf16(NE)
    v_att = A.bf16(18 * 130).rearrange("p (n c) -> p n c", n=18)
    nsink = A.f32(4)
    amark = A.off
    rC = A.f32(NE)
    rS = A.f32(NE)
    P.op("sp", lambda e: e.dma_start(out=rC, in_=C.rope_d[0]), r=["AE"], w=["rC"], dma="rC")
    P.op("sp", lambda e: e.dma_start(out=rS, in_=C.rope_d[1]), r=["AE"], w=["rS"], dma="rS")
    raw = [A.bf16(TL), A.bf16(TL)]
    r1 = [A.f32(TL), A.f32(TL)]
    r2 = [A.f32(TL), A.f32(TL)]
    P.op("dve", lambda e: e.tensor_scalar(out=nsink, in0=spt[:, SINK:SINK + 4], scalar1=-1.0, scalar2=None, op0=ALU.mult), r=["spt", "AE"], w=["nsink"])
    P.op("pool", lambda e: e.memset(v_att.rearrange("p n c -> p (n c)"), 1.0), r=["AE"], w=["vones"])
    cnt = [0]

    def rope_proj(wt, wkey, rng, dst, dkey, scale):
        rhs_fn, eoff, ntok, keyf = rng
        i = cnt[0] % 2
        cnt[0] += 1
        bk, bkey = banks[i], "b%d" % i
        bp, bpkey = banks[2 + i], "b%d" % (2 + i)
        proj_ext(C, wt, wkey, rng, bk, bkey)
        P.op("act", lambda e: e.activation(out=raw[i][:, 0:ntok], in_=bk[:, 0:ntok], func=AF.Copy, scale=scale), r=[bkey, "AE"], w=["raw%d" % i])
        P.op("pe", lambda e: e.matmul(bp[:, 0:ntok], lhsT=cbf[:, PMAT:PMAT + 128], rhs=raw[i][:, 0:ntok], start=True, stop=True),
             r=["raw%d" % i, "cbf"], w=[bpkey])
        P.op("dve", lambda e: e.tensor_tensor(out=r1[i][:, 0:ntok], in0=raw[i][:, 0:ntok], in1=rC[:, eoff:eoff + ntok], op=ALU.mult),
             r=["raw%d" % i, "rC", "AE"], w=["r1%d" % i])
        P.op("dve", lambda e: e.tensor_tensor(out=r2[i][:, 0:ntok], in0=bp[:, 0:ntok], in1=rS[:, eoff:eoff + ntok], op=ALU.mult),
             r=[bpkey, "rS", "AE"], w=["r2%d" % i])
        P.op("pool", lambda e: e.tensor_tensor(out=dst, in0=r1[i][:, 0:ntok], in1=r2[i][:, 0:ntok], op=ALU.add),
             r=["r1%d" % i, "r2%d" % i, "AE"], w=[dkey])

    for gq in range(2):
        wt, wkey, wb_ = W.get()
        for t in range(NTL):
            rope_proj(wt, wkey, C.ext_ranges[t], q_att[:, gq, t * TL:(t + 1) * TL], "qa:%d:%d" % (gq, t), 0.125)
        W.done(wb_)
    wt, wkey, wb_ = W.get()
    for ri, rng in enumerate(C.ext_ranges):
        rope_proj(wt, wkey, rng, k_att[:, rng[1]:rng[1] + rng[2]], "ka:%d" % ri, 1.0)
    W.done(wb_)
    wt, wkey, wb_ = W.get()
    for eb in range(18):
        if eb == 0:
            lf, kf = (lambda kc: C.xnh[:, kc, 0:HB]), (lambda kc: "xnh:%d" % kc)
        elif eb == 17:
            lf, kf = (lambda kc: C.xnh[:, kc, HB:2 * HB]), (lambda kc: "xnh:%d" % kc)
        else:
            lf, kf = (lambda kc, eb=eb: C.xn[:, kc, (eb - 1) * 128:eb * 128]), (lambda kc, eb=eb: "xn:%d:%d" % (kc, (eb - 1) // 4))
        bk, bkey = banks[4 + eb % 2], "b%d" % (4 + eb % 2)
        for kc in range(KC):
            P.op("pe", lambda e, kc=kc, lf=lf, bk=bk: e.matmul(bk[:, 0:128], lhsT=lf(kc), rhs=wt[:, kc * 128:(kc + 1) * 128],
                                                                start=(kc == 0), stop=(kc == KC - 1)),
                 r=[wkey, kf(kc)], w=[bkey])
        P.op("act", lambda e, eb=eb, bk=bk: e.activation(out=v_att[:, eb, :].rearrange("p (g c) -> p g c", g=2)[:, :, 0:64],
                                                          in_=bk[:, 0:128].rearrange("p (g c) -> p g c", g=2), func=AF.Copy),
             r=[bkey, "vones", "AE"], w=["va:%d" % eb])
    W.done(wb_)
    barrier(C)
    A.off = amark
    mx = [A.f32(4), A.f32(4)]
    negm = [A.f32(4), A.f32(4)]
    es = [A.f32(4), A.f32(4)]
    den = A.f32(4)
    Pb = [[A.bf16(768), A.bf16(768)], [A.bf16(768), A.bf16(768)]]
    Pm = [[A.bf16(768), A.bf16(768)], [A.bf16(768), A.bf16(768)]]
    PTs = [A.bf16(768), A.bf16(768)]
    on = [A.bf16(256), A.bf16(256)]
    wkeys = ["ka:%d" % ri for ri in range(6)]

    def stage_a(n):
        pb = n % 2
        v = 0 if n == 0 else (2 if n == NB - 1 else 1)
        msk = C.amb[:, v * 384:(v + 1) * 384]
        qs = slice(n * 128, (n + 1) * 128)
        win = slice(n * 128, n * 128 + 384)
        for k in range(2):
            pr = slice(64 * k, 64 * k + 64)
            for g_ in range(2):
                bk, bkey = banks[2 * k + g_], "b%d" % (2 * k + g_)
                P.op("pe", lambda e, bk=bk, g_=g_, pr=pr: e.matmul(bk[:, 0:384], lhsT=q_att[pr, g_, qs], rhs=k_att[pr, win], start=True, stop=True),
                     r=["qa:%d:%d" % (g_, n // 4)] + wkeys, w=[bkey])
        for h in range(4):
            P.op("dve", lambda e, h=h: e.reduce_max(out=mx[pb][:, h:h + 1], in_=banks[h][:, 0:384], axis=AX.X), r=["b%d" % h, "AE"], w=["mx%d:%d" % (pb, h)])
        P.op("dve", lambda e: e.scalar_tensor_tensor(out=negm[pb], in0=mx[pb], scalar=-1.0, in1=nsink, op0=ALU.mult, op1=ALU.min),
             r=["mx%d:%d" % (pb, h) for h in range(4)] + ["nsink"], w=["negm%d" % pb])

    def stage_a2(n):
        pb = n % 2
        v = 0 if n == 0 else (2 if n == NB - 1 else 1)
        msk = C.amb[:, v * 384:(v + 1) * 384]
        for k in range(2):
            for g_ in range(2):
                h = 2 * k + g_
                P.op("act", lambda e, g_=g_, h=h, k=k: e.activation(out=Pb[pb][k][:, g_ * 384:(g_ + 1) * 384], in_=banks[h][:, 0:384], func=AF.Exp, bias=negm[pb][:, h:h + 1]),
                     r=["b%d" % h, "negm%d" % pb, "AE"], w=["Pb%d:%d:%d" % (pb, k, g_)])
            P.op("pool", lambda e, k=k: e.tensor_tensor(out=Pm[pb][k].rearrange("p (g s) -> p g s", g=2), in0=Pb[pb][k].rearrange("p (g s) -> p g s", g=2),
                                                        in1=msk.unsqueeze(1).to_broadcast([128, 2, 384]), op=ALU.mult),
                 r=["Pb%d:%d:0" % (pb, k), "Pb%d:%d:1" % (pb, k), "amb", "AE"], w=["Pm%d:%d" % (pb, k)])
        P.op("dve", lambda e: e.tensor_tensor(out=es[pb], in0=negm[pb], in1=spt[:, SINK:SINK + 4], op=ALU.add), r=["negm%d" % pb, "spt", "AE"], w=["es%d" % pb])
        P.op("act", lambda e: e.activation(out=es[pb], in_=es[pb], func=AF.Exp), r=["es%d" % pb], w=["es%d" % pb])

    def stage_b(n):
        pb = n % 2
        qs = slice(n * 128, (n + 1) * 128)
        for k in range(2):
            bt = bfview(banks[4 + k])
            btkey = "b%d" % (4 + k)
            for j in range(6):
                P.op("pe", lambda e, j=j, k=k, bt=bt: e.transpose(out=bt[:, j * 128:(j + 1) * 128], in_=Pm[pb][k][:, j * 128:(j + 1) * 128], identity=cbf[:, IDENT:IDENT + 128]),
                     r=["Pm%d:%d" % (pb, k), "cbf"], w=[btkey])
            P.op("act", lambda e, k=k, bt=bt: e.activation(out=PTs[k], in_=bt[:, 0:768], func=AF.Copy), r=[btkey, "AE"], w=["PTs%d" % k])

    def stage_b2(n):
        pb = n % 2
        qs = slice(n * 128, (n + 1) * 128)
        for k in range(2):
            for g_ in range(2):
                h = 2 * k + g_
                for w_ in range(3):
                    P.op("pe", lambda e, g_=g_, h=h, w_=w_, k=k: e.matmul(banks[6][:, h * 65:(h + 1) * 65], lhsT=PTs[k][:, (g_ * 3 + w_) * 128:(g_ * 3 + w_ + 1) * 128],
                                                                          rhs=v_att[:, n + w_, k * 65:(k + 1) * 65], start=(w_ == 0), stop=(w_ == 2)),
                         r=["PTs%d" % k] + ["va:%d" % (n + w_)], w=["b6"])
        b6v = banks[6][:, 0:260].rearrange("p (h c) -> p h c", h=4)
        P.op("dve", lambda e: e.tensor_tensor(out=den, in0=b6v[:, :, 64], in1=es[pb], op=ALU.add), r=["b6", "es%d" % pb, "AE"], w=["den"])
        P.op("dve", lambda e: e.reciprocal(out=den, in_=den), r=["den"], w=["den"])
        P.op("dve", lambda e: e.tensor_tensor(out=on[pb].rearrange("p (h c) -> p h c", h=4), in0=b6v[:, :, 0:64],
                                              in1=den.unsqueeze(2).to_broadcast([128, 4, 64]), op=ALU.mult), r=["b6", "den", "AE"], w=["on%d" % pb])
        b7 = bfview(banks[7])
        for c in range(2):
            P.op("pe", lambda e, c=c: e.transpose(out=b7[:, c * 128:(c + 1) * 128], in_=on[pb][:, c * 128:(c + 1) * 128], identity=cbf[:, IDENT:IDENT + 128]),
                 r=["on%d" % pb, "cbf"], w=["b7"])
        P.op("act", lambda e: e.activation(out=ys[:, 4:6, qs], in_=b7[:, 0:256].rearrange("p (c t) -> p c t", c=2), func=AF.Copy),
             r=["b7", "AE"], w=["ys:4:%d" % (n // 4), "ys:5:%d" % (n // 4)])

    stage_a(0)
    stage_a2(0)
    for n in range(NB):
        if n + 1 < NB:
            stage_a(n + 1)
        stage_b(n)
        if n + 1 < NB:
            stage_a2(n + 1)
        stage_b2(n)
    barrier(C)
    A.reset()


def conv_phase(C):
    P, W, banks, A, ys = C.P, C.W, C.banks, C.A, C.ys
    cbf, spt = C.cbf, C.spt
    u_ext = A.bf16(2 * NE).rearrange("p (c t) -> p c t", c=2)
    Dm = A.bf16(2 * 31 * 128).rearrange("p (c k j) -> p c k j", c=2, k=31)
    sg = [A.f32(TL), A.f32(TL)]
    for c in range(2):
        P.op("pool", lambda e, c=c: e.tensor_tensor(out=Dm[:, c, :, :], in0=cbf[:, IDENT:IDENT + 128].unsqueeze(1).to_broadcast([128, 31, 128]),
                                                     in1=spt[:, CONVW + 31 * c:CONVW + 31 * (c + 1)].unsqueeze(2).to_broadcast([128, 31, 128]), op=ALU.mult),
             r=["cbf", "spt", "AE"], w=["Dm%d" % c])
    i = 0
    for c in range(2):
        wa, ka, ba = W.get()
        wg, kg, bg = W.get()
        for ri, rng in enumerate(C.ext_ranges):
            eoff, ntok = rng[1], rng[2]
            j = i % 2
            i += 1
            proj_ext(C, wa, ka, rng, banks[j], "b%d" % j)
            proj_ext(C, wg, kg, rng, banks[2 + j], "b%d" % (2 + j))
            P.op("act", lambda e, j=j, ntok=ntok: e.activation(out=sg[j][:, 0:ntok], in_=banks[2 + j][:, 0:ntok], func=AF.Sigmoid), r=["b%d" % (2 + j), "AE"], w=["sg%d" % j])
            P.op("dve", lambda e, j=j, c=c, eoff=eoff, ntok=ntok: e.tensor_tensor(out=u_ext[:, c, eoff:eoff + ntok], in0=banks[j][:, 0:ntok], in1=sg[j][:, 0:ntok], op=ALU.mult),
                 r=["b%d" % j, "sg%d" % j, "AE"], w=["u:%d:%d" % (c, ri)])
        W.done(ba)
        W.done(bg)
    ysb = A.f32(2 * TL).rearrange("p (c t) -> p c t", c=2)
    ybf = A.bf16(2 * TL).rearrange("p (c t) -> p c t", c=2)
    ysq = A.bf16(2 * TL).rearrange("p (c t) -> p c t", c=2)
    mean = A.f32(TL)
    var = A.f32(TL)
    dd = A.f32(TL)
    msq = dd
    for t in range(NTL):
        for c in range(2):
            bk, bkey = banks[4 + c], "b%d" % (4 + c)
            for k in range(31):
                s0 = HB + t * TL + k - 15
                P.op("pe", lambda e, c=c, k=k, s0=s0, bk=bk: e.matmul(bk[:, :], lhsT=Dm[:, c, k, :], rhs=u_ext[:, c, s0:s0 + TL], start=(k == 0), stop=(k == 30)),
                     r=["Dm%d" % c] + ["u:%d:%d" % (c, ri) for ri in range(6)], w=[bkey])
            P.op("act", lambda e, c=c, bk=bk: e.activation(out=ysb[:, c, :], in_=bk[:, :], func=AF.Identity, bias=spt[:, CONVB + c:CONVB + c + 1]),
                 r=[bkey, "spt", "AE"], w=["ysb%d" % c])
            P.op("act", lambda e, c=c, bk=bk: e.activation(out=ysq[:, c, :], in_=bk[:, :], func=AF.Square, bias=spt[:, CONVB + c:CONVB + c + 1]),
                 r=[bkey, "spt", "AE"], w=["ysq%d" % c])
            P.op("pool", lambda e, c=c: e.tensor_copy(out=ybf[:, c, :], in_=ysb[:, c, :]), r=["ysb%d" % c, "AE"], w=["ybf%d" % c])
        for c in range(2):
            P.op("pe", lambda e, c=c: e.matmul(banks[6][:, :], lhsT=cbf[:, ONES:ONES + 128], rhs=ybf[:, c, :], start=(c == 0), stop=(c == 1)), r=["ybf%d" % c, "cbf"], w=["b6"])
        for c in range(2):
            P.op("pe", lambda e, c=c: e.matmul(banks[7][:, :], lhsT=cbf[:, ONES:ONES + 128], rhs=ysq[:, c, :], start=(c == 0), stop=(c == 1)), r=["ysq%d" % c, "cbf"], w=["b7"])
        P.op("dve", lambda e: e.tensor_scalar(out=mean, in0=banks[6][:, :], scalar1=1.0 / 256, scalar2=None, op0=ALU.mult), r=["b6", "AE"], w=["mean"])
        P.op("dve", lambda e: e.tensor_tensor(out=msq, in0=mean, in1=mean, op=ALU.mult), r=["mean", "AE"], w=["dd"])
        P.op("dve", lambda e: e.scalar_tensor_tensor(out=var, in0=banks[7][:, :], scalar=1.0 / 256, in1=msq, op0=ALU.mult, op1=ALU.subtract), r=["b7", "dd", "AE"], w=["var"])
        P.op("act", lambda e: e.activation(out=var, in_=var, func=AF.Sqrt, bias=EPS), r=["var"], w=["var"])
        P.op("dve", lambda e: e.reciprocal(out=var, in_=var), r=["var"], w=["var"])
        for c in range(2):
            P.op("dve", lambda e, c=c: e.tensor_tensor(out=dd, in0=ysb[:, c, :], in1=mean, op=ALU.subtract), r=["ysb%d" % c, "mean", "AE"], w=["dd"])
            P.op("dve", lambda e: e.tensor_tensor(out=dd, in0=dd, in1=var, op=ALU.mult), r=["dd", "var"], w=["dd"])
            P.op("act", lambda e, c=c, t=t: e.activation(out=ys[:, c, t * TL:(t + 1) * TL], in_=dd, func=AF.Silu, scale=spt[:, LNG + c:LNG + c + 1], bias=spt[:, LNB + c:LNB + c + 1]),
                 r=["dd", "spt", "AE"], w=["ys:%d:%d" % (c, t)])
    barrier(C)
    A.reset()


def pool_phase(C):
    P, W, banks, A, ys = C.P, C.W, C.banks, C.A, C.ys
    spt = C.spt
    pin = A.f32(2 * NE).rearrange("p (c t) -> p c t", c=2)
    B1 = A.f32(NE)
    B2 = A.f32(NE)
    dbf = [A.bf16(TL), A.bf16(TL)]
    pwb = A.bf16(256)
    P.op("dve", lambda e: e.tensor_copy(out=pwb, in_=spt[:, POOLW:POOLW + 256]), r=["spt", "AE"], w=["pwb"])
    i = 0
    for c in range(2):
        wt, wkey, wb_ = W.get()
        for ri, rng in enumerate(C.ext_ranges):
            eoff, ntok = rng[1], rng[2]
            j = i % 2
            i += 1
            proj_ext(C, wt, wkey, rng, banks[j], "b%d" % j)
            P.op("act", lambda e, j=j, c=c, eoff=eoff, ntok=ntok: e.activation(out=pin[:, c, eoff:eoff + ntok], in_=banks[j][:, 0:ntok], func=AF.Copy),
                 r=["b%d" % j, "AE"], w=["pin:%d:%d" % (c, ri)])
        W.done(wb_)
    Wd = NE
    invw = C.CF(INVW, 2)
    corr = C.PCF(PCORR, 32).rearrange("p (c j) -> p c j", c=2)
    for c in range(2):
        u = pin[:, c, :]
        pk = ["pin:%d:%d" % (c, ri) for ri in range(6)]
        P.op("pool", lambda e, u=u: e.tensor_tensor(out=B1[:, 1:Wd], in0=u[:, 0:Wd - 1], in1=u[:, 1:Wd], op=ALU.add), r=pk + ["AE"], w=["B1"])
        P.op("pool", lambda e: e.tensor_tensor(out=B2[:, 2:Wd - 1], in0=B1[:, 1:Wd - 2], in1=B1[:, 3:Wd], op=ALU.add), r=["B1", "AE"], w=["B2"])
        if c == 1:
            P.op("pool", lambda e: e.tensor_tensor(out=B1[:, 4:Wd - 3], in0=B2[:, 2:Wd - 5], in1=B2[:, 6:Wd - 1], op=ALU.add), r=["B2"], w=["B1"])
            P.op("pool", lambda e: e.tensor_tensor(out=B2[:, 8:Wd - 7], in0=B1[:, 4:Wd - 11], in1=B1[:, 12:Wd - 3], op=ALU.add), r=["B1"], w=["B2"])
        for (Bx, bkey, pr) in ((B1, "B1", slice(0, 64)), (B2, "B2", slice(64, 128))):
            P.op("dve", lambda e, Bx=Bx, pr=pr, c=c: e.tensor_tensor(out=Bx[pr, HB:HB + 8], in0=Bx[pr, HB:HB + 8], in1=corr[pr, c, 0:8], op=ALU.mult), r=[bkey, "pcf"], w=[bkey])
            P.op("dve", lambda e, Bx=Bx, pr=pr, c=c: e.tensor_tensor(out=Bx[pr, HB + NT - 8:HB + NT], in0=Bx[pr, HB + NT - 8:HB + NT], in1=corr[pr, c, 8:16], op=ALU.mult), r=[bkey, "pcf"], w=[bkey])
        for t in range(NTL):
            j = t % 2
            es = slice(HB + t * TL, HB + (t + 1) * TL)
            for (Bx, bkey, pr) in ((B1, "B1", slice(0, 64)), (B2, "B2", slice(64, 128))):
                P.op("dve", lambda e, Bx=Bx, pr=pr, c=c, j=j, es=es: e.scalar_tensor_tensor(out=dbf[j][pr, :], in0=Bx[pr, es], scalar=invw[pr, c:c + 1], in1=pin[pr, c, es],
                                                                                             op0=ALU.mult, op1=ALU.subtract),
                     r=[bkey, "cf"] + pk + ["AE"], w=["dbf%d:%d" % (j, pr.start)])
            bk, bkey2 = banks[2 + j], "b%d" % (2 + j)
            P.op("pe", lambda e, c=c, j=j, bk=bk: e.matmul(bk[:, :], lhsT=pwb[:, c * 128:(c + 1) * 128], rhs=dbf[j], start=True, stop=True),
                 r=["dbf%d:0" % j, "dbf%d:64" % j, "pwb"], w=[bkey2])
            P.op("act", lambda e, c=c, t=t, bk=bk: e.activation(out=ys[:, 6 + c, t * TL:(t + 1) * TL], in_=bk[:, :], func=AF.Identity, scale=spt[:, PSC + c:PSC + c + 1]),
                 r=[bkey2, "spt", "AE"], w=["ys:%d:%d" % (6 + c, t)])
    barrier(C)
    A.reset()


def merge_phase(C):
    P, W, banks, A, ys, xn, xT = C.P, C.W, C.banks, C.A, C.ys, C.xn, C.xT
    merged = A.bf16(KC * NT).rearrange("p (k t) -> p k t", k=KC)
    acc = A.f32(NTL * TL).rearrange("p (a t) -> p a t", a=NTL)
    tmp = [A.f32(TL), A.f32(TL)]
    sgv = C.xnh[:, :, :].rearrange("p k t -> p (k t)").bitcast(F32)
    sg = [sgv[:, 0:TL], sgv[:, TL:2 * TL]]
    i = 0
    for f in range(KC):
        wb, kb, bb = W.get()
        wb4 = wb[:, :].rearrange("p (n k j) -> p n k j", n=4, k=2)
        for n in range(4):
            wg, kg, bg = W.get()
            for t in range(NTL):
                j = i % 2
                i += 1
                G, gk = banks[j], "b%d" % j
                Pj, pk = banks[2 + j], "b%d" % (2 + j)
                for kc in range(KC):
                    P.op("pe", lambda e, kc=kc, t=t, wg=wg, G=G: e.matmul(G[:, :], lhsT=wg[:, kc * 128:(kc + 1) * 128], rhs=xn[:, kc, t * TL:(t + 1) * TL],
                                                                          start=(kc == 0), stop=(kc == KC - 1)), r=[kg, "xn:%d:%d" % (kc, t)], w=[gk])
                for k2 in range(2):
                    P.op("pe", lambda e, k2=k2, t=t, n=n, Pj=Pj, wb4=wb4: e.matmul(Pj[:, :], lhsT=wb4[:, n, k2, :], rhs=ys[:, 2 * n + k2, t * TL:(t + 1) * TL],
                                                                                   start=(k2 == 0), stop=(k2 == 1)), r=[kb, "ys:%d:%d" % (2 * n + k2, t)], w=[pk])
                P.op("act", lambda e, j=j, G=G: e.activation(out=sg[j], in_=G[:, :], func=AF.Sigmoid), r=[gk, "xnhfree"], w=["sgm%d" % j])
                if n == 0:
                    P.op("dve", lambda e, j=j, t=t, Pj=Pj: e.tensor_tensor(out=acc[:, t, :], in0=Pj[:, :], in1=sg[j], op=ALU.mult), r=[pk, "sgm%d" % j, "AE"], w=["acc%d" % t])
                else:
                    P.op("dve", lambda e, j=j, Pj=Pj: e.tensor_tensor(out=tmp[j], in0=Pj[:, :], in1=sg[j], op=ALU.mult), r=[pk, "sgm%d" % j, "AE"], w=["tmp%d" % j])
                    if n < 3:
                        P.op("pool", lambda e, j=j, t=t: e.tensor_tensor(out=acc[:, t, :], in0=acc[:, t, :], in1=tmp[j], op=ALU.add), r=["acc%d" % t, "tmp%d" % j], w=["acc%d" % t])
                    else:
                        P.op("pool", lambda e, j=j, t=t, f=f: e.tensor_tensor(out=merged[:, f, t * TL:(t + 1) * TL], in0=acc[:, t, :], in1=tmp[j], op=ALU.add),
                             r=["acc%d" % t, "tmp%d" % j], w=["mg:%d:%d" % (f, t)])
            W.done(bg)
        W.done(bb)
    i = 0
    for f in range(KC):
        wo, ko, bo = W.get()
        for t in range(NTL):
            j = i % 4
            i += 1
            bk, bkey = banks[4 + j], "b%d" % (4 + j)
            for kc in range(KC):
                P.op("pe", lambda e, kc=kc, t=t, bk=bk, wo=wo: e.matmul(bk[:, :], lhsT=wo[:, kc * 128:(kc + 1) * 128], rhs=merged[:, kc, t * TL:(t + 1) * TL],
                                                                        start=(kc == 0), stop=(kc == KC - 1)), r=[ko, "mg:%d:%d" % (kc, t)], w=[bkey])
            P.op("dve", lambda e, f=f, t=t, bk=bk: e.tensor_tensor(out=xT[:, f, t * TL:(t + 1) * TL], in0=bk[:, :], in1=xT[:, f, t * TL:(t + 1) * TL], op=ALU.add),
                 r=[bkey, "x:%d:%d" % (f, t)], w=["x:%d:%d" % (f, t)])
        W.done(bo)
    barrier(C)
    A.reset()


def ffn_phase(C):
    P, W, banks, A, ys, xn, xT = C.P, C.W, C.banks, C.A, C.ys, C.xn, C.xT
    arn = dict(sq=A.bf16(KC * TL), rs=[A.f32(TL), A.f32(TL)])
    for t in range(NTL):
        rms_tile(C, lambda kc, t=t: xT[:, kc, t * TL:(t + 1) * TL], FFNG, lambda kc, t=t: xn[:, kc, t * TL:(t + 1) * TL], TL,
                 ["x:%d:%d" % (kc, t) for kc in range(KC)], ["xn:%d:%d" % (kc, t) for kc in range(KC)], arn, [banks[2], banks[3]], ["b2", "b3"], "f%d" % t)
    rl = [A.f32(TL), A.f32(TL)]
    act = ys
    i = 0
    for g in range(4):
        for c in range(KC):
            wu, ku, bu = W.get()
            for t in range(NTL):
                j = i % 2
                i += 1
                bk, bkey = banks[j], "b%d" % j
                for kc in range(KC):
                    P.op("pe", lambda e, kc=kc, t=t, bk=bk, wu=wu: e.matmul(bk[:, :], lhsT=wu[:, kc * 128:(kc + 1) * 128], rhs=xn[:, kc, t * TL:(t + 1) * TL],
                                                                            start=(kc == 0), stop=(kc == KC - 1)), r=[ku, "xn:%d:%d" % (kc, t)], w=[bkey])
                P.op("act", lambda e, j=j, bk=bk: e.activation(out=rl[j], in_=bk[:, :], func=AF.Relu), r=[bkey, "AE"], w=["rl%d" % j])
                P.op("dve", lambda e, j=j, c=c, t=t: e.tensor_tensor(out=act[:, c, t * TL:(t + 1) * TL], in0=rl[j], in1=rl[j], op=ALU.mult),
                     r=["rl%d" % j, "AE"], w=["ys:%d:%d" % (c, t)])
            W.done(bu)
        for f in range(KC):
            wd, kd, bd = W.get()
            for t in range(NTL):
                j = i % 4
                i += 1
                bk, bkey = banks[4 + j], "b%d" % (4 + j)
                for kc in range(KC):
                    P.op("pe", lambda e, kc=kc, t=t, bk=bk, wd=wd: e.matmul(bk[:, :], lhsT=wd[:, kc * 128:(kc + 1) * 128], rhs=act[:, kc, t * TL:(t + 1) * TL],
                                                                            start=(kc == 0), stop=(kc == KC - 1)), r=[kd, "ys:%d:%d" % (kc, t)], w=[bkey])
                P.op("dve", lambda e, f=f, t=t, bk=bk: e.tensor_tensor(out=xT[:, f, t * TL:(t + 1) * TL], in0=bk[:, :], in1=xT[:, f, t * TL:(t + 1) * TL], op=ALU.add),
                     r=[bkey, "x:%d:%d" % (f, t)], w=["x:%d:%d" % (f, t)])
            W.done(bd)
    barrier(C)
    A.reset()


def output_phase(C):
    P, banks, A, xT, spt = C.P, C.banks, C.A, C.xT, C.spt
    fins = []
    for kc in range(KC):
        fins.append(P.op("sp", lambda e, kc=kc: e.dma_start(out=C.xout[kc * 128:(kc + 1) * 128, :], in_=xT[:, kc, :]),
                         r=["x:%d:%d" % (kc, t) for t in range(NTL)], dma="xo%d" % kc))
    sq = A.bf16(KC * TL).rearrange("p (k t) -> p k t", k=KC)
    rs = A.f32(TL)
    yo = [A.f32(TL), A.f32(TL), A.f32(TL), A.f32(TL)]
    i = 0
    for t in range(NTL):
        ts = slice(t * TL, (t + 1) * TL)
        for kc in range(KC):
            P.op("act", lambda e, kc=kc, ts=ts: e.activation(out=sq[:, kc, :], in_=xT[:, kc, ts], func=AF.Square), r=["x:%d:%d" % (kc, t), "AE"], w=["sq%d" % kc])
        for kc in range(KC):
            P.op("pe", lambda e, kc=kc: e.matmul(banks[3][:, :], lhsT=C.cbf[:, ONES:ONES + 128], rhs=sq[:, kc, :], start=(kc == 0), stop=(kc == KC - 1)),
                 r=["sq%d" % kc, "cbf"], w=["b3"])
        P.op("act", lambda e: e.activation(out=rs, in_=banks[3][:, :], func=AF.Sqrt, bias=EPS, scale=1.0 / D), r=["b3", "AE"], w=["rs"])
        P.op("dve", lambda e: e.reciprocal(out=rs, in_=rs), r=["rs"], w=["rs"])
        for kc in range(KC):
            j = i % 4
            i += 1
            P.op("dve", lambda e, kc=kc, ts=ts, j=j: e.scalar_tensor_tensor(out=yo[j], in0=xT[:, kc, ts], scalar=spt[:, FING + kc:FING + kc + 1], in1=rs, op0=ALU.mult, op1=ALU.mult),
                 r=["x:%d:%d" % (kc, t), "rs", "spt", "AE"], w=["yo%d" % j])
            fins.append(P.op("sp", lambda e, kc=kc, ts=ts, j=j: e.dma_start(out=C.yout[kc * 128:(kc + 1) * 128, ts], in_=yo[j]), r=["yo%d" % j], dma="yo%d" % j))
    return fins


def build_layer_full(debug=False, upto=99):
    nc, st, P, C = build_layer(debug)
    if upto >= 2:
        attention_phase(C)
    if upto >= 3:
        conv_phase(C)
    if upto >= 4:
        pool_phase(C)
    fins = []
    if debug:
        fins.append(P.op("sp", lambda e: e.dma_start(out=C.dbg[:, :], in_=C.ys[:, :, :].rearrange("p k t -> p (k t)")),
                         r=["ys:%d:%d" % (k, t) for k in range(KC) for t in range(NTL)] + ["AE"], dma="dbg"))
        barrier(C)
    if upto >= 5:
        P.op("pool", lambda e: e.memset(C.scr[:, 0:8], 0.0), w=["xnh:%d" % kc for kc in range(KC)] + ["xnhfree"])
        merge_phase(C)
    if upto >= 6:
        ffn_phase(C)
    fins += output_phase(C)
    P.emit(final_waits=fins)
    return nc, st


_CACHE = {}


def _prog(name):
    if name not in _CACHE:
        if name == "sum":
            _CACHE[name] = build_sum()
        else:
            _CACHE[name] = build_layer_full(debug=False)
    return _CACHE[name][0]


def _ext_from_segments(segs, c):
    p = c % 4
    left = segs[c - 1][:, NT - HB:] if p > 0 else np.zeros((D, HB), np.float32)
    right = segs[c + 1][:, :HB] if p < 3 else np.zeros((D, HB), np.float32)
    return np.ascontiguousarray(np.concatenate([left, segs[c], right], axis=1))


def kernel(**inputs):
    inp = {k: np.asarray(v, dtype=np.float32) for k, v in inputs.items()}
    x = inp["x"]
    B, T, _ = x.shape
    cores = list(range(8))
    cst = const_block()
    pcs = [percore_consts(c) for c in cores]
    ropes = [rope_tables(c) for c in cores]
    segs = [np.ascontiguousarray(x[c // 4, (c % 4) * NT:(c % 4 + 1) * NT, :].T) for c in cores]
    y = None
    for l in range(2):
        sp = small_params(inp, l)
        ws_s = sum_stream(inp, l)
        res = run_bass_kernel_spmd(_prog("sum"), [{"xT": segs[c], "ws": ws_s, "sp": sp, "cst": cst} for c in cores], core_ids=cores)
        sums = [np.asarray(res.results[c]["so"]) for c in cores]
        ws_l = layer_stream(inp, l)
        in_maps = []
        for c in cores:
            b = c // 4
            gs = np.ascontiguousarray(np.stack([sums[b * 4 + r] for r in range(4)], 1).reshape(128, 4 * 130))
            in_maps.append({"xT": _ext_from_segments(segs, c), "ws": ws_l, "sp": sp, "cst": cst, "pc": pcs[c], "rope": ropes[c], "gsum": gs})
        res = run_bass_kernel_spmd(_prog("layer"), in_maps, core_ids=cores)
        segs = [np.asarray(res.results[c]["xo"]) for c in cores]
        y = [np.asarray(res.results[c]["yo"]) for c in cores]
    out = np.empty((B, T, D), np.float32)
    for c in cores:
        out[c // 4, (c % 4) * NT:(c % 4 + 1) * NT, :] = y[c].T
    return out
```

```python
import math
import numpy as np
import concourse.bass as bass
import concourse.mybir as mybir
from concourse.bass_utils import run_bass_kernel_spmd
from contextlib import ExitStack

F32 = mybir.dt.float32
BF16 = mybir.dt.bfloat16
ALU = mybir.AluOpType
AF = mybir.ActivationFunctionType
AX = mybir.AxisListType

D = 1024
KC = 8
NT = 2048
TL = 512
NTL = 4
HB = 128
NE = NT + 2 * HB
NB = 16
SEQ = 8192
EPS = 1e-6
LNC = -0.5 * math.log(32.0)

MIXG, FFNG, FING, CONVB, LNG, LNB, GLAG, SINK, PSC, CONVW, WUP, POOLW, NSP = 0, 8, 16, 24, 26, 28, 30, 32, 36, 38, 100, 356, 612
IDENT, ONES, ONESBLK, PMAT, TRIF, TRIB, NCBF = 0, 128, 256, 384, 512, 640, 768
BDM, HEADM, INVW, SCANM, NCST = 768, 1024, 1028, 1032, 1544
AMASK, PCORR, COEFF, COEFB, NPC = 0, 1152, 1184, 1196, 1208

IN_OFF = dict(glu_a=0, glu_g=256, gq=512, gk=640, gv=768, gr=1024, glr=1280, aq=1312, ak=1568, av=1696, pin=1824, gate=2080)


class Prog:
    ENGS = ("pe", "act", "dve", "pool", "sp")

    def __init__(self, nc, stack):
        self.nc = nc
        self.stack = stack
        self.ops = []
        self.last_w = {}
        self.readers = {}
        self.sems = {}
        self.semcnt = {}
        self.nsb = 0

    def sem(self, key):
        if key not in self.sems:
            self.sems[key] = self.stack.enter_context(self.nc.semaphore("s%d" % len(self.sems)))
            self.semcnt[key] = 0
        return self.sems[key]

    def sb(self, shape, dt, name=None):
        self.nsb += 1
        return self.stack.enter_context(self.nc.sbuf_tensor("s_" + (name or ("sb%d" % self.nsb)), list(shape), dt))

    def ps(self, shape, dt=F32, name=None):
        self.nsb += 1
        return self.stack.enter_context(self.nc.psum_tensor(name or ("ps%d" % self.nsb), list(shape), dt))

    def op(self, eng, fn, r=(), w=(), dma=None):
        idx = len(self.ops)
        deps = set()
        for k in r:
            x = self.last_w.get(k)
            if x is not None:
                deps.add(x)
        for k in w:
            x = self.last_w.get(k)
            if x is not None:
                deps.add(x)
            rl = self.readers.get(k)
            if rl:
                deps.update(rl)
        if dma is not None:
            skey = ("dma", dma)
            self.sem(skey)
            self.semcnt[skey] += 16
        else:
            skey = ("eng", eng)
            self.sem(skey)
            self.semcnt[skey] += 1
        done = (skey, self.semcnt[skey])
        self.ops.append((eng, fn, deps, done, dma is not None))
        for k in w:
            self.last_w[k] = idx
            self.readers[k] = []
        for k in r:
            self.readers.setdefault(k, []).append(idx)
        return idx

    def barrier(self, eng, fn, key):
        last = {}
        dmas = []
        start = getattr(self, "_bar_at", 0)
        for i in range(len(self.ops) - 1, -1, -1):
            o = self.ops[i]
            if o[4]:
                if i >= start:
                    dmas.append(i)
            elif o[0] not in last:
                last[o[0]] = i
            if i < start and len(last) >= 4:
                break
        idx = self.op(eng, fn, w=[key])
        e, f, deps, done, isd = self.ops[idx]
        deps.update(last.values())
        deps.update(dmas)
        deps.discard(idx)
        self._bar_at = idx
        return idx

    def emit(self, final_waits=()):
        nc = self.nc
        known = {e: {} for e in self.ENGS}
        per = {e: [] for e in self.ENGS}
        for (e, fn, deps, done, isdma) in self.ops:
            need = {}
            for d in deps:
                (de, _, _, (sk, val), ddma) = self.ops[d]
                if de == "pe" and e == "pe" and not ddma:
                    continue
                if need.get(sk, 0) < val:
                    need[sk] = val
            waits = []
            kn = known[e]
            for sk, val in need.items():
                if kn.get(sk, 0) >= val:
                    continue
                kn[sk] = val
                waits.append((sk, val))
            per[e].append((waits, fn, done, isdma))
        block = self.stack.enter_context(nc.Block())
        sems = self.sems

        def run(engobj, lst, extra=()):
            for waits, fn, (sk, val), isdma in lst:
                for wk, wv in waits:
                    engobj.wait_ge(sems[wk], wv)
                ins = fn(engobj)
                ins.then_inc(sems[sk], 16 if isdma else 1)
            for wk, wv in extra:
                engobj.wait_ge(sems[wk], wv)

        fin = [self.ops[i][3] for i in final_waits]

        @block.tensor
        def _(e):
            run(e, per["pe"])

        @block.scalar
        def _(e):
            run(e, per["act"])

        @block.vector
        def _(e):
            run(e, per["dve"])

        @block.gpsimd
        def _(e):
            run(e, per["pool"])

        @block.sync
        def _(e):
            run(e, per["sp"], fin)


class Arena:
    def __init__(self, t, nwords):
        self.t = t
        self.n = nwords
        self.off = 0

    def reset(self):
        self.off = 0

    def f32(self, n):
        n8 = (n + 7) // 8 * 8
        assert self.off + n8 <= self.n, ("arena overflow", self.off, n8, self.n)
        v = self.t[:, self.off:self.off + n]
        self.off += n8
        return v

    def bf16(self, n):
        w = (n + 1) // 2
        w = (w + 7) // 8 * 8
        assert self.off + w <= self.n, ("arena overflow", self.off, w, self.n)
        v = self.t[:, self.off:self.off + w].bitcast(BF16)[:, 0:n]
        self.off += w
        return v


class WStream:
    def __init__(self, P, dram, nch, wst, wbf, look=1):
        self.P, self.dram, self.nch, self.wst, self.wbf, self.look = P, dram, nch, wst, wbf, look
        self.issued = 0
        self.nxt = 0
        self.live = {}

    def _issue(self, i):
        s = i % len(self.wst)
        b = i % len(self.wbf)
        assert self.live.get(b) is None, ("weight slot still live", i, b, self.live.get(b))
        self.live[b] = i
        wst, wbf, dram = self.wst, self.wbf, self.dram
        self.P.op("sp", lambda e: e.dma_start(out=wst[s][:], in_=dram[i]), w=["wst%d" % s], dma="wst%d" % s)
        self.P.op("pool", lambda e: e.tensor_copy(out=wbf[b][:], in_=wst[s][:]), r=["wst%d" % s], w=["wbf%d" % b])

    def get(self):
        i = self.nxt
        self.nxt += 1
        while self.issued < min(self.nch, i + 1 + self.look):
            self._issue(self.issued)
            self.issued += 1
        b = i % len(self.wbf)
        return self.wbf[b], "wbf%d" % b, b

    def done(self, b):
        self.live[b] = None


class Ctx:
    pass


def setup_common(nc, P, C, nch, look=1, nbf=6):
    C.nc, C.P = nc, P
    C.ws_d = nc.dram_tensor("ws", [nch, 128, 1024], F32, kind="ExternalInput").ap()
    C.sp_d = nc.dram_tensor("sp", [128, NSP], F32, kind="ExternalInput").ap()
    C.cst_d = nc.dram_tensor("cst", [128, NCST], F32, kind="ExternalInput").ap()
    C.banks = [P.ps([128, 512], F32, name="bank%d" % i) for i in range(8)]
    C.wst = [P.sb([128, 1024], F32, name="wst%d" % i) for i in range(2)]
    C.wbf = [P.sb([128, 1024], BF16, name="wbf%d" % i) for i in range(nbf)]
    C.W = WStream(P, C.ws_d, nch, C.wst, C.wbf, look=look)
    C.spt = P.sb([128, NSP], F32, name="spt")
    C.cbf = P.sb([128, NCBF], BF16, name="cbf")
    C.cf = P.sb([128, NCST - NCBF], F32, name="cf")
    C.wupb = P.sb([128, 256], BF16, name="wupb")
    P.op("sp", lambda e: e.dma_start(out=C.spt[:], in_=C.sp_d[:, :]), w=["spt"], dma="spt")
    P.op("sp", lambda e: e.dma_start(out=C.cf[:], in_=C.cst_d[:, NCBF:NCST]), w=["cf"], dma="cf")
    P.op("pool", lambda e: e.tensor_copy(out=C.wupb[:], in_=C.spt[:, WUP:WUP + 256]), r=["spt"], w=["wupb"])


def bfview(bank):
    return bank[:].bitcast(BF16)


def rms_tile(C, xsrc, gcol, dst, ntok, rkeys, wkeys, ar, bank, bkey, tag):
    P = C.P
    sq = ar["sq"][:, 0:KC * ntok].rearrange("p (k t) -> p k t", k=KC)
    ar["cnt"] = ar.get("cnt", 0) + 1
    par = ar["cnt"] % 2
    rsl = ar["rs"] if isinstance(ar["rs"], list) else [ar["rs"]]
    rs = rsl[par % len(rsl)][:, 0:ntok]
    rskey = "rs%d" % (par % len(rsl))
    if isinstance(bank, list):
        bkey = bkey[par % len(bank)]
        bank = bank[par % len(bank)]
    for kc in range(KC):
        P.op("act", lambda e, kc=kc: e.activation(out=sq[:, kc, :], in_=xsrc(kc), func=AF.Square),
             r=rkeys + ["AE"], w=["sq%d" % kc])
    for kc in range(KC):
        P.op("pe", lambda e, kc=kc: e.matmul(bank[:, 0:ntok], lhsT=C.cbf[:, ONES:ONES + 128], rhs=sq[:, kc, :],
                                              start=(kc == 0), stop=(kc == KC - 1)),
             r=["sq%d" % kc, "cbf"], w=[bkey])
    P.op("act", lambda e: e.activation(out=rs, in_=bank[:, 0:ntok], func=AF.Sqrt, bias=EPS, scale=1.0 / D),
         r=[bkey, "AE"], w=[rskey])
    P.op("dve", lambda e: e.reciprocal(out=rs, in_=rs), r=[rskey], w=[rskey])
    for kc in range(KC):
        P.op("dve", lambda e, kc=kc: e.scalar_tensor_tensor(out=dst(kc), in0=xsrc(kc), scalar=C.spt[:, gcol + kc:gcol + kc + 1],
                                                             in1=rs, op0=ALU.mult, op1=ALU.mult),
             r=rkeys + [rskey, "spt"], w=[wkeys[kc]])


def barrier(C):
    C.P.barrier("pool", lambda e: e.memset(C.scr[:, 0:8], 0.0), "AE")


def gla_phase1(C, xn_tile, xn_blk, ar, with_q, kt_f, kt_b, qt_f, qt_b, v_sb, U, etf, etb):
    P, W, banks = C.P, C.W, C.banks
    w_lr, k_lr, b_lr = W.get()
    if with_q:
        w_q, k_q, b_q = W.get()
    w_k, k_k, b_k = W.get()
    w_v0, k_v0, b_v0 = W.get()
    w_v1, k_v1, b_v1 = W.get()

    def v_pair(n2):
        bk = banks[2 + (n2 % 2)]
        bkey = "b%d" % (2 + (n2 % 2))
        for j in range(2):
            n = 2 * n2 + j
            for half, (wv, kv) in enumerate(((w_v0, k_v0), (w_v1, k_v1))):
                for kc in range(KC):
                    P.op("pe", lambda e, kc=kc, n=n, j=j, half=half, wv=wv, bk=bk: e.matmul(
                        bk[:, j * 256 + half * 128:j * 256 + half * 128 + 128], lhsT=xn_blk(kc, n), rhs=wv[:, kc * 128:(kc + 1) * 128],
                        start=(kc == 0), stop=(kc == KC - 1)),
                         r=[kv, "xn:%d:%d" % (kc, n // 4)], w=[bkey])
        P.op("act", lambda e, n2=n2, bk=bk: e.activation(out=v_sb[:, 2 * n2:2 * n2 + 2, :], in_=bk[:, :].rearrange("p (a c) -> p a c", a=2), func=AF.Copy),
             r=[bkey, "AE"], w=["v:%d" % n2])

    glra = ar["glra"]
    t1 = [ar["t1f"], ar["t1b"]]
    t2 = [ar["t2f"], ar["t2b"]]
    t3 = [ar["t3f"], ar["t3b"]]
    t4 = [ar["t4f"], ar["t4b"]]
    t5 = ar["t5"]
    P.op("pool", lambda e: e.memset(glra[32:33, :], 1.0), r=["AE"], w=["glra1"])
    for t in range(NTL):
        sl = slice(t * TL, (t + 1) * TL)
        for kc in range(KC):
            P.op("pe", lambda e, kc=kc, t=t: e.matmul(banks[6][0:32, :], lhsT=w_lr[:, kc * 128:kc * 128 + 32], rhs=xn_tile(kc, t),
                                                       start=(kc == 0), stop=(kc == KC - 1)),
                 r=[k_lr, "xn:%d:%d" % (kc, t)], w=["b6"])
        P.op("act", lambda e: e.activation(out=glra[0:32, :], in_=banks[6][0:32, :], func=AF.Copy), r=["b6", "AE"], w=["glra"])
        for d in range(2):
            bk = banks[4 + d]
            P.op("pe", lambda e, d=d, bk=bk: e.matmul(bk[:, :], lhsT=C.wupb[0:33, d * 128:(d + 1) * 128], rhs=glra[0:33, :],
                                                       start=True, stop=True),
                 r=["glra", "glra1", "wupb"], w=["b%d" % (4 + d)])
            P.op("act", lambda e, d=d, bk=bk: e.activation(out=t1[d], in_=bk[:, :], func=AF.Exp, scale=-1.0),
                 r=["b%d" % (4 + d), "AE"], w=["t1%d" % d])
            P.op("act", lambda e, d=d: e.activation(out=t1[d], in_=t1[d], func=AF.Ln, bias=1.0), r=["t1%d" % d], w=["t1%d" % d])
            P.op("dve", lambda e, d=d: e.tensor_tensor_scan(out=t2[d], data0=C.cf[:, SCANM - NCBF:SCANM - NCBF + 512], data1=t1[d],
                                                             initial=0.0, op0=ALU.mult, op1=ALU.add),
                 r=["t1%d" % d, "cf", "AE"], w=["t2%d" % d])
            et = etf if d == 0 else etb
            c3 = t2[d].rearrange("p (a c) -> p a c", a=4)
            P.op("act", lambda e, et=et, c3=c3, t=t: e.activation(out=et[:, 4 * t:4 * t + 4], in_=c3[:, :, 127], func=AF.Exp, scale=-1.0 / 16),
                 r=["t2%d" % d, "AE"], w=["et%d:%d" % (d, t)])
            if d == 0:
                src = t2[0]
            else:
                P.op("dve", lambda e: e.tensor_tensor(out=t5, in0=t1[1], in1=t2[1], op=ALU.subtract), r=["t11", "t21", "AE"], w=["t5"])
                t53 = t5.rearrange("p (a c) -> p a c", a=4)
                P.op("dve", lambda e, t53=t53, c3=c3: e.tensor_tensor(out=t53, in0=t53, in1=c3[:, :, 127:128].to_broadcast([128, 4, 128]), op=ALU.add),
                     r=["t5", "t21"], w=["t5"])
                src = t5
            skey = "t20" if d == 0 else "t5"
            if with_q:
                P.op("act", lambda e, d=d, src=src: e.activation(out=t3[d], in_=src, func=AF.Exp, scale=-1.0 / 16, bias=LNC),
                     r=[skey, "AE"], w=["t3%d" % d])
            P.op("act", lambda e, d=d, src=src: e.activation(out=t4[d], in_=src, func=AF.Exp, scale=1.0 / 16), r=[skey, "AE"], w=["t4%d" % d])
        v_pair(2 * t)
        v_pair(2 * t + 1)
        if with_q:
            for kc in range(KC):
                P.op("pe", lambda e, kc=kc, t=t: e.matmul(banks[0][:, :], lhsT=w_q[:, kc * 128:(kc + 1) * 128], rhs=xn_tile(kc, t),
                                                           start=(kc == 0), stop=(kc == KC - 1)),
                     r=[k_q, "xn:%d:%d" % (kc, t)], w=["b0"])
            for d, qt in ((0, qt_f), (1, qt_b)):
                P.op("dve", lambda e, d=d, qt=qt, sl=sl: e.tensor_tensor(out=qt[:, sl], in0=banks[0][:, :], in1=t3[d], op=ALU.mult),
                     r=["b0", "t3%d" % d, "AE"], w=["qt%d:%d" % (d, t)])
        for kc in range(KC):
            P.op("pe", lambda e, kc=kc, t=t: e.matmul(banks[1][:, :], lhsT=w_k[:, kc * 128:(kc + 1) * 128], rhs=xn_tile(kc, t),
                                                       start=(kc == 0), stop=(kc == KC - 1)),
                 r=[k_k, "xn:%d:%d" % (kc, t)], w=["b1"])
        for d, kt in ((0, kt_f), (1, kt_b)):
            P.op("dve", lambda e, d=d, kt=kt, sl=sl: e.tensor_tensor(out=kt[:, sl], in0=banks[1][:, :], in1=t4[d], op=ALU.mult),
                 r=["b1", "t4%d" % d, "AE"], w=["kt%d:%d" % (d, t)])
    W.done(b_lr)
    W.done(b_k)
    if with_q:
        W.done(b_q)
    W.done(b_v0)
    W.done(b_v1)
    kh = [[ar["kh0"], ar["kh1"]], [ar["kh2"], ar["kh3"]]]
    khs = [[ar["khs0"], ar["khs1"]], [ar["khs2"], ar["khs3"]]]
    tmpU = ar["tmpU"]
    P.op("pool", lambda e: e.memset(U.rearrange("p a n c -> p (a n c)"), 0.0), r=["AE"], w=["Uz"])
    bTs = [bfview(banks[4]), bfview(banks[5])]
    dirs = ((kt_f, etf), (kt_b, etb))

    def u_a(n):
        pb = n % 2
        bs = slice(n * 128, (n + 1) * 128)
        for d, (kt, et) in enumerate(dirs):
            P.op("dve", lambda e, d=d, kt=kt, et=et: e.tensor_scalar(out=kh[pb][d], in0=kt[:, bs], scalar1=et[:, n:n + 1], scalar2=None, op0=ALU.mult),
                 r=["kt%d:%d" % (d, n // 4), "et%d:%d" % (d, n // 4), "AE"], w=["kh%d:%d" % (pb, d)])
        for d in range(2):
            P.op("pe", lambda e, d=d: e.transpose(out=bTs[d][:, 0:128], in_=kh[pb][d], identity=C.cbf[:, IDENT:IDENT + 128]),
                 r=["kh%d:%d" % (pb, d), "cbf"], w=["b%d" % (4 + d)])
            P.op("act", lambda e, d=d: e.activation(out=khs[pb][d], in_=bTs[d][:, 0:128], func=AF.Copy), r=["b%d" % (4 + d), "AE"], w=["khs%d:%d" % (pb, d)])

    def u_b(n):
        pb = n % 2
        ub, ukey = banks[6 + pb], "b%d" % (6 + pb)
        for d in range(2):
            P.op("pe", lambda e, d=d: e.matmul(ub[:, d * 256:(d + 1) * 256], lhsT=khs[pb][d], rhs=v_sb[:, n, :], start=True, stop=True),
                 r=["khs%d:%d" % (pb, d), "v:%d" % (n // 2)], w=[ukey])
        P.op("dve", lambda e: e.tensor_tensor(out=tmpU.rearrange("p (a c) -> p a c", a=2), in0=ub[:, :].rearrange("p (a c) -> p a c", a=2),
                                              in1=C.cf[:, BDM - NCBF:BDM - NCBF + 256].unsqueeze(1).to_broadcast([128, 2, 256]), op=ALU.mult),
             r=[ukey, "cf", "AE"], w=["tmpU"])
        for d in range(2):
            P.op("dve", lambda e, d=d: e.tensor_reduce(out=U[:, d, n, 0:64], in_=tmpU[:, d * 256:(d + 1) * 256].rearrange("p (h c) -> p c h", h=4),
                                                       axis=AX.X, op=ALU.add),
                 r=["tmpU", "Uz", "AE"], w=["U:%d:%d" % (d, n)])

    u_a(0)
    for n in range(NB):
        if n + 1 < NB:
            u_a(n + 1)
        u_b(n)


def load_consts(C, ar):
    P = C.P
    tmp = ar.f32(NCBF)
    P.op("sp", lambda e: e.dma_start(out=tmp, in_=C.cst_d[:, 0:NCBF]), r=["AE"], w=["cst_tmp"], dma="cst")
    P.op("dve", lambda e: e.tensor_copy(out=C.cbf[:], in_=tmp), r=["cst_tmp"], w=["cbf"])
    C.scr = C.P.sb([128, 8], F32, name="scr")


SUM_CH = 4


def build_sum():
    nc = bass.Bass("TRN2", target_bir_lowering=False)
    st = ExitStack()
    P = Prog(nc, st)
    C = Ctx()
    setup_common(nc, P, C, SUM_CH)
    xin = nc.dram_tensor("xT", [D, NT], F32, kind="ExternalInput").ap()
    sout = nc.dram_tensor("so", [128, 2 * 65], F32, kind="ExternalOutput").ap()
    xn = P.sb([128, KC, NT], BF16, name="xn")
    ysb = P.sb([128, 4, NT], BF16, name="ysb")
    art = P.sb([128, 16384], F32, name="arena")
    A = Arena(art, 16384)
    load_consts(C, A)
    barrier(C)
    A.reset()
    xt = [A.f32(KC * TL), A.f32(KC * TL)]
    ar = dict(sq=A.bf16(KC * TL), rs=[A.f32(TL), A.f32(TL)])
    xv = xin.rearrange("(k p) t -> p k t", p=128)
    for t in range(NTL):
        b = t % 2
        xb = xt[b].rearrange("p (k t) -> p k t", k=KC)
        P.op("sp" if t % 2 == 0 else "pool", lambda e, xb=xb, t=t: e.dma_start(out=xb, in_=xv[:, :, t * TL:(t + 1) * TL]), r=["AE"], w=["xt%d" % b], dma="xt%d" % b)
        rms_tile(C, lambda kc, xb=xb: xb[:, kc, :], MIXG, lambda kc, t=t: xn[:, kc, t * TL:(t + 1) * TL], TL,
                 ["xt%d" % b], ["xn:%d:%d" % (kc, t) for kc in range(KC)], ar, [C.banks[2], C.banks[3]], ["b2", "b3"], "s%d" % t)
    barrier(C)
    A.reset()
    g = dict(glra=A.bf16(TL), t1f=A.f32(TL), t1b=A.f32(TL), t2f=A.f32(TL), t2b=A.f32(TL), t3f=None, t3b=None,
             t4f=A.f32(TL), t4b=A.f32(TL), t5=A.f32(TL), kh0=A.bf16(128), kh1=A.bf16(128), khs0=A.bf16(128), khs1=A.bf16(128), kh2=A.bf16(128), kh3=A.bf16(128), khs2=A.bf16(128), khs3=A.bf16(128),
             tmpU=A.f32(512))
    U = A.f32(2 * 16 * 65).rearrange("p (a n c) -> p a n c", a=2, n=16)
    etf = A.f32(16)
    etb = A.f32(16)
    Sf = A.f32(65)
    Sb = A.f32(65)
    v_sb = ysb[:, 2:4, :].rearrange("p c t -> p (c t)").rearrange("p (n c) -> p n c", n=16)
    gla_phase1(C, lambda kc, t: xn[:, kc, t * TL:(t + 1) * TL], lambda kc, n: xn[:, kc, n * 128:(n + 1) * 128], g, False,
               ysb[:, 0, :], ysb[:, 1, :], None, None, v_sb, U, etf, etb)
    ukeys = lambda d: ["U:%d:%d" % (d, n) for n in range(NB)]
    for S, d in ((Sf, 0), (Sb, 1)):
        P.op("pool", lambda e, S=S: e.memset(S[:, 0:64], 0.0), r=["AE"], w=["S%d" % d])
        P.op("pool", lambda e, S=S: e.memset(S[:, 64:65], 1.0), r=["AE"], w=["S%db" % d])
    for n in range(NB):
        P.op("dve", lambda e, n=n: e.scalar_tensor_tensor(out=Sf, in0=Sf, scalar=etf[:, n:n + 1], in1=U[:, 0, n, :], op0=ALU.mult, op1=ALU.add),
             r=["S0", "S0b", "U:0:%d" % n, "Uz", "et0:%d" % (n // 4), "AE"], w=["S0"])
    for n in range(NB - 1, -1, -1):
        P.op("dve", lambda e, n=n: e.scalar_tensor_tensor(out=Sb, in0=Sb, scalar=etb[:, n:n + 1], in1=U[:, 1, n, :], op0=ALU.mult, op1=ALU.add),
             r=["S1", "S1b", "U:1:%d" % n, "Uz", "et1:%d" % (n // 4), "AE"], w=["S1"])
    o1 = P.op("sp", lambda e: e.dma_start(out=sout[:, 0:65], in_=Sf), r=["S0"], dma="o1")
    o2 = P.op("sp", lambda e: e.dma_start(out=sout[:, 65:130], in_=Sb), r=["S1"], dma="o2")
    P.emit(final_waits=[o1, o2])
    return nc, st


def _chunk_cols(Wm, cols):
    sub = Wm[:, cols]
    if sub.shape[1] < 128:
        sub = np.concatenate([sub, np.zeros((sub.shape[0], 128 - sub.shape[1]), np.float32)], 1)
    return np.ascontiguousarray(sub.reshape(KC, 128, 128).transpose(1, 0, 2)).reshape(128, 1024)


def win_chunk(w_in_l, name):
    o = IN_OFF
    if name == "glr":
        cols = np.arange(o["glr"], o["glr"] + 32)
    elif name in ("gq", "gk", "ak", "av"):
        cols = np.arange(o[name], o[name] + 128)
    elif name in ("gv0", "gv1", "gr0", "gr1", "pin0", "pin1"):
        b = o[name[:-1]] + 128 * int(name[-1])
        cols = np.arange(b, b + 128)
    elif name in ("A0", "A1"):
        b = o["glu_a"] + 128 * int(name[-1])
        cols = np.arange(b, b + 128)
    elif name in ("G0", "G1"):
        b = o["glu_g"] + 128 * int(name[-1])
        cols = np.arange(b, b + 128)
    elif name == "QA":
        cols = np.concatenate([np.arange(o["aq"], o["aq"] + 64), np.arange(o["aq"] + 128, o["aq"] + 192)])
    elif name == "QB":
        cols = np.concatenate([np.arange(o["aq"] + 64, o["aq"] + 128), np.arange(o["aq"] + 192, o["aq"] + 256)])
    else:
        raise KeyError(name)
    return _chunk_cols(w_in_l, cols)


def sum_stream(inp, l):
    w = inp["w_in"][l]
    return np.stack([win_chunk(w, n) for n in ("glr", "gk", "gv0", "gv1")], 0)


def small_params(inp, l):
    sp = np.zeros((128, NSP), np.float32)
    sp[:, MIXG:MIXG + 8] = inp["norm_mix_g"][l].reshape(8, 128).T
    sp[:, FFNG:FFNG + 8] = inp["norm_ffn_g"][l].reshape(8, 128).T
    sp[:, FING:FING + 8] = inp["final_norm_g"].reshape(8, 128).T
    sp[:, CONVB:CONVB + 2] = inp["conv_b"][l].reshape(2, 128).T
    sp[:, LNG:LNG + 2] = inp["conv_ln_g"][l].reshape(2, 128).T
    sp[:, LNB:LNB + 2] = inp["conv_ln_b"][l].reshape(2, 128).T
    sp[:, GLAG:GLAG + 2] = inp["gla_norm_g"][l].reshape(2, 128).T
    sp[:, SINK:SINK + 4] = inp["attn_sink"][l][None, :]
    sp[:, PSC:PSC + 2] = inp["pool_scale"][l].reshape(2, 128).T
    cw = inp["conv_w"][l]
    for c in range(2):
        sp[:, CONVW + 31 * c:CONVW + 31 * (c + 1)] = cw[:, c * 128:(c + 1) * 128].T
    wu = inp["gla_w_up"][l]
    bu = inp["gla_b_up"][l]
    sp[0:16, WUP:WUP + 128] = wu[0]
    sp[16:32, WUP + 128:WUP + 256] = wu[1]
    sp[32, WUP:WUP + 128] = bu[0]
    sp[32, WUP + 128:WUP + 256] = bu[1]
    pw = inp["pool_w"][l]
    for c in range(2):
        for gg in range(2):
            g = 2 * c + gg
            sp[gg * 64:(gg + 1) * 64, POOLW + c * 128 + gg * 64:POOLW + c * 128 + (gg + 1) * 64] = pw[g]
    return sp


def const_block():
    c = np.zeros((128, NCST), np.float32)
    i = np.arange(128)
    c[:, IDENT:IDENT + 128] = np.eye(128, dtype=np.float32)
    c[:, ONES:ONES + 128] = 1.0
    c[:, ONESBLK:ONESBLK + 128] = (i[:, None] // 64 == i[None, :] // 64)
    pm = np.zeros((128, 128), np.float32)
    for hb in (0, 64):
        for j in range(8):
            pm[hb + j + 8, hb + j] = -1.0
            pm[hb + j, hb + j + 8] = 1.0
    c[:, PMAT:PMAT + 128] = pm
    c[:, TRIF:TRIF + 128] = (i[:, None] <= i[None, :])
    c[:, TRIB:TRIB + 128] = (i[:, None] >= i[None, :])
    c[:, BDM:BDM + 256] = (i[:, None] // 32 == np.arange(256)[None, :] // 64)
    c[:, HEADM:HEADM + 4] = (i[:, None] // 32 == np.arange(4)[None, :])
    c[:, INVW] = np.where(i < 64, 1.0 / 2, 1.0 / 4)
    c[:, INVW + 1] = np.where(i < 64, 1.0 / 8, 1.0 / 16)
    sm = np.ones(512, np.float32)
    sm[::128] = 0.0
    c[:, SCANM:SCANM + 512] = sm[None, :]
    return c


LAYER_STREAM = (["glr", "gq", "gk", "gv0", "gv1", "gr0", "gr1", "QA", "QB", "ak", "av", "A0", "G0", "A1", "G1", "pin0", "pin1"])
N_INPROJ = len(LAYER_STREAM)
LAYER_NCH = N_INPROJ + 8 * 5 + 8 + 4 * 16


def layer_stream(inp, l):
    w = inp["w_in"][l]
    ch = [win_chunk(w, n) for n in LAYER_STREAM]
    wb = inp["w_branch"][l]
    for f in range(8):
        blk = np.stack([wb[n][:, f * 128:(f + 1) * 128].reshape(2, 128, 128).transpose(1, 0, 2) for n in range(4)], 1)
        ch.append(np.ascontiguousarray(blk).reshape(128, 1024))
        for n in range(4):
            b = IN_OFF["gate"] + n * 1024 + f * 128
            ch.append(_chunk_cols(w, np.arange(b, b + 128)))
    wo = inp["w_out"][l]
    for f in range(8):
        ch.append(_chunk_cols(wo, np.arange(f * 128, (f + 1) * 128)))
    wu = inp["w_ffn_up"][l]
    wd = inp["w_ffn_down"][l]
    for g in range(4):
        for c in range(8):
            cc = g * 8 + c
            ch.append(_chunk_cols(wu, np.arange(cc * 128, (cc + 1) * 128)))
        for f in range(8):
            ch.append(_chunk_cols(wd[g * 1024:(g + 1) * 1024], np.arange(f * 128, (f + 1) * 128)))
    out = np.stack(ch, 0)
    assert out.shape[0] == LAYER_NCH
    return out


def percore_consts(core):
    p = core % 4
    pc = np.zeros((128, NPC), np.float32)
    q = np.arange(128)[:, None]
    s = np.arange(384)[None, :]
    base = ((s >= q) & (s <= q + 256))
    for v in range(3):
        m = base.copy()
        if v == 0 and p == 0:
            m &= (s >= 128)
        if v == 2 and p == 3:
            m &= (s < 256)
        pc[:, AMASK + v * 384:AMASK + (v + 1) * 384] = m
    i = np.arange(128)
    wid = [np.where(i < 64, 2, 4), np.where(i < 64, 8, 16)]
    corr = np.ones((128, 2, 16), np.float32)
    for c in range(2):
        w = wid[c].astype(np.float64)
        for j in range(8):
            if p == 0:
                pos = j
                lo = np.maximum(pos - w / 2, 0)
                hi = pos + w / 2
                corr[:, c, j] = w / (hi - lo)
            if p == 3:
                pos = SEQ - 8 + j
                lo = pos - w / 2
                hi = np.minimum(pos + w / 2, SEQ)
                corr[:, c, 8 + j] = w / (hi - lo)
    pc[:, PCORR:PCORR + 32] = corr.reshape(128, 32)
    cf = np.zeros((4, 3), np.float32)
    cb = np.zeros((4, 3), np.float32)
    for r in range(4):
        cf[r] = (1, 0, 1) if r < p else (0, 1, 0)
        cb[r] = (1, 0, 1) if r > p else (0, 1, 0)
    pc[:, COEFF:COEFF + 12] = cf.reshape(1, 12)
    pc[:, COEFB:COEFB + 12] = cb.reshape(1, 12)
    return pc


def rope_tables(core):
    p = core % 4
    pos = (p * NT - HB + np.arange(NE)).astype(np.float32)
    inv = (1.0 / (np.float32(500000.0) ** (np.arange(0, 16, 2, dtype=np.float32) / np.float32(16)))).astype(np.float32)
    ang = pos[:, None] * inv[None, :]
    cs = np.cos(ang).astype(np.float32).T
    sn = np.sin(ang).astype(np.float32).T
    Cm = np.ones((128, NE), np.float32)
    Sm = np.zeros((128, NE), np.float32)
    for hb in (0, 64):
        Cm[hb:hb + 8] = cs
        Cm[hb + 8:hb + 16] = cs
        Sm[hb:hb + 8] = sn
        Sm[hb + 8:hb + 16] = sn
    return np.stack([Cm, Sm], 0)


def build_layer(debug=False):
    nc = bass.Bass("TRN2", target_bir_lowering=False)
    st = ExitStack()
    P = Prog(nc, st)
    C = Ctx()
    setup_common(nc, P, C, LAYER_NCH)
    W, banks = C.W, C.banks
    xin = nc.dram_tensor("xT", [D, NE], F32, kind="ExternalInput").ap()
    pc_d = nc.dram_tensor("pc", [128, NPC], F32, kind="ExternalInput").ap()
    rope_d = nc.dram_tensor("rope", [2, 128, NE], F32, kind="ExternalInput").ap()
    gsum_d = nc.dram_tensor("gsum", [128, 4 * 130], F32, kind="ExternalInput").ap()
    xout = nc.dram_tensor("xo", [D, NT], F32, kind="ExternalOutput").ap()
    yout = nc.dram_tensor("yo", [D, NT], F32, kind="ExternalOutput").ap()
    if debug:
        dbg = nc.dram_tensor("dbg", [128, KC * NT], BF16, kind="ExternalOutput").ap()

    xT = P.sb([128, KC, NT], F32, name="xT")
    xn = P.sb([128, KC, NT], BF16, name="xn")
    xnh = P.sb([128, KC, 2 * HB], BF16, name="xnh")
    ys = P.sb([128, KC, NT], BF16, name="ys")
    pcf = P.sb([128, NPC - AMASK - 1152], F32, name="pcf")
    amb = P.sb([128, 1152], BF16, name="amb")
    AW = 11328
    art = P.sb([128, AW], F32, name="arena")
    A = Arena(art, AW)
    cbf, cf, spt = C.cbf, C.cf, C.spt
    CF = lambda off, n: cf[:, off - NCBF:off - NCBF + n]

    load_consts(C, A)
    tmpm = A.f32(1152)
    P.op("sp", lambda e: e.dma_start(out=tmpm, in_=pc_d[:, AMASK:AMASK + 1152]), r=["AE"], w=["tmpm"], dma="pc1")
    P.op("dve", lambda e: e.tensor_copy(out=amb[:], in_=tmpm), r=["tmpm"], w=["amb"])
    P.op("sp", lambda e: e.dma_start(out=pcf[:], in_=pc_d[:, PCORR:NPC]), w=["pcf"], dma="pc2")
    PCF = lambda off, n: pcf[:, off - PCORR:off - PCORR + n]
    barrier(C)
    A.reset()

    xv = xin.rearrange("(k p) t -> p k t", p=128)
    for kc in range(KC):
        P.op("sp" if kc % 2 == 0 else "pool", lambda e, kc=kc: e.dma_start(out=xT[:, kc, :], in_=xin[kc * 128:(kc + 1) * 128, HB:HB + NT]),
             w=["x:%d:%d" % (kc, t) for t in range(NTL)], dma="xl%d" % kc)
    xh = A.f32(KC * 2 * HB).rearrange("p (k t) -> p k t", k=KC)
    P.op("sp", lambda e: e.dma_start(out=xh[:, :, 0:HB], in_=xv[:, :, 0:HB]), r=["AE"], w=["xh0"], dma="xh0")
    P.op("sp", lambda e: e.dma_start(out=xh[:, :, HB:2 * HB], in_=xv[:, :, HB + NT:NE]), r=["AE"], w=["xh1"], dma="xh1")
    arn = dict(sq=A.bf16(KC * TL), rs=[A.f32(TL), A.f32(TL)])

    def norm_all(gcol):
        for t in range(NTL):
            rms_tile(C, lambda kc, t=t: xT[:, kc, t * TL:(t + 1) * TL], gcol, lambda kc, t=t: xn[:, kc, t * TL:(t + 1) * TL], TL,
                     ["x:%d:%d" % (kc, t) for kc in range(KC)], ["xn:%d:%d" % (kc, t) for kc in range(KC)], arn, [banks[2], banks[3]], ["b2", "b3"], "n%d" % t)

    norm_all(MIXG)
    rms_tile(C, lambda kc: xh[:, kc, :], MIXG, lambda kc: xnh[:, kc, :], 2 * HB, ["xh0", "xh1"], ["xnh:%d" % kc for kc in range(KC)], arn, [banks[2], banks[3]], ["b2", "b3"], "nh")
    barrier(C)
    A.reset()

    xn_tile = lambda kc, t: xn[:, kc, t * TL:(t + 1) * TL]
    xn_blk = lambda kc, n: xn[:, kc, n * 128:(n + 1) * 128]
    ext_ranges = [(lambda kc, t=t: xn[:, kc, t * TL:(t + 1) * TL], HB + t * TL, TL, (lambda kc, t=t: "xn:%d:%d" % (kc, t))) for t in range(NTL)]
    ext_ranges.append((lambda kc: xnh[:, kc, 0:HB], 0, HB, lambda kc: "xnh:%d" % kc))
    ext_ranges.append((lambda kc: xnh[:, kc, HB:2 * HB], HB + NT, HB, lambda kc: "xnh:%d" % kc))

    qt_f, qt_b, kt_f, kt_b = ys[:, 0, :], ys[:, 1, :], ys[:, 4, :], ys[:, 5, :]
    v_sb = ys[:, 6:8, :].rearrange("p c t -> p (c t)").rearrange("p (n c) -> p n c", n=16)
    U = A.f32(2 * 16 * 65).rearrange("p (a n c) -> p a n c", a=2, n=16)
    etf = A.f32(16)
    etb = A.f32(16)
    G = A.f32(4 * 130)
    Sst = A.f32(17 * 64 * 2).rearrange("p (d n c) -> p d n c", d=2, n=17)
    wr = A.f32(8)
    Tr = A.f32(4 * 64)
    amark = A.off
    g = dict(glra=A.bf16(TL), t1f=A.f32(TL), t1b=A.f32(TL), t2f=A.f32(TL), t2b=A.f32(TL), t3f=A.f32(TL), t3b=A.f32(TL),
             t4f=A.f32(TL), t4b=A.f32(TL), t5=A.f32(TL), kh0=A.bf16(128), kh1=A.bf16(128), khs0=A.bf16(128), khs1=A.bf16(128), kh2=A.bf16(128), kh3=A.bf16(128), khs2=A.bf16(128), khs3=A.bf16(128),
             tmpU=A.f32(512))
    gla_phase1(C, xn_tile, xn_blk, g, True, kt_f, kt_b, qt_f, qt_b, v_sb, U, etf, etb)
    G4 = G.rearrange("p (r d c) -> p r d c", r=4, d=2)
    P.op("sp", lambda e: e.dma_start(out=G, in_=gsum_d[:, :]), r=["AE"], w=["G"], dma="gs")
    for d, coff in ((0, COEFF), (1, COEFB)):
        co = PCF(coff, 12).rearrange("p (r c) -> p r c", r=4)
        P.op("dve", lambda e, d=d, co=co: e.tensor_tensor(out=wr[:, 0:4], in0=co[:, :, 0], in1=G4[:, :, d, 64], op=ALU.mult), r=["G", "pcf", "AE"], w=["wr"])
        P.op("dve", lambda e, co=co: e.tensor_tensor(out=wr[:, 0:4], in0=wr[:, 0:4], in1=co[:, :, 1], op=ALU.add), r=["wr", "pcf"], w=["wr"])
        P.op("dve", lambda e, d=d, co=co: e.tensor_tensor(out=Tr.rearrange("p (r c) -> p r c", r=4), in0=G4[:, :, d, 0:64],
                                                          in1=co[:, :, 2:3].to_broadcast([128, 4, 64]), op=ALU.mult), r=["G", "pcf", "AE"], w=["Tr"])
        s0 = Sst[:, 0, 0, :] if d == 0 else Sst[:, 1, 16, :]
        P.op("pool", lambda e, s0=s0: e.memset(s0, 0.0), r=["AE"], w=["S0_%d" % d])
        order = range(4) if d == 0 else range(3, -1, -1)
        for r_ in order:
            P.op("dve", lambda e, r_=r_, s0=s0: e.scalar_tensor_tensor(out=s0, in0=s0, scalar=wr[:, r_:r_ + 1], in1=Tr[:, r_ * 64:(r_ + 1) * 64],
                                                                         op0=ALU.mult, op1=ALU.add), r=["S0_%d" % d, "wr", "Tr"], w=["S0_%d" % d])
    for n in range(NB):
        P.op("dve", lambda e, n=n: e.scalar_tensor_tensor(out=Sst[:, 0, n + 1, :], in0=Sst[:, 0, n, :], scalar=etf[:, n:n + 1], in1=U[:, 0, n, 0:64],
                                                           op0=ALU.mult, op1=ALU.add),
             r=["S0_0" if n == 0 else "Sf:%d" % n, "U:0:%d" % n, "et0:%d" % (n // 4), "AE"], w=["Sf:%d" % (n + 1)])
    for n in range(NB - 1, -1, -1):
        P.op("dve", lambda e, n=n: e.scalar_tensor_tensor(out=Sst[:, 1, n, :], in0=Sst[:, 1, n + 1, :], scalar=etb[:, n:n + 1], in1=U[:, 1, n, 0:64],
                                                           op0=ALU.mult, op1=ALU.add),
             r=["S0_1" if n == NB - 1 else "Sb:%d" % (n + 1), "U:1:%d" % n, "et1:%d" % (n // 4), "AE"], w=["Sb:%d" % n])
    barrier(C)
    A.off = amark
    w_g0, k_g0, b_g0 = W.get()
    w_g1, k_g1, b_g1 = W.get()
    qm = [[A.bf16(512), A.bf16(512)], [A.bf16(512), A.bf16(512)]]
    scs = [[A.bf16(512), A.bf16(512)], [A.bf16(512), A.bf16(512)]]
    sbd = [[A.bf16(256), A.bf16(256)], [A.bf16(256), A.bf16(256)]]
    grs = A.bf16(2 * TL).rearrange("p (c t) -> p c t", c=2)
    sqo = A.bf16(2 * TL).rearrange("p (c t) -> p c t", c=2)
    rso = A.f32(2 * TL).rearrange("p (c t) -> p c t", c=2)
    yt = A.f32(TL)
    hm = CF(HEADM, 4)
    dirs = ((qt_f, kt_f, TRIF), (qt_b, kt_b, TRIB))

    def g_a(n):
        pb, t = n % 2, n // 4
        bs = slice(n * 128, (n + 1) * 128)
        for d, (qt, kt, tri) in enumerate(dirs):
            P.op("pool", lambda e, d=d, qt=qt: e.tensor_tensor(out=qm[pb][d].rearrange("p (h i) -> p h i", h=4),
                                                              in0=qt[:, bs].unsqueeze(1).to_broadcast([128, 4, 128]),
                                                              in1=hm.unsqueeze(2).to_broadcast([128, 4, 128]), op=ALU.mult),
                 r=["qt%d:%d" % (d, t), "cf", "AE"], w=["qm%d:%d" % (pb, d)])
        for d, (qt, kt, tri) in enumerate(dirs):
            P.op("pe", lambda e, d=d, kt=kt: e.matmul(banks[2 * pb + d][:, :], lhsT=kt[:, bs], rhs=qm[pb][d], start=True, stop=True),
                 r=["kt%d:%d" % (d, t), "qm%d:%d" % (pb, d)], w=["b%d" % (2 * pb + d)])

    def g_b(n):
        pb, t, j = n % 2, n // 4, n % 4
        bs = slice(n * 128, (n + 1) * 128)
        for d, (qt, kt, tri) in enumerate(dirs):
            P.op("dve", lambda e, d=d, tri=tri: e.tensor_tensor(out=scs[pb][d].rearrange("p (h i) -> p h i", h=4),
                                                                in0=banks[2 * pb + d][:, :].rearrange("p (h i) -> p h i", h=4),
                                                                in1=cbf[:, tri:tri + 128].unsqueeze(1).to_broadcast([128, 4, 128]), op=ALU.mult),
                 r=["b%d" % (2 * pb + d), "cbf", "AE"], w=["scs%d:%d" % (pb, d)])
            ssrc = Sst[:, 0, n, :] if d == 0 else Sst[:, 1, n + 1, :]
            skey = ("S0_0" if n == 0 else "Sf:%d" % n) if d == 0 else ("S0_1" if n == NB - 1 else "Sb:%d" % (n + 1))
            P.op("dve", lambda e, d=d, ssrc=ssrc: e.tensor_tensor(out=sbd[pb][d].rearrange("p (h c) -> p h c", h=4),
                                                                  in0=ssrc.unsqueeze(1).to_broadcast([128, 4, 64]),
                                                                  in1=hm.unsqueeze(2).to_broadcast([128, 4, 64]), op=ALU.mult),
                 r=[skey, "cf", "AE"], w=["sbd%d:%d" % (pb, d)])
        for hp in range(2):
            ob = banks[4 + hp]
            okey = "b%d" % (4 + hp)
            oc = slice(j * 128, (j + 1) * 128)
            P.op("pe", lambda e, hp=hp, ob=ob, oc=oc: e.matmul(ob[:, oc], lhsT=sbd[pb][0][:, hp * 128:(hp + 1) * 128], rhs=qt_f[:, bs], start=True, stop=False),
                 r=["sbd%d:0" % pb, "qt0:%d" % t], w=[okey])
            P.op("pe", lambda e, hp=hp, ob=ob, oc=oc: e.matmul(ob[:, oc], lhsT=sbd[pb][1][:, hp * 128:(hp + 1) * 128], rhs=qt_b[:, bs], start=False, stop=False),
                 r=["sbd%d:1" % pb, "qt1:%d" % t], w=[okey])
            for hh in range(2):
                h = 2 * hp + hh
                for d in range(2):
                    last = (hh == 1 and d == 1)
                    P.op("pe", lambda e, ob=ob, oc=oc, hh=hh, h=h, d=d, last=last: e.matmul(
                        ob[hh * 64:(hh + 1) * 64, oc], lhsT=v_sb[:, n, h * 64:(h + 1) * 64], rhs=scs[pb][d][:, h * 128:(h + 1) * 128], start=False, stop=last),
                         r=["scs%d:%d" % (pb, d), "v:%d" % (n // 2)], w=[okey])

    def g_gr(t):
        for c, (wg, kg) in enumerate(((w_g0, k_g0), (w_g1, k_g1))):
            for kc in range(KC):
                P.op("pe", lambda e, kc=kc, c=c, wg=wg: e.matmul(banks[6 + c][:, :], lhsT=wg[:, kc * 128:(kc + 1) * 128], rhs=xn_tile(kc, t),
                                                                 start=(kc == 0), stop=(kc == KC - 1)),
                     r=[kg, "xn:%d:%d" % (kc, t)], w=["b%d" % (6 + c)])
            P.op("act", lambda e, c=c: e.activation(out=grs[:, c, :], in_=banks[6 + c][:, :], func=AF.Silu), r=["b%d" % (6 + c), "AE"], w=["grs%d" % c])

    def g_post(t):
        for hp in range(2):
            ob = banks[4 + hp]
            okey = "b%d" % (4 + hp)
            P.op("act", lambda e, hp=hp, ob=ob: e.activation(out=sqo[:, hp, :], in_=ob[:, :], func=AF.Square), r=[okey, "AE"], w=["sqo%d" % hp])
            P.op("pe", lambda e, hp=hp: e.matmul(banks[6 + hp][:, :], lhsT=cbf[:, ONESBLK:ONESBLK + 128], rhs=sqo[:, hp, :], start=True, stop=True),
                 r=["sqo%d" % hp, "cbf", "grs%d" % hp], w=["b%d" % (6 + hp)])
            P.op("act", lambda e, hp=hp: e.activation(out=rso[:, hp, :], in_=banks[6 + hp][:, :], func=AF.Sqrt, bias=EPS, scale=1.0 / 64),
                 r=["b%d" % (6 + hp), "AE"], w=["rso%d" % hp])
            P.op("dve", lambda e, hp=hp: e.reciprocal(out=rso[:, hp, :], in_=rso[:, hp, :]), r=["rso%d" % hp], w=["rso%d" % hp])
            P.op("dve", lambda e, hp=hp, ob=ob: e.tensor_tensor(out=yt, in0=ob[:, :], in1=rso[:, hp, :], op=ALU.mult), r=[okey, "rso%d" % hp, "AE"], w=["yt"])
            P.op("dve", lambda e, hp=hp: e.scalar_tensor_tensor(out=ys[:, 2 + hp, t * TL:(t + 1) * TL], in0=yt, scalar=spt[:, GLAG + hp:GLAG + hp + 1],
                                                                 in1=grs[:, hp, :], op0=ALU.mult, op1=ALU.mult),
                 r=["yt", "grs%d" % hp, "spt"], w=["ys:%d:%d" % (2 + hp, t)])

    g_gr(0)
    g_a(0)
    for n in range(NB):
        if n + 1 < NB:
            g_a(n + 1)
        g_b(n)
        if n % 4 == 3:
            g_post(n // 4)
            if n + 1 < NB:
                g_gr(n // 4 + 1)
    W.done(b_g0)
    W.done(b_g1)
    barrier(C)
    A.reset()
    C.A, C.xT, C.xn, C.xnh, C.ys, C.amb, C.PCF, C.CF, C.ext_ranges = A, xT, xn, xnh, ys, amb, PCF, CF, ext_ranges
    C.rope_d, C.xout, C.yout, C.norm_all, C.arn_fn = rope_d, xout, yout, norm_all, None
    C.debug = debug
    if debug:
        C.dbg = dbg
    return nc, st, P, C


def proj_ext(C, wt, wkey, rng, bank, bkey, M=128):
    rhs_fn, eoff, ntok, keyf = rng
    for kc in range(KC):
        C.P.op("pe", lambda e, kc=kc: e.matmul(bank[0:M, 0:ntok], lhsT=wt[:, kc * 128:kc * 128 + M], rhs=rhs_fn(kc),
                                                start=(kc == 0), stop=(kc == KC - 1)),
               r=[wkey, keyf(kc)], w=[bkey])


def attention_phase(C):
    P, W, banks, A, ys = C.P, C.W, C.banks, C.A, C.ys
    cbf, spt = C.cbf, C.spt
    q_att = ys[:, 0:2, :]
    k_att = A.bf16(NE)
    v_att = A.bf16(18 * 130).rearrange("p (n c) -> p n c", n=18)
    nsink = A.f32(4)
    amark = A.off
    rC = A.f32(NE)
    rS = A.f32(NE)
    P.op("sp", lambda e: e.dma_start(out=rC, in_=C.rope_d[0]), r=["AE"], w=["rC"], dma="rC")
    P.op("sp", lambda e: e.dma_start(out=rS, in_=C.rope_d[1]), r=["AE"], w=["rS"], dma="rS")
    raw = [A.bf16(TL), A.bf16(TL)]
    r1 = [A.f32(TL), A.f32(TL)]
    r2 = [A.f32(TL), A.f32(TL)]
    P.op("dve", lambda e: e.tensor_scalar(out=nsink, in0=spt[:, SINK:SINK + 4], scalar1=-1.0, scalar2=None, op0=ALU.mult), r=["spt", "AE"], w=["nsink"])
    P.op("pool", lambda e: e.memset(v_att.rearrange("p n c -> p (n c)"), 1.0), r=["AE"], w=["vones"])
    cnt = [0]

    def rope_proj(wt, wkey, rng, dst, dkey, scale):
        rhs_fn, eoff, ntok, keyf = rng
        i = cnt[0] % 2
        cnt[0] += 1
        bk, bkey = banks[i], "b%d" % i
        bp, bpkey = banks[2 + i], "b%d" % (2 + i)
        proj_ext(C, wt, wkey, rng, bk, bkey)
        P.op("act", lambda e: e.activation(out=raw[i][:, 0:ntok], in_=bk[:, 0:ntok], func=AF.Copy, scale=scale), r=[bkey, "AE"], w=["raw%d" % i])
        P.op("pe", lambda e: e.matmul(bp[:, 0:ntok], lhsT=cbf[:, PMAT:PMAT + 128], rhs=raw[i][:, 0:ntok], start=True, stop=True),
             r=["raw%d" % i, "cbf"], w=[bpkey])
        P.op("dve", lambda e: e.tensor_tensor(out=r1[i][:, 0:ntok], in0=raw[i][:, 0:ntok], in1=rC[:, eoff:eoff + ntok], op=ALU.mult),
             r=["raw%d" % i, "rC", "AE"], w=["r1%d" % i])
        P.op("dve", lambda e: e.tensor_tensor(out=r2[i][:, 0:ntok], in0=bp[:, 0:ntok], in1=rS[:, eoff:eoff + ntok], op=ALU.mult),
             r=[bpkey, "rS", "AE"], w=["r2%d" % i])
        P.op("pool", lambda e: e.tensor_tensor(out=dst, in0=r1[i][:, 0:ntok], in1=r2[i][:, 0:ntok], op=ALU.add),
             r=["r1%d" % i, "r2%d" % i, "AE"], w=[dkey])

    for gq in range(2):
        wt, wkey, wb_ = W.get()
        for t in range(NTL):
            rope_proj(wt, wkey, C.ext_ranges[t], q_att[:, gq, t * TL:(t + 1) * TL], "qa:%d:%d" % (gq, t), 0.125)
        W.done(wb_)
    wt, wkey, wb_ = W.get()
    for ri, rng in enumerate(C.ext_ranges):
        rope_proj(wt, wkey, rng, k_att[:, rng[1]:rng[1] + rng[2]], "ka:%d" % ri, 1.0)
    W.done(wb_)
    wt, wkey, wb_ = W.get()
    for eb in range(18):
        if eb == 0:
            lf, kf = (lambda kc: C.xnh[:, kc, 0:HB]), (lambda kc: "xnh:%d" % kc)
        elif eb == 17:
            lf, kf = (lambda kc: C.xnh[:, kc, HB:2 * HB]), (lambda kc: "xnh:%d" % kc)
        else:
            lf, kf = (lambda kc, eb=eb: C.xn[:, kc, (eb - 1) * 128:eb * 128]), (lambda kc, eb=eb: "xn:%d:%d" % (kc, (eb - 1) // 4))
        bk, bkey = banks[4 + eb % 2], "b%d" % (4 + eb % 2)
        for kc in range(KC):
            P.op("pe", lambda e, kc=kc, lf=lf, bk=bk: e.matmul(bk[:, 0:128], lhsT=lf(kc), rhs=wt[:, kc * 128:(kc + 1) * 128],
                                                                start=(kc == 0), stop=(kc == KC - 1)),
                 r=[wkey, kf(kc)], w=[bkey])
        P.op("act", lambda e, eb=eb, bk=bk: e.activation(out=v_att[:, eb, :].rearrange("p (g c) -> p g c", g=2)[:, :, 0:64],
                                                          in_=bk[:, 0:128].rearrange("p (g c) -> p g c", g=2), func=AF.Copy),
             r=[bkey, "vones", "AE"], w=["va:%d" % eb])
    W.done(wb_)
    barrier(C)
    A.off = amark
    mx = [A.f32(4), A.f32(4)]
    negm = [A.f32(4), A.f32(4)]
    es = [A.f32(4), A.f32(4)]
    den = A.f32(4)
    Pb = [[A.bf16(768), A.bf16(768)], [A.bf16(768), A.bf16(768)]]
    Pm = [[A.bf16(768), A.bf16(768)], [A.bf16(768), A.bf16(768)]]
    PTs = [A.bf16(768), A.bf16(768)]
    on = [A.bf16(256), A.bf16(256)]
    wkeys = ["ka:%d" % ri for ri in range(6)]

    def stage_a(n):
        pb = n % 2
        v = 0 if n == 0 else (2 if n == NB - 1 else 1)
        msk = C.amb[:, v * 384:(v + 1) * 384]
        qs = slice(n * 128, (n + 1) * 128)
        win = slice(n * 128, n * 128 + 384)
        for k in range(2):
            pr = slice(64 * k, 64 * k + 64)
            for g_ in range(2):
                bk, bkey = banks[2 * k + g_], "b%d" % (2 * k + g_)
                P.op("pe", lambda e, bk=bk, g_=g_, pr=pr: e.matmul(bk[:, 0:384], lhsT=q_att[pr, g_, qs], rhs=k_att[pr, win], start=True, stop=True),
                     r=["qa:%d:%d" % (g_, n // 4)] + wkeys, w=[bkey])
        for h in range(4):
            P.op("dve", lambda e, h=h: e.reduce_max(out=mx[pb][:, h:h + 1], in_=banks[h][:, 0:384], axis=AX.X), r=["b%d" % h, "AE"], w=["mx%d:%d" % (pb, h)])
        P.op("dve", lambda e: e.scalar_tensor_tensor(out=negm[pb], in0=mx[pb], scalar=-1.0, in1=nsink, op0=ALU.mult, op1=ALU.min),
             r=["mx%d:%d" % (pb, h) for h in range(4)] + ["nsink"], w=["negm%d" % pb])

    def stage_a2(n):
        pb = n % 2
        v = 0 if n == 0 else (2 if n == NB - 1 else 1)
        msk = C.amb[:, v * 384:(v + 1) * 384]
        for k in range(2):
            for g_ in range(2):
                h = 2 * k + g_
                P.op("act", lambda e, g_=g_, h=h, k=k: e.activation(out=Pb[pb][k][:, g_ * 384:(g_ + 1) * 384], in_=banks[h][:, 0:384], func=AF.Exp, bias=negm[pb][:, h:h + 1]),
                     r=["b%d" % h, "negm%d" % pb, "AE"], w=["Pb%d:%d:%d" % (pb, k, g_)])
            P.op("pool", lambda e, k=k: e.tensor_tensor(out=Pm[pb][k].rearrange("p (g s) -> p g s", g=2), in0=Pb[pb][k].rearrange("p (g s) -> p g s", g=2),
                                                        in1=msk.unsqueeze(1).to_broadcast([128, 2, 384]), op=ALU.mult),
                 r=["Pb%d:%d:0" % (pb, k), "Pb%d:%d:1" % (pb, k), "amb", "AE"], w=["Pm%d:%d" % (pb, k)])
        P.op("dve", lambda e: e.tensor_tensor(out=es[pb], in0=negm[pb], in1=spt[:, SINK:SINK + 4], op=ALU.add), r=["negm%d" % pb, "spt", "AE"], w=["es%d" % pb])
        P.op("act", lambda e: e.activation(out=es[pb], in_=es[pb], func=AF.Exp), r=["es%d" % pb], w=["es%d" % pb])

    def stage_b(n):
        pb = n % 2
        qs = slice(n * 128, (n + 1) * 128)
        for k in range(2):
            bt = bfview(banks[4 + k])
            btkey = "b%d" % (4 + k)
            for j in range(6):
                P.op("pe", lambda e, j=j, k=k, bt=bt: e.transpose(out=bt[:, j * 128:(j + 1) * 128], in_=Pm[pb][k][:, j * 128:(j + 1) * 128], identity=cbf[:, IDENT:IDENT + 128]),
                     r=["Pm%d:%d" % (pb, k), "cbf"], w=[btkey])
            P.op("act", lambda e, k=k, bt=bt: e.activation(out=PTs[k], in_=bt[:, 0:768], func=AF.Copy), r=[btkey, "AE"], w=["PTs%d" % k])

    def stage_b2(n):
        pb = n % 2
        qs = slice(n * 128, (n + 1) * 128)
        for k in range(2):
            for g_ in range(2):
                h = 2 * k + g_
                for w_ in range(3):
                    P.op("pe", lambda e, g_=g_, h=h, w_=w_, k=k: e.matmul(banks[6][:, h * 65:(h + 1) * 65], lhsT=PTs[k][:, (g_ * 3 + w_) * 128:(g_ * 3 + w_ + 1) * 128],
                                                                          rhs=v_att[:, n + w_, k * 65:(k + 1) * 65], start=(w_ == 0), stop=(w_ == 2)),
                         r=["PTs%d" % k] + ["va:%d" % (n + w_)], w=["b6"])
        b6v = banks[6][:, 0:260].rearrange("p (h c) -> p h c", h=4)
        P.op("dve", lambda e: e.tensor_tensor(out=den, in0=b6v[:, :, 64], in1=es[pb], op=ALU.add), r=["b6", "es%d" % pb, "AE"], w=["den"])
        P.op("dve", lambda e: e.reciprocal(out=den, in_=den), r=["den"], w=["den"])
        P.op("dve", lambda e: e.tensor_tensor(out=on[pb].rearrange("p (h c) -> p h c", h=4), in0=b6v[:, :, 0:64],
                                              in1=den.unsqueeze(2).to_broadcast([128, 4, 64]), op=ALU.mult), r=["b6", "den", "AE"], w=["on%d" % pb])
        b7 = bfview(banks[7])
        for c in range(2):
            P.op("pe", lambda e, c=c: e.transpose(out=b7[:, c * 128:(c + 1) * 128], in_=on[pb][:, c * 128:(c + 1) * 128], identity=cbf[:, IDENT:IDENT + 128]),
                 r=["on%d" % pb, "cbf"], w=["b7"])
        P.op("act", lambda e: e.activation(out=ys[:, 4:6, qs], in_=b7[:, 0:256].rearrange("p (c t) -> p c t", c=2), func=AF.Copy),
             r=["b7", "AE"], w=["ys:4:%d" % (n // 4), "ys:5:%d" % (n // 4)])

    stage_a(0)
    stage_a2(0)
    for n in range(NB):
        if n + 1 < NB:
            stage_a(n + 1)
        stage_b(n)
        if n + 1 < NB:
            stage_a2(n + 1)
        stage_b2(n)
    barrier(C)
    A.reset()


def conv_phase(C):
    P, W, banks, A, ys = C.P, C.W, C.banks, C.A, C.ys
    cbf, spt = C.cbf, C.spt
    u_ext = A.bf16(2 * NE).rearrange("p (c t) -> p c t", c=2)
    Dm = A.bf16(2 * 31 * 128).rearrange("p (c k j) -> p c k j", c=2, k=31)
    sg = [A.f32(TL), A.f32(TL)]
    for c in range(2):
        P.op("pool", lambda e, c=c: e.tensor_tensor(out=Dm[:, c, :, :], in0=cbf[:, IDENT:IDENT + 128].unsqueeze(1).to_broadcast([128, 31, 128]),
                                                     in1=spt[:, CONVW + 31 * c:CONVW + 31 * (c + 1)].unsqueeze(2).to_broadcast([128, 31, 128]), op=ALU.mult),
             r=["cbf", "spt", "AE"], w=["Dm%d" % c])
    i = 0
    for c in range(2):
        wa, ka, ba = W.get()
        wg, kg, bg = W.get()
        for ri, rng in enumerate(C.ext_ranges):
            eoff, ntok = rng[1], rng[2]
            j = i % 2
            i += 1
            proj_ext(C, wa, ka, rng, banks[j], "b%d" % j)
            proj_ext(C, wg, kg, rng, banks[2 + j], "b%d" % (2 + j))
            P.op("act", lambda e, j=j, ntok=ntok: e.activation(out=sg[j][:, 0:ntok], in_=banks[2 + j][:, 0:ntok], func=AF.Sigmoid), r=["b%d" % (2 + j), "AE"], w=["sg%d" % j])
            P.op("dve", lambda e, j=j, c=c, eoff=eoff, ntok=ntok: e.tensor_tensor(out=u_ext[:, c, eoff:eoff + ntok], in0=banks[j][:, 0:ntok], in1=sg[j][:, 0:ntok], op=ALU.mult),
                 r=["b%d" % j, "sg%d" % j, "AE"], w=["u:%d:%d" % (c, ri)])
        W.done(ba)
        W.done(bg)
    ysb = A.f32(2 * TL).rearrange("p (c t) -> p c t", c=2)
    ybf = A.bf16(2 * TL).rearrange("p (c t) -> p c t", c=2)
    ysq = A.bf16(2 * TL).rearrange("p (c t) -> p c t", c=2)
    mean = A.f32(TL)
    var = A.f32(TL)
    dd = A.f32(TL)
    msq = dd
    for t in range(NTL):
        for c in range(2):
            bk, bkey = banks[4 + c], "b%d" % (4 + c)
            for k in range(31):
                s0 = HB + t * TL + k - 15
                P.op("pe", lambda e, c=c, k=k, s0=s0, bk=bk: e.matmul(bk[:, :], lhsT=Dm[:, c, k, :], rhs=u_ext[:, c, s0:s0 + TL], start=(k == 0), stop=(k == 30)),
                     r=["Dm%d" % c] + ["u:%d:%d" % (c, ri) for ri in range(6)], w=[bkey])
            P.op("act", lambda e, c=c, bk=bk: e.activation(out=ysb[:, c, :], in_=bk[:, :], func=AF.Identity, bias=spt[:, CONVB + c:CONVB + c + 1]),
                 r=[bkey, "spt", "AE"], w=["ysb%d" % c])
            P.op("act", lambda e, c=c, bk=bk: e.activation(out=ysq[:, c, :], in_=bk[:, :], func=AF.Square, bias=spt[:, CONVB + c:CONVB + c + 1]),
                 r=[bkey, "spt", "AE"], w=["ysq%d" % c])
            P.op("pool", lambda e, c=c: e.tensor_copy(out=ybf[:, c, :], in_=ysb[:, c, :]), r=["ysb%d" % c, "AE"], w=["ybf%d" % c])
        for c in range(2):
            P.op("pe", lambda e, c=c: e.matmul(banks[6][:, :], lhsT=cbf[:, ONES:ONES + 128], rhs=ybf[:, c, :], start=(c == 0), stop=(c == 1)), r=["ybf%d" % c, "cbf"], w=["b6"])
        for c in range(2):
            P.op("pe", lambda e, c=c: e.matmul(banks[7][:, :], lhsT=cbf[:, ONES:ONES + 128], rhs=ysq[:, c, :], start=(c == 0), stop=(c == 1)), r=["ysq%d" % c, "cbf"], w=["b7"])
        P.op("dve", lambda e: e.tensor_scalar(out=mean, in0=banks[6][:, :], scalar1=1.0 / 256, scalar2=None, op0=ALU.mult), r=["b6", "AE"], w=["mean"])
        P.op("dve", lambda e: e.tensor_tensor(out=msq, in0=mean, in1=mean, op=ALU.mult), r=["mean", "AE"], w=["dd"])
        P.op("dve", lambda e: e.scalar_tensor_tensor(out=var, in0=banks[7][:, :], scalar=1.0 / 256, in1=msq, op0=ALU.mult, op1=ALU.subtract), r=["b7", "dd", "AE"], w=["var"])
        P.op("act", lambda e: e.activation(out=var, in_=var, func=AF.Sqrt, bias=EPS), r=["var"], w=["var"])
        P.op("dve", lambda e: e.reciprocal(out=var, in_=var), r=["var"], w=["var"])
        for c in range(2):
            P.op("dve", lambda e, c=c: e.tensor_tensor(out=dd, in0=ysb[:, c, :], in1=mean, op=ALU.subtract), r=["ysb%d" % c, "mean", "AE"], w=["dd"])
            P.op("dve", lambda e: e.tensor_tensor(out=dd, in0=dd, in1=var, op=ALU.mult), r=["dd", "var"], w=["dd"])
            P.op("act", lambda e, c=c, t=t: e.activation(out=ys[:, c, t * TL:(t + 1) * TL], in_=dd, func=AF.Silu, scale=spt[:, LNG + c:LNG + c + 1], bias=spt[:, LNB + c:LNB + c + 1]),
                 r=["dd", "spt", "AE"], w=["ys:%d:%d" % (c, t)])
    barrier(C)
    A.reset()


def pool_phase(C):
    P, W, banks, A, ys = C.P, C.W, C.banks, C.A, C.ys
    spt = C.spt
    pin = A.f32(2 * NE).rearrange("p (c t) -> p c t", c=2)
    B1 = A.f32(NE)
    B2 = A.f32(NE)
    dbf = [A.bf16(TL), A.bf16(TL)]
    pwb = A.bf16(256)
    P.op("dve", lambda e: e.tensor_copy(out=pwb, in_=spt[:, POOLW:POOLW + 256]), r=["spt", "AE"], w=["pwb"])
    i = 0
    for c in range(2):
        wt, wkey, wb_ = W.get()
        for ri, rng in enumerate(C.ext_ranges):
            eoff, ntok = rng[1], rng[2]
            j = i % 2
            i += 1
            proj_ext(C, wt, wkey, rng, banks[j], "b%d" % j)
            P.op("act", lambda e, j=j, c=c, eoff=eoff, ntok=ntok: e.activation(out=pin[:, c, eoff:eoff + ntok], in_=banks[j][:, 0:ntok], func=AF.Copy),
                 r=["b%d" % j, "AE"], w=["pin:%d:%d" % (c, ri)])
        W.done(wb_)
    Wd = NE
    invw = C.CF(INVW, 2)
    corr = C.PCF(PCORR, 32).rearrange("p (c j) -> p c j", c=2)
    for c in range(2):
        u = pin[:, c, :]
        pk = ["pin:%d:%d" % (c, ri) for ri in range(6)]
        P.op("pool", lambda e, u=u: e.tensor_tensor(out=B1[:, 1:Wd], in0=u[:, 0:Wd - 1], in1=u[:, 1:Wd], op=ALU.add), r=pk + ["AE"], w=["B1"])
        P.op("pool", lambda e: e.tensor_tensor(out=B2[:, 2:Wd - 1], in0=B1[:, 1:Wd - 2], in1=B1[:, 3:Wd], op=ALU.add), r=["B1", "AE"], w=["B2"])
        if c == 1:
            P.op("pool", lambda e: e.tensor_tensor(out=B1[:, 4:Wd - 3], in0=B2[:, 2:Wd - 5], in1=B2[:, 6:Wd - 1], op=ALU.add), r=["B2"], w=["B1"])
            P.op("pool", lambda e: e.tensor_tensor(out=B2[:, 8:Wd - 7], in0=B1[:, 4:Wd - 11], in1=B1[:, 12:Wd - 3], op=ALU.add), r=["B1"], w=["B2"])
        for (Bx, bkey, pr) in ((B1, "B1", slice(0, 64)), (B2, "B2", slice(64, 128))):
            P.op("dve", lambda e, Bx=Bx, pr=pr, c=c: e.tensor_tensor(out=Bx[pr, HB:HB + 8], in0=Bx[pr, HB:HB + 8], in1=corr[pr, c, 0:8], op=ALU.mult), r=[bkey, "pcf"], w=[bkey])
            P.op("dve", lambda e, Bx=Bx, pr=pr, c=c: e.tensor_tensor(out=Bx[pr, HB + NT - 8:HB + NT], in0=Bx[pr, HB + NT - 8:HB + NT], in1=corr[pr, c, 8:16], op=ALU.mult), r=[bkey, "pcf"], w=[bkey])
        for t in range(NTL):
            j = t % 2
            es = slice(HB + t * TL, HB + (t + 1) * TL)
            for (Bx, bkey, pr) in ((B1, "B1", slice(0, 64)), (B2, "B2", slice(64, 128))):
                P.op("dve", lambda e, Bx=Bx, pr=pr, c=c, j=j, es=es: e.scalar_tensor_tensor(out=dbf[j][pr, :], in0=Bx[pr, es], scalar=invw[pr, c:c + 1], in1=pin[pr, c, es],
                                                                                             op0=ALU.mult, op1=ALU.subtract),
                     r=[bkey, "cf"] + pk + ["AE"], w=["dbf%d:%d" % (j, pr.start)])
            bk, bkey2 = banks[2 + j], "b%d" % (2 + j)
            P.op("pe", lambda e, c=c, j=j, bk=bk: e.matmul(bk[:, :], lhsT=pwb[:, c * 128:(c + 1) * 128], rhs=dbf[j], start=True, stop=True),
                 r=["dbf%d:0" % j, "dbf%d:64" % j, "pwb"], w=[bkey2])
            P.op("act", lambda e, c=c, t=t, bk=bk: e.activation(out=ys[:, 6 + c, t * TL:(t + 1) * TL], in_=bk[:, :], func=AF.Identity, scale=spt[:, PSC + c:PSC + c + 1]),
                 r=[bkey2, "spt", "AE"], w=["ys:%d:%d" % (6 + c, t)])
    barrier(C)
    A.reset()


def merge_phase(C):
    P, W, banks, A, ys, xn, xT = C.P, C.W, C.banks, C.A, C.ys, C.xn, C.xT
    merged = A.bf16(KC * NT).rearrange("p (k t) -> p k t", k=KC)
    acc = A.f32(NTL * TL).rearrange("p (a t) -> p a t", a=NTL)
    tmp = [A.f32(TL), A.f32(TL)]
    sgv = C.xnh[:, :, :].rearrange("p k t -> p (k t)").bitcast(F32)
    sg = [sgv[:, 0:TL], sgv[:, TL:2 * TL]]
    i = 0
    for f in range(KC):
        wb, kb, bb = W.get()
        wb4 = wb[:, :].rearrange("p (n k j) -> p n k j", n=4, k=2)
        for n in range(4):
            wg, kg, bg = W.get()
            for t in range(NTL):
                j = i % 2
                i += 1
                G, gk = banks[j], "b%d" % j
                Pj, pk = banks[2 + j], "b%d" % (2 + j)
                for kc in range(KC):
                    P.op("pe", lambda e, kc=kc, t=t, wg=wg, G=G: e.matmul(G[:, :], lhsT=wg[:, kc * 128:(kc + 1) * 128], rhs=xn[:, kc, t * TL:(t + 1) * TL],
                                                                          start=(kc == 0), stop=(kc == KC - 1)), r=[kg, "xn:%d:%d" % (kc, t)], w=[gk])
                for k2 in range(2):
                    P.op("pe", lambda e, k2=k2, t=t, n=n, Pj=Pj, wb4=wb4: e.matmul(Pj[:, :], lhsT=wb4[:, n, k2, :], rhs=ys[:, 2 * n + k2, t * TL:(t + 1) * TL],
                                                                                   start=(k2 == 0), stop=(k2 == 1)), r=[kb, "ys:%d:%d" % (2 * n + k2, t)], w=[pk])
                P.op("act", lambda e, j=j, G=G: e.activation(out=sg[j], in_=G[:, :], func=AF.Sigmoid), r=[gk, "xnhfree"], w=["sgm%d" % j])
                if n == 0:
                    P.op("dve", lambda e, j=j, t=t, Pj=Pj: e.tensor_tensor(out=acc[:, t, :], in0=Pj[:, :], in1=sg[j], op=ALU.mult), r=[pk, "sgm%d" % j, "AE"], w=["acc%d" % t])
                else:
                    P.op("dve", lambda e, j=j, Pj=Pj: e.tensor_tensor(out=tmp[j], in0=Pj[:, :], in1=sg[j], op=ALU.mult), r=[pk, "sgm%d" % j, "AE"], w=["tmp%d" % j])
                    if n < 3:
                        P.op("pool", lambda e, j=j, t=t: e.tensor_tensor(out=acc[:, t, :], in0=acc[:, t, :], in1=tmp[j], op=ALU.add), r=["acc%d" % t, "tmp%d" % j], w=["acc%d" % t])
                    else:
                        P.op("pool", lambda e, j=j, t=t, f=f: e.tensor_tensor(out=merged[:, f, t * TL:(t + 1) * TL], in0=acc[:, t, :], in1=tmp[j], op=ALU.add),
                             r=["acc%d" % t, "tmp%d" % j], w=["mg:%d:%d" % (f, t)])
            W.done(bg)
        W.done(bb)
    i = 0
    for f in range(KC):
        wo, ko, bo = W.get()
        for t in range(NTL):
            j = i % 4
            i += 1
            bk, bkey = banks[4 + j], "b%d" % (4 + j)
            for kc in range(KC):
                P.op("pe", lambda e, kc=kc, t=t, bk=bk, wo=wo: e.matmul(bk[:, :], lhsT=wo[:, kc * 128:(kc + 1) * 128], rhs=merged[:, kc, t * TL:(t + 1) * TL],
                                                                        start=(kc == 0), stop=(kc == KC - 1)), r=[ko, "mg:%d:%d" % (kc, t)], w=[bkey])
            P.op("dve", lambda e, f=f, t=t, bk=bk: e.tensor_tensor(out=xT[:, f, t * TL:(t + 1) * TL], in0=bk[:, :], in1=xT[:, f, t * TL:(t + 1) * TL], op=ALU.add),
                 r=[bkey, "x:%d:%d" % (f, t)], w=["x:%d:%d" % (f, t)])
        W.done(bo)
    barrier(C)
    A.reset()


def ffn_phase(C):
    P, W, banks, A, ys, xn, xT = C.P, C.W, C.banks, C.A, C.ys, C.xn, C.xT
    C.early_stores = []
    arn = dict(sq=A.bf16(KC * TL), rs=[A.f32(TL), A.f32(TL)])
    for t in range(NTL):
        rms_tile(C, lambda kc, t=t: xT[:, kc, t * TL:(t + 1) * TL], FFNG, lambda kc, t=t: xn[:, kc, t * TL:(t + 1) * TL], TL,
                 ["x:%d:%d" % (kc, t) for kc in range(KC)], ["xn:%d:%d" % (kc, t) for kc in range(KC)], arn, [banks[2], banks[3]], ["b2", "b3"], "f%d" % t)
    rl = [A.f32(TL), A.f32(TL)]
    act = ys
    i = 0
    for g in range(4):
        for c in range(KC):
            wu, ku, bu = W.get()
            for t in range(NTL):
                j = i % 2
                i += 1
                bk, bkey = banks[j], "b%d" % j
                for kc in range(KC):
                    P.op("pe", lambda e, kc=kc, t=t, bk=bk, wu=wu: e.matmul(bk[:, :], lhsT=wu[:, kc * 128:(kc + 1) * 128], rhs=xn[:, kc, t * TL:(t + 1) * TL],
                                                                            start=(kc == 0), stop=(kc == KC - 1)), r=[ku, "xn:%d:%d" % (kc, t)], w=[bkey])
                P.op("act", lambda e, j=j, bk=bk: e.activation(out=rl[j], in_=bk[:, :], func=AF.Relu), r=[bkey, "AE"], w=["rl%d" % j])
                P.op("dve", lambda e, j=j, c=c, t=t: e.tensor_tensor(out=act[:, c, t * TL:(t + 1) * TL], in0=rl[j], in1=rl[j], op=ALU.mult),
                     r=["rl%d" % j, "AE"], w=["ys:%d:%d" % (c, t)])
            W.done(bu)
        for f in range(KC):
            wd, kd, bd = W.get()
            for t in range(NTL):
                j = i % 4
                i += 1
                bk, bkey = banks[4 + j], "b%d" % (4 + j)
                for kc in range(KC):
                    P.op("pe", lambda e, kc=kc, t=t, bk=bk, wd=wd: e.matmul(bk[:, :], lhsT=wd[:, kc * 128:(kc + 1) * 128], rhs=act[:, kc, t * TL:(t + 1) * TL],
                                                                            start=(kc == 0), stop=(kc == KC - 1)), r=[kd, "ys:%d:%d" % (kc, t)], w=[bkey])
                P.op("dve", lambda e, f=f, t=t, bk=bk: e.tensor_tensor(out=xT[:, f, t * TL:(t + 1) * TL], in0=bk[:, :], in1=xT[:, f, t * TL:(t + 1) * TL], op=ALU.add),
                     r=[bkey, "x:%d:%d" % (f, t)], w=["x:%d:%d" % (f, t)])
            if g == 3:
                C.early_stores.append(P.op("act", lambda e, f=f: e.dma_start(out=C.xout[f * 128:(f + 1) * 128, :], in_=xT[:, f, :]),
                                           r=["x:%d:%d" % (f, t) for t in range(NTL)], dma="xo%d" % f))
            W.done(bd)
    barrier(C)
    A.reset()


def output_phase(C):
    P, banks, A, xT, spt = C.P, C.banks, C.A, C.xT, C.spt
    fins = list(getattr(C, "early_stores", []))
    if not fins:
        for kc in range(KC):
            fins.append(P.op("sp", lambda e, kc=kc: e.dma_start(out=C.xout[kc * 128:(kc + 1) * 128, :], in_=xT[:, kc, :]),
                             r=["x:%d:%d" % (kc, t) for t in range(NTL)], dma="xo%d" % kc))
    sq = A.bf16(KC * TL).rearrange("p (k t) -> p k t", k=KC)
    rs = A.f32(TL)
    yo = [A.f32(TL), A.f32(TL), A.f32(TL), A.f32(TL)]
    i = 0
    for t in range(NTL):
        ts = slice(t * TL, (t + 1) * TL)
        for kc in range(KC):
            P.op("act", lambda e, kc=kc, ts=ts: e.activation(out=sq[:, kc, :], in_=xT[:, kc, ts], func=AF.Square), r=["x:%d:%d" % (kc, t), "AE"], w=["sq%d" % kc])
        for kc in range(KC):
            P.op("pe", lambda e, kc=kc: e.matmul(banks[3][:, :], lhsT=C.cbf[:, ONES:ONES + 128], rhs=sq[:, kc, :], start=(kc == 0), stop=(kc == KC - 1)),
                 r=["sq%d" % kc, "cbf"], w=["b3"])
        P.op("act", lambda e: e.activation(out=rs, in_=banks[3][:, :], func=AF.Sqrt, bias=EPS, scale=1.0 / D), r=["b3", "AE"], w=["rs"])
        P.op("dve", lambda e: e.reciprocal(out=rs, in_=rs), r=["rs"], w=["rs"])
        for kc in range(KC):
            j = i % 4
            i += 1
            P.op("dve", lambda e, kc=kc, ts=ts, j=j: e.scalar_tensor_tensor(out=yo[j], in0=xT[:, kc, ts], scalar=spt[:, FING + kc:FING + kc + 1], in1=rs, op0=ALU.mult, op1=ALU.mult),
                 r=["x:%d:%d" % (kc, t), "rs", "spt", "AE"], w=["yo%d" % j])
            fins.append(P.op("sp", lambda e, kc=kc, ts=ts, j=j: e.dma_start(out=C.yout[kc * 128:(kc + 1) * 128, ts], in_=yo[j]), r=["yo%d" % j], dma="yo%d" % j))
    return fins


def build_layer_full(debug=False, upto=99):
    nc, st, P, C = build_layer(debug)
    if upto >= 2:
        attention_phase(C)
    if upto >= 3:
        conv_phase(C)
    if upto >= 4:
        pool_phase(C)
    fins = []
    if debug:
        fins.append(P.op("sp", lambda e: e.dma_start(out=C.dbg[:, :], in_=C.ys[:, :, :].rearrange("p k t -> p (k t)")),
                         r=["ys:%d:%d" % (k, t) for k in range(KC) for t in range(NTL)] + ["AE"], dma="dbg"))
        barrier(C)
    if upto >= 5:
        P.op("pool", lambda e: e.memset(C.scr[:, 0:8], 0.0), w=["xnh:%d" % kc for kc in range(KC)] + ["xnhfree"])
        merge_phase(C)
    if upto >= 6:
        ffn_phase(C)
    fins += output_phase(C)
    P.emit(final_waits=fins)
    return nc, st


_CACHE = {}


def _prog(name):
    if name not in _CACHE:
        if name == "sum":
            _CACHE[name] = build_sum()
        else:
            _CACHE[name] = build_layer_full(debug=False)
    return _CACHE[name][0]


def _ext_from_segments(segs, c):
    p = c % 4
    left = segs[c - 1][:, NT - HB:] if p > 0 else np.zeros((D, HB), np.float32)
    right = segs[c + 1][:, :HB] if p < 3 else np.zeros((D, HB), np.float32)
    return np.ascontiguousarray(np.concatenate([left, segs[c], right], axis=1))


def kernel(**inputs):
    inp = {k: np.asarray(v, dtype=np.float32) for k, v in inputs.items()}
    x = inp["x"]
    B, T, _ = x.shape
    cores = list(range(8))
    cst = const_block()
    pcs = [percore_consts(c) for c in cores]
    ropes = [rope_tables(c) for c in cores]
    segs = [np.ascontiguousarray(x[c // 4, (c % 4) * NT:(c % 4 + 1) * NT, :].T) for c in cores]
    y = None
    for l in range(2):
        sp = small_params(inp, l)
        ws_s = sum_stream(inp, l)
        res = run_bass_kernel_spmd(_prog("sum"), [{"xT": segs[c], "ws": ws_s, "sp": sp, "cst": cst} for c in cores], core_ids=cores)
        sums = [np.asarray(res.results[c]["so"]) for c in cores]
        ws_l = layer_stream(inp, l)
        in_maps = []
        for c in cores:
            b = c // 4
            gs = np.ascontiguousarray(np.stack([sums[b * 4 + r] for r in range(4)], 1).reshape(128, 4 * 130))
            in_maps.append({"xT": _ext_from_segments(segs, c), "ws": ws_l, "sp": sp, "cst": cst, "pc": pcs[c], "rope": ropes[c], "gsum": gs})
        res = run_bass_kernel_spmd(_prog("layer"), in_maps, core_ids=cores)
        segs = [np.asarray(res.results[c]["xo"]) for c in cores]
        y = [np.asarray(res.results[c]["yo"]) for c in cores]
    out = np.empty((B, T, D), np.float32)
    for c in cores:
        out[c // 4, (c % 4) * NT:(c % 4 + 1) * NT, :] = y[c].T
    return out
```
